# Optimizing a Trainium2 kernel written in Bass

```python
import math
import jax, jax.numpy as jnp
from jax import lax
import numpy as np

D_MODEL = 1024
BATCH = 8
SEQ = 2048
DEPTH = 1

PLE_DIM = 256
D_FF = 2816
CONV_CH = 512
CONV_K = 3
SSM_WIDTH = 512
SSM_GROUP = 16
SSM_GROUPS = SSM_WIDTH // SSM_GROUP
SSM_STATE = 64
ALPHA = (2.0 * DEPTH) ** 0.25
BETA = (8.0 * DEPTH) ** -0.25
LN_EPS = 1e-5
IN_COLS = 3 * CONV_CH + SSM_WIDTH + 2 * D_MODEL

kernel_name = "hybrid_conv_s5_macaron_deepnorm_block"


def layer_norm(x, g, b):
    xf = x.astype(jnp.float32)
    mu = jnp.mean(xf, axis=-1, keepdims=True)
    xc = xf - mu
    var = jnp.mean(xc * xc, axis=-1, keepdims=True)
    y = xc * lax.rsqrt(var + LN_EPS) * g.astype(jnp.float32) + b.astype(jnp.float32)
    return y.astype(x.dtype)


def swiglu(x, w_in, w_out):
    gate, up = jnp.split(x @ w_in, 2, axis=-1)
    return (jax.nn.silu(gate) * up) @ w_out


def causal_depthwise_conv(z, w, b):
    c = z.shape[-1]
    y = lax.conv_general_dilated(
        z, w[:, None, :].astype(z.dtype), window_strides=(1,),
        padding=[(CONV_K - 1, 0)], dimension_numbers=("NWC", "WIO", "NWC"),
        feature_group_count=c)
    return y + b


def s5_scan(u, lam_re, lam_im, log_step, b_re, b_im, c_re, c_im, d_skip):
    f32 = jnp.float32
    lam = lax.complex(lam_re.astype(f32), lam_im.astype(f32))
    dt = jnp.exp(log_step.astype(f32))[:, None]
    lam_bar = jnp.exp(lam * dt)
    b_c = lax.complex(b_re.astype(f32), b_im.astype(f32))
    c_c = lax.complex(c_re.astype(f32), c_im.astype(f32))
    b_bar = ((lam_bar - 1.0) / lam)[..., None] * b_c
    uf = u.astype(f32)
    bu = jnp.einsum("blgi,gni->blgn", uf.astype(jnp.complex64), b_bar)
    a = jnp.broadcast_to(lam_bar, bu.shape)

    def combine(left, right):
        a1, s1 = left
        a2, s2 = right
        return a1 * a2, a2 * s1 + s2

    _, states = lax.associative_scan(combine, (a, bu), axis=1)
    y = jnp.einsum("gin,blgn->blgi", c_c, states).real + d_skip.astype(f32) * uf
    bsz, seq = u.shape[0], u.shape[1]
    return y.reshape(bsz, seq, SSM_WIDTH).astype(u.dtype)


def token_mixer(h, w_in, conv_w, conv_b, conv_w_out, lam_re, lam_im, log_step,
                b_re, b_im, c_re, c_im, d_skip, w_glu, w_out):
    bsz, seq, _ = h.shape
    proj = h @ w_in
    cb, cc, ch, su, g_conv, g_ssm = jnp.split(
        proj, [CONV_CH, 2 * CONV_CH, 3 * CONV_CH, 3 * CONV_CH + SSM_WIDTH,
               3 * CONV_CH + SSM_WIDTH + D_MODEL], axis=-1)
    z = causal_depthwise_conv(cc * ch, conv_w, conv_b)
    y_conv = (cb * z) @ conv_w_out
    s = s5_scan(su.reshape(bsz, seq, SSM_GROUPS, SSM_GROUP), lam_re, lam_im, log_step,
                b_re, b_im, c_re, c_im, d_skip)
    s = jax.nn.gelu(s)
    ga, gb = jnp.split(s @ w_glu, 2, axis=-1)
    y_ssm = ga * jax.nn.sigmoid(gb)
    merged = jax.nn.sigmoid(g_conv) * y_conv + jax.nn.sigmoid(g_ssm) * y_ssm
    return merged @ w_out


def setup_inputs(seed: int = 0) -> dict:
    key = jax.random.key(seed)
    ks = jax.random.split(key, 40)
    f32 = jnp.float32
    nrm = lambda k, shape, s: (jax.random.normal(k, shape, f32) * s)
    L = DEPTH

    def gain(k):
        return 1.0 + nrm(k, (L, D_MODEL), 0.01)

    def bias(k, n=D_MODEL):
        return nrm(k, (L, n), 0.01)

    n_idx = jnp.arange(SSM_STATE, dtype=f32)
    lam_re = -0.5 + nrm(ks[20], (L, SSM_GROUPS, SSM_STATE), 0.01)
    lam_im = math.pi * n_idx[None, None, :] + nrm(ks[21], (L, SSM_GROUPS, SSM_STATE), 0.01)
    log_step = jax.random.uniform(ks[22], (L, SSM_GROUPS), f32,
                                  math.log(0.001), math.log(0.1))
    return {
        "x": jax.random.normal(ks[0], (BATCH, SEQ, D_MODEL), f32),
        "p": jax.random.normal(ks[1], (DEPTH, BATCH, SEQ, PLE_DIM), f32),
        "ffn1_w_in": nrm(ks[2], (L, D_MODEL, 2 * D_FF), D_MODEL ** -0.5),
        "ffn1_w_out": nrm(ks[3], (L, D_FF, D_MODEL), BETA * D_FF ** -0.5),
        "ln1_g": gain(ks[4]),
        "ln1_b": bias(ks[5]),
        "mix_w_in": nrm(ks[6], (L, D_MODEL, IN_COLS), D_MODEL ** -0.5),
        "conv_w": nrm(ks[7], (L, CONV_K, CONV_CH), CONV_K ** -0.5),
        "conv_b": bias(ks[8], CONV_CH),
        "conv_w_out": nrm(ks[9], (L, CONV_CH, D_MODEL), BETA * CONV_CH ** -0.5),
        "ssm_lam_re": lam_re,
        "ssm_lam_im": lam_im,
        "ssm_log_step": log_step,
        "ssm_b_re": nrm(ks[23], (L, SSM_GROUPS, SSM_STATE, SSM_GROUP), (2.0 * SSM_GROUP) ** -0.5),
        "ssm_b_im": nrm(ks[24], (L, SSM_GROUPS, SSM_STATE, SSM_GROUP), (2.0 * SSM_GROUP) ** -0.5),
        "ssm_c_re": nrm(ks[25], (L, SSM_GROUPS, SSM_GROUP, SSM_STATE), (2.0 * SSM_STATE) ** -0.5),
        "ssm_c_im": nrm(ks[26], (L, SSM_GROUPS, SSM_GROUP, SSM_STATE), (2.0 * SSM_STATE) ** -0.5),
        "ssm_d": nrm(ks[27], (L, SSM_GROUPS, SSM_GROUP), 1.0),
        "ssm_w_glu": nrm(ks[28], (L, SSM_WIDTH, 2 * D_MODEL), BETA * SSM_WIDTH ** -0.5),
        "mix_w_out": nrm(ks[29], (L, D_MODEL, D_MODEL), BETA * D_MODEL ** -0.5),
        "ln2_g": gain(ks[10]),
        "ln2_b": bias(ks[11]),
        "ffn2_w_in": nrm(ks[12], (L, D_MODEL, 2 * D_FF), D_MODEL ** -0.5),
        "ffn2_w_out": nrm(ks[13], (L, D_FF, D_MODEL), BETA * D_FF ** -0.5),
        "ln3_g": gain(ks[14]),
        "ln3_b": bias(ks[15]),
        "ple_w_in": nrm(ks[16], (L, PLE_DIM, D_MODEL), BETA * PLE_DIM ** -0.5),
        "ple_w_gate": nrm(ks[17], (L, D_MODEL, D_MODEL), D_MODEL ** -0.5),
        "ln4_g": gain(ks[18]),
        "ln4_b": bias(ks[19]),
    }


def reference(x, p, ffn1_w_in, ffn1_w_out, ln1_g, ln1_b, mix_w_in, conv_w, conv_b,
              conv_w_out, ssm_lam_re, ssm_lam_im, ssm_log_step, ssm_b_re, ssm_b_im,
              ssm_c_re, ssm_c_im, ssm_d, ssm_w_glu, mix_w_out, ln2_g, ln2_b,
              ffn2_w_in, ffn2_w_out, ln3_g, ln3_b, ple_w_in, ple_w_gate, ln4_g, ln4_b):
    for i in range(DEPTH):
        x = layer_norm(ALPHA * x + 0.5 * swiglu(x, ffn1_w_in[i], ffn1_w_out[i]),
                       ln1_g[i], ln1_b[i])
        mix = token_mixer(x, mix_w_in[i], conv_w[i], conv_b[i], conv_w_out[i],
                          ssm_lam_re[i], ssm_lam_im[i], ssm_log_step[i],
                          ssm_b_re[i], ssm_b_im[i], ssm_c_re[i], ssm_c_im[i], ssm_d[i],
                          ssm_w_glu[i], mix_w_out[i])
        x = layer_norm(ALPHA * x + mix, ln2_g[i], ln2_b[i])
        x = layer_norm(ALPHA * x + 0.5 * swiglu(x, ffn2_w_in[i], ffn2_w_out[i]),
                       ln3_g[i], ln3_b[i])
        e = (p[i] @ ple_w_in[i]) * jax.nn.sigmoid(x @ ple_w_gate[i])
        x = layer_norm(ALPHA * x + e, ln4_g[i], ln4_b[i])
    return x
```

```python
import math
import os
from contextlib import ExitStack

import numpy as np
import concourse.bass as bass
import concourse.mybir as mybir
from concourse.bass_utils import run_bass_kernel_spmd

F32 = mybir.dt.float32
BF16 = mybir.dt.bfloat16
U8 = mybir.dt.uint8
AF = mybir.ActivationFunctionType
ALU = mybir.AluOpType

D = 1024
SEQ = 2048
DFF = 2816
NF = DFF // 128
PLE = 256
ALPHA = 2.0 ** 0.25
EPS = 1e-5
NCORES = int(os.environ.get('KDBG_CORES', '8'))
STAGE = 99


class Eng:
    def __init__(self, name, h, sem):
        self.name, self.h, self.sem = name, h, sem
        self.count = 0
        self.waited = {}


class DSem:
    def __init__(self, h):
        self.h = h
        self.count = 0


class Buf:
    __slots__ = ("w", "r", "ds", "name")

    def __init__(self, name=""):
        self.w = None
        self.r = []
        self.ds = None
        self.name = name


class KB:
    def __init__(self, nc, es):
        self.nc = nc
        self.es = es
        self.eng = {}
        for name, h in (("pe", nc.tensor), ("act", nc.scalar), ("dve", nc.vector),
                        ("pool", nc.gpsimd), ("sp", nc.sync)):
            sem = es.enter_context(nc.semaphore("s_" + name))
            self.eng[name] = Eng(name, h, sem)
        self.dsems = []
        self.nds = 0
        self.defer = False
        self.deferred_q = []
        self.chain = False

    def new_dsem(self):
        h = self.es.enter_context(self.nc.semaphore("d%d" % self.nds))
        self.nds += 1
        d = DSem(h)
        self.dsems.append(d)
        return d

    def _wait(self, E, tok):
        key, sem, val, owner = tok
        if owner == E.name and (owner == "pe" or self.chain):
            return
        if owner is not None:
            assert self.eng[owner].count >= val, "pending token"
        if E.waited.get(key, 0) >= val:
            return
        E.h.wait_ge(sem, val)
        E.waited[key] = val

    def _deps(self, E, reads, writes):
        for b in reads:
            if b.w is not None:
                self._wait(E, b.w)
        for b in writes:
            if b.w is not None:
                self._wait(E, b.w)
            for t in b.r:
                self._wait(E, t)

    def _commit(self, tok, reads, writes):
        for b in writes:
            b.w = tok
            b.r = []
        for b in reads:
            if b not in writes:
                b.r = [t for t in b.r if t[0] != tok[0]] + [tok]

    def run_deferred(self, n):
        q = self.deferred_q
        for _ in range(min(n, len(q))):
            a = q.pop(0)
            if a[0] == "__dma__":
                self.dma(*a[1:], _replay=True)
            else:
                self.op(*a, _replay=True)

    def op(self, en, fn, reads=(), writes=(), mark=True, _replay=False):
        if self.defer and not _replay:
            self.deferred_q.append((en, fn, list(reads), list(writes), mark))
            return None
        E = self.eng[en]
        self._deps(E, reads, writes)
        ins = fn(E.h)
        if en != "pe":
            mark = True
        if mark:
            ins.then_inc(E.sem, 1)
            E.count += 1
            tok = ("e_" + en, E.sem, E.count, en)
        else:
            tok = ("e_" + en, E.sem, E.count + 1, en)
        self._commit(tok, reads, writes)
        return ins

    def dma(self, qn, pairs, reads=(), writes=(), ds=None, slow=False, eager=False, _replay=False):
        if self.defer and not eager and not _replay:
            self.deferred_q.append(("__dma__", qn, list(pairs), list(reads), list(writes), ds, slow))
            return
        Q = self.eng[qn]
        self._deps(Q, reads, writes)
        if ds is None:
            tgt = writes[0] if writes else reads[0]
            if tgt.ds is None:
                tgt.ds = self.new_dsem()
            ds = tgt.ds
        for (o, i) in pairs:
            if slow:
                Q.h.dma_start(out=o, in_=i, allow_slow_non_contiguous=True).then_inc(ds.h, 16)
            else:
                Q.h.dma_start(out=o, in_=i).then_inc(ds.h, 16)
            ds.count += 16
        tok = ("d%d" % id(ds), ds.h, ds.count, None)
        self._commit(tok, reads, writes)

    def handoff(self, old, new):
        toks = {}
        for b in old:
            for t in ([b.w] if b.w is not None else []) + list(b.r):
                if t[0] not in toks or toks[t[0]][2] < t[2]:
                    toks[t[0]] = t
        for b in new:
            b.w = None
            b.r = list(toks.values())

    def finish(self):
        for E in self.eng.values():
            for O in self.eng.values():
                if O is not E and O.count > 0:
                    self._wait(E, ("e_" + O.name, O.sem, O.count, O.name))
        sp = self.eng["sp"]
        for d in self.dsems:
            if d.count > 0:
                sp.h.wait_ge(d.h, d.count)


def build_program():
    nc = bass.Bass("TRN2", target_bir_lowering=False)
    es = ExitStack()

    def din(name, shape):
        return nc.dram_tensor(name, list(shape), F32, kind="ExternalInput").ap()

    x_d = din("x", (SEQ, D))
    p_d = din("p", (SEQ, PLE))
    w1i_d = din("ffn1_w_in", (D, 2 * DFF))
    w1o_d = din("ffn1_w_out", (DFF, D))
    w2i_d = din("ffn2_w_in", (D, 2 * DFF))
    w2o_d = din("ffn2_w_out", (DFF, D))
    lng_d = [din("ln%d_g" % i, (1, D)) for i in range(1, 5)]
    lnb_d = [din("ln%d_b" % i, (1, D)) for i in range(1, 5)]
    mwi_d = din("mix_w_in", (D, 4096))
    cw_d = din("conv_w", (3, 512))
    cb_d = din("conv_b", (1, 512))
    cwo_d = din("conv_w_out", (512, D))
    lre_d = din("ssm_lam_re", (32, 64))
    lim_d = din("ssm_lam_im", (32, 64))
    lst_d = din("ssm_log_step", (1, 32))
    bre_d = din("ssm_b_re", (32, 64, 16))
    bim_d = din("ssm_b_im", (32, 64, 16))
    cre_d = din("ssm_c_re", (32, 16, 64))
    cim_d = din("ssm_c_im", (32, 16, 64))
    dsk_d = din("ssm_d", (32, 16))
    glu_d = din("ssm_w_glu", (512, 2 * D))
    mwo_d = din("mix_w_out", (D, D))
    pwi_d = din("ple_w_in", (PLE, D))
    pwg_d = din("ple_w_gate", (D, D))
    out_d = nc.dram_tensor("out", [SEQ, D], F32, kind="ExternalOutput").ap()

    KBY = 1024
    OFF = {}
    cur = 0

    def region(name, nbytes):
        nonlocal cur
        OFF[name] = cur
        cur += nbytes

    region("X", 32 * KBY)
    region("XT", 16 * KBY)
    region("GB", 8 * KBY)
    region("R1", 44 * KBY)
    region("R2", 44 * KBY)
    region("R3", 24 * KBY)
    region("S5C", 24 * KBY)
    MISC_BYTES = 16000
    region("MISC", MISC_BYTES)
    ARENA_BYTES = cur
    arena = es.enter_context(nc.sbuf_tensor("arena", [128, ARENA_BYTES], U8))

    def view(off, dtype, shape):
        n = 1
        for s in shape:
            n *= s
        nb = n * (4 if dtype == F32 else 2)
        a = arena[:, off:off + nb].bitcast(dtype)
        if len(shape) == 1:
            return a
        names = " ".join("d%d" % i for i in range(len(shape)))
        kw = {"d%d" % i: shape[i] for i in range(len(shape) - 1)}
        return a.rearrange("p (%s) -> p %s" % (names, names), **kw)

    X = view(OFF["X"], F32, [8, 1024])
    XT = view(OFF["XT"], BF16, [8, 1024])
    GB = view(OFF["GB"], F32, [2, 1024])
    H = view(OFF["R1"], BF16, [NF, 1024])
    WOUT = view(OFF["R2"], BF16, [NF, 1024])
    WS = [view(OFF["R3"] + 8 * KBY * i, BF16, [4096]) for i in range(3)]
    TOEP = view(OFF["S5C"], BF16, [32, 128])
    PT = view(OFF["S5C"] + 8 * KBY, BF16, [32, 2, 64])
    QT = view(OFF["S5C"] + 16 * KBY, BF16, [32, 128])
    mo = OFF["MISC"]

    def misc(dtype, shape):
        nonlocal mo
        n = 1
        for s in shape:
            n *= s
        v = view(mo, dtype, shape)
        mo += n * (4 if dtype == F32 else 2)
        mo = (mo + 63) // 64 * 64
        return v

    IDF = misc(F32, [128])
    IDB = misc(BF16, [128])
    MASK4 = misc(F32, [512])
    CW = misc(F32, [4, 4])
    VH = misc(F32, [4, 8])
    SC = misc(F32, [16, 2])
    A1 = misc(F32, [16, 2])
    A2 = misc(F32, [16, 2])
    DSK = misc(F32, [32])
    LNS0 = misc(F32, [4])
    LNS1 = misc(F32, [4])
    LNA0 = misc(F32, [4])
    LNA1 = misc(F32, [4])
    LNS = [LNS0, LNS1]
    LNA = [LNA0, LNA1]
    HALFPI = misc(F32, [1])
    SM = misc(F32, [22, 16])
    PW16 = misc(F32, [2, 16, 16])
    A1C = misc(F32, [16, 2])
    A2C = misc(F32, [16, 2])
    KA1 = [misc(F32, [16, 2]) for _ in range(4)]
    KA2 = [misc(F32, [16, 2]) for _ in range(4)]
    PWF = misc(F32, [2, 16, 8])
    PWR = misc(F32, [2, 16, 8])

    PSALL = es.enter_context(nc.psum_tensor("psall", [128, 4096], F32))
    PS = [PSALL[:, 512 * i:512 * (i + 1)] for i in range(8)]

    kb = KB(nc, es)
    TMPA = [misc(F32, [512]) for _ in range(2)]
    EPS4 = misc(F32, [1])
    EPS1 = misc(F32, [1])
    assert mo <= OFF["MISC"] + MISC_BYTES, (mo - OFF["MISC"])
    if os.environ.get("KDBG_PRINT"):
        print("MISC used", mo - OFF["MISC"], "of", MISC_BYTES)

    bX = [Buf() for _ in range(8)]
    bXT = [Buf() for _ in range(8)]
    bGB = Buf()
    bH = [[Buf() for _ in range(2)] for _ in range(NF)]
    WCH = [(0, 6), (6, 12), (12, 17), (17, 22)]
    bWOUT = [Buf() for _ in range(4)]
    bWS = [Buf() for _ in range(3)]
    bPS = [Buf() for _ in range(8)]
    bS5C = Buf()
    bPRO = Buf()
    bMISC = Buf()
    bTMPA = [Buf(), Buf()]
    bST = [Buf(), Buf()]
    bSTA = [Buf(), Buf()]
    bVH = Buf()
    bSC = Buf()

    def chunk_of(k):
        for i, (a, b) in enumerate(WCH):
            if a <= k < b:
                return i

    rr = {"A": 0, "B": 0, "slot": 0, "tmp": 0, "ev": 0}

    def nextA():
        rr["A"] = (rr["A"] + 1) % 4
        return rr["A"]

    def nextB():
        rr["B"] = (rr["B"] + 1) % 4
        return 4 + rr["B"]

    def nextAll():
        rr["all"] = (rr.get("all", 0) + 1) % 8
        return rr["all"]

    def nextBpair():
        rr["Bp"] = (rr.get("Bp", 0) + 1) % 2
        return 4 + 2 * rr["Bp"]

    def next_slot():
        rr["slot"] = (rr["slot"] + 1) % 3
        return rr["slot"]

    def next_tmp():
        rr["tmp"] = (rr["tmp"] + 1) % 2
        return rr["tmp"]

    def ev_eng():
        rr["ev"] = (rr["ev"] + 1) % 2
        return "act" if rr["ev"] else "dve"

    def mm(out, lhsT, rhs, start, stop, reads, writes, mark):
        kb.op("pe", lambda e: e.matmul(out, lhsT=lhsT, rhs=rhs, start=start, stop=stop),
              reads, writes, mark)

    def copy_op(en, out, in_, reads, writes, mark=True):
        if en == "act":
            kb.op("act", lambda e: e.activation(out=out, in_=in_, func=AF.Copy), reads, writes, mark)
        else:
            kb.op(en, lambda e: e.tensor_copy(out, in_), reads, writes, mark)

    def tt(en, out, a, b, op, reads, writes, mark=True):
        kb.op(en, lambda e: e.tensor_tensor(out, a, b, op), reads, writes, mark)

    def ts(en, out, a, s1, s2, op0, op1, reads, writes, mark=True):
        if s2 is None:
            kb.op(en, lambda e: e.tensor_scalar(out, a, s1, None, op0), reads, writes, mark)
        else:
            kb.op(en, lambda e: e.tensor_scalar(out, a, s1, s2, op0, op1), reads, writes, mark)

    def stt(out, in0, scalar, in1, op0, op1, reads, writes, mark=True):
        kb.op("dve", lambda e: e.scalar_tensor_tensor(out=out, in0=in0, scalar=scalar, in1=in1,
                                                      op0=op0, op1=op1), reads, writes, mark)

    def act(out, in_, func, reads, writes, bias=None, scale=None, mark=True):
        kw = {}
        if bias is not None:
            kw["bias"] = bias
        if scale is not None:
            kw["scale"] = scale
        kb.op("act", lambda e: e.activation(out=out, in_=in_, func=func, **kw), reads, writes, mark)

    kb.op("dve", lambda e: e.memset(IDF, 1.0), writes=[bMISC], mark=False)
    kb.op("dve", lambda e: e.memset(MASK4, 1.0), writes=[bMISC], mark=False)
    kb.op("dve", lambda e: e.memset(HALFPI, math.pi / 2), writes=[bMISC], mark=False)
    kb.op("dve", lambda e: e.memset(EPS4, 4 * EPS), writes=[bMISC], mark=False)
    kb.op("dve", lambda e: e.memset(EPS1, EPS), writes=[bMISC], mark=False)
    kb.op("dve", lambda e: e.memset(VH, 0.0), writes=[bVH], mark=False)
    kb.op("dve", lambda e: e.memset(SC, 0.0), writes=[bSC], mark=True)
    kb.op("pool", lambda e: e.affine_select(out=IDF, in_=IDF, pattern=[[-1, 128]],
                                            compare_op=ALU.is_equal, fill=0.0, base=0,
                                            channel_multiplier=1), writes=[bMISC])
    kb.op("pool", lambda e: e.affine_select(out=MASK4.rearrange("p (a b c) -> p a b c", a=4, b=8),
                                            in_=MASK4.rearrange("p (a b c) -> p a b c", a=4, b=8),
                                            pattern=[[0, 4], [16, 8], [0, 16]],
                                            compare_op=ALU.is_ge, fill=0.0, base=15,
                                            channel_multiplier=-1), writes=[bMISC])
    copy_op("dve", IDB, IDF, [bMISC], [bMISC])

    x_v = x_d.rearrange("(m c s) d -> m c s d", m=2, s=8)
    o_v = out_d.rearrange("(m c s) d -> m c s d", m=2, s=8)
    p_v = p_d.rearrange("(m c s) d -> m c s d", m=2, s=8)

    def load_x(m):
        for s in range(8):
            kb.dma("sp", [(X[:, s, :], x_v[m, :, s, :])], writes=[bX[s]])

    def store_x(m):
        for s in range(8):
            kb.dma("sp", [(o_v[m, :, s, :], X[:, s, :])], reads=[bX[s]])

    def transposes_tile(s):
        for kbk in range(2):
            b = nextA()
            for kk in range(4):
                k = 4 * kbk + kk
                kb.op("pe", lambda e: e.transpose(PS[b][:, kk * 128:(kk + 1) * 128],
                                                  X[:, s, k * 128:(k + 1) * 128], IDF),
                      reads=[bX[s], bMISC], writes=[bPS[b]], mark=(kk == 3))
            copy_op(ev_eng(), XT[:, 4 * kbk:4 * kbk + 4, s * 128:(s + 1) * 128],
                    PS[b][:].rearrange("p (a b) -> p a b", a=4), [bPS[b]], [bXT[s]])

    pending_tr = []

    def flush_pending():
        while pending_tr:
            transposes_tile(pending_tr.pop(0))

    def load_gb(i):
        kb.dma("sp", [(GB[:, 0, :], lng_d[i][0].partition_broadcast(128)),
                      (GB[:, 1, :], lnb_d[i][0].partition_broadcast(128))], writes=[bGB])

    def ln(s, src, src_bufs, scal, epsap, junk, junk_buf):
        sl = s % 2
        xs = X[:, s, :]
        kb.op("dve", lambda e: e.scalar_tensor_tensor(out=xs, in0=xs, scalar=scal, in1=src, op0=ALU.mult,
                                                      op1=ALU.add, accum_out=LNS[sl][:, 0:1]),
              reads=src_bufs, writes=[bX[s], bST[sl]])
        kb.op("act", lambda e: e.activation(out=junk, in_=xs, func=AF.Square, accum_out=LNA[sl][:, 0:1]),
              reads=[bX[s]], writes=[junk_buf, bSTA[sl]])
        ts("dve", LNS[sl][:, 1:2], LNS[sl][:, 0:1], -1.0 / 1024, None, ALU.mult, None, [bST[sl]], [bST[sl]])
        tt("dve", LNS[sl][:, 2:3], LNS[sl][:, 1:2], LNS[sl][:, 1:2], ALU.mult, [bST[sl]], [bST[sl]])
        stt(LNS[sl][:, 3:4], LNA[sl][:, 0:1], 1.0 / 1024, LNS[sl][:, 2:3], ALU.mult, ALU.subtract,
            [bST[sl], bSTA[sl]], [bST[sl]])
        act(LNA[sl][:, 1:2], LNS[sl][:, 3:4], AF.Ln, [bST[sl], bMISC], [bSTA[sl]], bias=epsap)
        act(LNA[sl][:, 2:3], LNA[sl][:, 1:2], AF.Exp, [bSTA[sl]], [bSTA[sl]], scale=-0.5)
        act(LNA[sl][:, 3:4], LNS[sl][:, 1:2], AF.Identity, [bST[sl], bSTA[sl]], [bSTA[sl]],
            scale=LNA[sl][:, 2:3])
        act(xs, xs, AF.Identity, [bSTA[sl]], [bX[s]], bias=LNA[sl][:, 3:4], scale=LNA[sl][:, 2:3])
        tt("pool", xs, xs, GB[:, 0, :], ALU.mult, [bGB], [bX[s]])
        tt("pool", xs, xs, GB[:, 1, :], ALU.add, [bGB], [bX[s]])

    bSM = Buf()
    GBs = view(OFF["GB"], F32, [6, 16, 16])
    PN = view(OFF["R2"], F32, [16, 2, 128])
    QN = view(OFF["R2"] + 16 * KBY, F32, [16, 2, 128])
    TA = view(OFF["R2"] + 32 * KBY, F32, [16, 128])
    TB = view(OFF["S5C"], F32, [16, 128])
    TMASK = view(OFF["R2"] + 40 * KBY, F32, [512])

    def T(i):
        return SM[:, i, :]

    def s5_prologue_math():
        e = "dve"
        (LR, LI, LS, DT, AA, TH, MAG, ZR, ZI, T1, T2, T3, T4,
         AR, DEN, WR, WI, NR, NI, IR, II) = [T(i) for i in range(21)]
        pairs = []
        for par in range(2):
            sl = slice(par * 64, (par + 1) * 64)
            pairs += [(LR[sl, :], lre_d[par:32:2, :].rearrange("g n -> n g")),
                      (LI[sl, :], lim_d[par:32:2, :].rearrange("g n -> n g")),
                      (LS[sl, :], lst_d[0, par:32:2].partition_broadcast(64))]
        kb.dma("sp", pairs, writes=[bSM], slow=True, eager=True)
        pairs = []
        for par in range(2):
            sl = slice(par * 64, (par + 1) * 64)
            pairs += [(GBs[sl, 0, :, :], bre_d[par:32:2].rearrange("g n j -> n g j")),
                      (GBs[sl, 1, :, :], bim_d[par:32:2].rearrange("g n j -> n g j"))]
            for gp in range(16):
                pairs += [(GBs[sl, 2, gp, :], cre_d[2 * gp + par].rearrange("i n -> n i")),
                          (GBs[sl, 3, gp, :], cim_d[2 * gp + par].rearrange("i n -> n i"))]
        for s in range(8):
            pairs.append((DSK[s * 16:(s + 1) * 16, :], dsk_d.rearrange("g j -> j g")))
        for k_ in range(3):
            pairs.append((CW[:, :, k_], cw_d[k_].rearrange("(ct p) -> p ct", p=128)))
        pairs.append((CW[:, :, 3], cb_d[0].rearrange("(ct p) -> p ct", p=128)))
        kb.dma("sp", pairs, writes=[bPRO], slow=True, eager=True)
        R, W = [bSM, bMISC], [bSM]
        act(DT, LS, AF.Exp, R, W)
        tt(e, AA, LR, DT, ALU.mult, R, W, mark=False)
        tt(e, TH, LI, DT, ALU.mult, R, W, mark=True)
        act(MAG, AA, AF.Exp, R, W, scale=1.0 / 32)
        act(ZI, TH, AF.Sin, R, W, scale=1.0 / 32)
        act(ZR, TH, AF.Sin, R, W, scale=1.0 / 32, bias=HALFPI)
        tt(e, ZR, ZR, MAG, ALU.mult, R, W, mark=False)
        tt(e, ZI, ZI, MAG, ALU.mult, R, W, mark=False)
        for _ in range(5):
            tt(e, T1, ZR, ZR, ALU.mult, R, W, mark=False)
            tt(e, T2, ZI, ZI, ALU.mult, R, W, mark=False)
            tt(e, T3, ZR, ZI, ALU.mult, R, W, mark=False)
            tt(e, ZR, T1, T2, ALU.subtract, R, W, mark=False)
            ts(e, ZI, T3, 2.0, None, ALU.mult, None, R, W, mark=False)
        kb.op(e, lambda en: en.memset(PWR[:, 0, :, 7], 1.0), reads=R, writes=W, mark=False)
        kb.op(e, lambda en: en.memset(PWR[:, 1, :, 7], 0.0), reads=R, writes=W, mark=False)
        copy_op(e, PWF[:, 0, :, 0], ZR, R, W, mark=False)
        copy_op(e, PWF[:, 1, :, 0], ZI, R, W, mark=False)
        for k in range(2, 9):
            pr, pi = PWF[:, 0, :, k - 2], PWF[:, 1, :, k - 2]
            nr, ni = PWF[:, 0, :, k - 1], PWF[:, 1, :, k - 1]
            tt(e, T1, pr, ZR, ALU.mult, R, W, mark=False)
            tt(e, T2, pi, ZI, ALU.mult, R, W, mark=False)
            tt(e, nr, T1, T2, ALU.subtract, R, W, mark=False)
            tt(e, T3, pr, ZI, ALU.mult, R, W, mark=False)
            tt(e, T4, pi, ZR, ALU.mult, R, W, mark=False)
            tt(e, ni, T3, T4, ALU.add, R, W, mark=False)
        for k in range(1, 8):
            copy_op(e, PWR[:, 0, :, 7 - k], PWF[:, 0, :, k - 1], R, W, mark=False)
            copy_op(e, PWR[:, 1, :, 7 - k], PWF[:, 1, :, k - 1], R, W, mark=False)
        ts(e, AR, ZR, -1.0, None, ALU.add, None, R, W, mark=False)
        tt(e, T1, LR, LR, ALU.mult, R, W, mark=False)
        tt(e, T2, LI, LI, ALU.mult, R, W, mark=False)
        tt(e, DEN, T1, T2, ALU.add, R, W, mark=False)
        kb.op(e, lambda en: en.reciprocal(DEN, DEN), reads=R, writes=W, mark=False)
        tt(e, T1, AR, LR, ALU.mult, R, W, mark=False)
        tt(e, T2, ZI, LI, ALU.mult, R, W, mark=False)
        tt(e, T1, T1, T2, ALU.add, R, W, mark=False)
        tt(e, WR, T1, DEN, ALU.mult, R, W, mark=False)
        tt(e, T1, ZI, LR, ALU.mult, R, W, mark=False)
        tt(e, T2, AR, LI, ALU.mult, R, W, mark=False)
        tt(e, T1, T1, T2, ALU.subtract, R, W, mark=False)
        tt(e, WI, T1, DEN, ALU.mult, R, W, mark=False)
        L8r, L8i = PWF[:, 0, :, 7], PWF[:, 1, :, 7]
        tt(e, T1, L8r, L8r, ALU.mult, R, W, mark=False)
        tt(e, T2, L8i, L8i, ALU.mult, R, W, mark=False)
        tt(e, T1, T1, T2, ALU.add, R, W, mark=False)
        kb.op(e, lambda en: en.reciprocal(T1, T1), reads=R, writes=W, mark=False)
        tt(e, IR, L8r, T1, ALU.mult, R, W, mark=False)
        stt(II, L8i, -1.0, T1, ALU.mult, ALU.mult, R, W, mark=False)
        copy_op(e, A1[:, :, 0], L8r, R, W, mark=False)
        copy_op(e, A1[:, :, 1], L8r, R, W, mark=False)
        ts(e, A2[:, :, 0], L8i, -1.0, None, ALU.mult, None, R, W, mark=False)
        copy_op(e, A2[:, :, 1], L8i, R, W, mark=True)
        copy_op(e, PW16[:, 0, :, 0], L8r, R, W)
        copy_op(e, PW16[:, 1, :, 0], L8i, R, W)
        for k in range(1, 16):
            pr, pi = PW16[:, 0, :, k - 1], PW16[:, 1, :, k - 1]
            nr, ni = PW16[:, 0, :, k], PW16[:, 1, :, k]
            tt(e, T1, pr, L8r, ALU.mult, R, W)
            tt(e, T2, pi, L8i, ALU.mult, R, W)
            tt(e, nr, T1, T2, ALU.subtract, R, W)
            tt(e, T3, pr, L8i, ALU.mult, R, W)
            tt(e, T4, pi, L8r, ALU.mult, R, W)
            tt(e, ni, T3, T4, ALU.add, R, W)
        copy_op(e, A1C[:, :, 0], PW16[:, 0, :, 15], R, W)
        copy_op(e, A1C[:, :, 1], PW16[:, 0, :, 15], R, W)
        ts(e, A2C[:, :, 0], PW16[:, 1, :, 15], -1.0, None, ALU.mult, None, R, W)
        copy_op(e, A2C[:, :, 1], PW16[:, 1, :, 15], R, W)
        KR, KI = T(17), T(18)
        for r in range(4):
            if r == 0:
                copy_op(e, KR, PW16[:, 0, :, 7], R, W)
                copy_op(e, KI, PW16[:, 1, :, 7], R, W)
            elif r == 1:
                copy_op(e, KR, PW16[:, 0, :, 15], R, W)
                copy_op(e, KI, PW16[:, 1, :, 15], R, W)
            else:
                tt(e, T1, KR, KR, ALU.mult, R, W)
                tt(e, T2, KI, KI, ALU.mult, R, W)
                tt(e, T3, KR, KI, ALU.mult, R, W)
                tt(e, KR, T1, T2, ALU.subtract, R, W)
                ts(e, KI, T3, 2.0, None, ALU.mult, None, R, W)
            copy_op(e, KA1[r][:, :, 0], KR, R, W)
            copy_op(e, KA1[r][:, :, 1], KR, R, W)
            ts(e, KA2[r][:, :, 0], KI, -1.0, None, ALU.mult, None, R, W)
            copy_op(e, KA2[r][:, :, 1], KI, R, W)
        R2_, W2_ = [bSM, bPRO], [bPRO]
        Bre, Bim, Cre, Cim, BBr, BBi = [GBs[:, i, :, :] for i in range(6)]
        wrb = WR.unsqueeze(2).broadcast_to([128, 16, 16])
        wib = WI.unsqueeze(2).broadcast_to([128, 16, 16])
        TAs = TA[:, :, 0:16]
        tt(e, BBr, Bre, wrb, ALU.mult, R2_, W2_, mark=False)
        tt(e, TAs, Bim, wib, ALU.mult, R2_, W2_, mark=False)
        tt(e, BBr, BBr, TAs, ALU.subtract, R2_, W2_, mark=False)
        tt(e, BBi, Bim, wrb, ALU.mult, R2_, W2_, mark=False)
        tt(e, TAs, Bre, wib, ALU.mult, R2_, W2_, mark=False)
        tt(e, BBi, BBi, TAs, ALU.add, R2_, W2_, mark=False)

        def v4(ap):
            return ap.rearrange("p g (s j) -> p g s j", s=8)

        def pw_b(pw, ri):
            return pw[:, ri, :, :].unsqueeze(3).broadcast_to([128, 16, 8, 16])

        def bc_b(x):
            return x.unsqueeze(2).broadcast_to([128, 16, 8, 16])

        PNr, PNi = v4(PN[:, :, 0, :]), v4(PN[:, :, 1, :])
        QNr, QNi = v4(QN[:, :, 0, :]), v4(QN[:, :, 1, :])
        TA4, TB4 = v4(TA), v4(TB)
        R3_, W3_ = [bSM, bPRO, bS5C], [bPRO, bS5C]
        tt(e, PNr, pw_b(PWR, 0), bc_b(BBr), ALU.mult, R3_, W3_, mark=False)
        tt(e, TA4, pw_b(PWR, 1), bc_b(BBi), ALU.mult, R3_, W3_, mark=False)
        tt(e, PNr, PNr, TA4, ALU.subtract, R3_, W3_, mark=False)
        tt(e, PNi, pw_b(PWR, 0), bc_b(BBi), ALU.mult, R3_, W3_, mark=False)
        tt(e, TA4, pw_b(PWR, 1), bc_b(BBr), ALU.mult, R3_, W3_, mark=False)
        tt(e, PNi, PNi, TA4, ALU.add, R3_, W3_, mark=False)
        tt(e, QNr, pw_b(PWF, 0), bc_b(Cre), ALU.mult, R3_, W3_, mark=False)
        tt(e, TA4, pw_b(PWF, 1), bc_b(Cim), ALU.mult, R3_, W3_, mark=False)
        tt(e, QNr, QNr, TA4, ALU.subtract, R3_, W3_, mark=False)
        tt(e, QNi, pw_b(PWF, 1), bc_b(Cre), ALU.mult, R3_, W3_, mark=False)
        tt(e, TA4, pw_b(PWF, 0), bc_b(Cim), ALU.mult, R3_, W3_, mark=False)
        stt(QNi, QNi, -1.0, TA4, ALU.mult, ALU.subtract, R3_, W3_, mark=True)
        QB = view(OFF["S5C"], BF16, [16, 2, 128])
        copy_op("act", QB, QN, R3_, W3_)
        for par in range(2):
            for ri in range(2):
                kb.dma("sp", [(QT[ri * 64:(ri + 1) * 64, par:32:2, :], QB[par * 64:(par + 1) * 64, :, ri, :])],
                       reads=[bPRO], writes=[bS5C])

    def s5_prologue_pe():
        e = "dve"
        R3_, W3_ = [bSM, bPRO, bS5C], [bPRO, bS5C]
        IR, II = T(19), T(20)
        parts = int(os.environ.get("KDBG_HOOKPARTS", "7"))
        for bk in range(8 if parts & 1 else 0):
            b = nextA()
            for gpl in range(2):
                gp = 2 * bk + gpl
                for ri in range(2):
                    c0 = (gpl * 2 + ri) * 128
                    kb.op("pe", lambda en: en.transpose(PS[b][:, c0:c0 + 128], PN[:, gp, ri, :], IDF),
                          reads=[bPRO, bMISC], writes=[bPS[b]], mark=(gpl == 1 and ri == 1))
            for gpl in range(2):
                gp = 2 * bk + gpl
                copy_op("act", PT[:, 2 * gp:2 * gp + 2, :, :],
                        PS[b][:, gpl * 256:(gpl + 1) * 256].rearrange("p (ri par n) -> p par ri n", ri=2, par=2),
                        [bPS[b]], [bS5C])
        if not parts & 2:
            return
        irb = IR.unsqueeze(2).broadcast_to([128, 16, 128])
        iib = II.unsqueeze(2).broadcast_to([128, 16, 128])
        Pr, Pi = PN[:, :, 0, :], PN[:, :, 1, :]
        tt(e, TA, Pr, irb, ALU.mult, R3_, W3_, mark=False)
        tt(e, TB, Pi, iib, ALU.mult, R3_, W3_, mark=False)
        tt(e, TA, TA, TB, ALU.subtract, R3_, W3_, mark=False)
        tt(e, TB, Pi, irb, ALU.mult, R3_, W3_, mark=False)
        tt(e, Pr, Pr, iib, ALU.mult, R3_, W3_, mark=False)
        tt(e, Pi, TB, Pr, ALU.add, R3_, W3_, mark=True)
        PPB = view(OFF["R2"] + 16 * KBY, BF16, [16, 2, 128])
        copy_op("act", PPB[:, :, 0, :], TA, R3_, W3_)
        copy_op("act", PPB[:, :, 1, :], Pi, R3_, W3_)
        PPS = view(OFF["R2"] + 32 * KBY, BF16, [32, 128])
        for par in range(2):
            for ri in range(2):
                kb.dma("sp", [(PPS[ri * 64:(ri + 1) * 64, par:32:2, :], PPB[par * 64:(par + 1) * 64, :, ri, :])],
                       reads=[bS5C], writes=[bPRO])
        if not parts & 4:
            return
        for bk in range(8):
            b = nextA()
            for gl in range(4):
                g = 4 * bk + gl
                gp, par = g // 2, g % 2
                sl = slice(par * 64, (par + 1) * 64)
                o = PS[b][:, gl * 128:(gl + 1) * 128]
                mm(o, PPS[:, g, :], QT[:, g, :], True, True, [bPRO, bS5C], [bPS[b]], gl == 3)
            tt(e, TMASK, PS[b][:], MASK4, ALU.mult, [bPS[b], bMISC], [bPRO], mark=False)
            for gl in range(4):
                g = 4 * bk + gl
                stt(TOEP[:, g, :], IDF, DSK[:, g:g + 1], TMASK[:, gl * 128:(gl + 1) * 128],
                    ALU.mult, ALU.add, [bPRO, bMISC], [bS5C], mark=(gl == 3))

    def ffn(wi_d, wo_d, ln_idx, hook=None):
        if hook is None:
            load_gb(ln_idx)
        wo_v = wo_d.rearrange("(kt p) n -> p kt n", p=128)
        wi_v = wi_d.rearrange("(kt p) c -> p kt c", p=128)
        slots = {}

        def load_w(j):
            sl = next_slot()
            slots[j] = sl
            wsv = WS[sl].rearrange("p (k c) -> p k c", k=8)
            kb.dma("pool", [(wsv[:, :, 0:256], wi_v[:, :, 256 * j:256 * j + 256]),
                            (wsv[:, :, 256:512], wi_v[:, :, DFF + 256 * j:DFF + 256 * j + 256])],
                   writes=[bWS[sl]])

        for j in range(3):
            load_w(j)
        def load_wout():
            kb.handoff(r2_users, bWOUT)
            for ci, (a, b_) in enumerate(WCH):
                kb.dma("pool", [(WOUT[:, a:b_, :], wo_v[:, a:b_, :])], writes=[bWOUT[ci]])

        if hook is None:
            load_wout()
        kb.handoff(r1_users, [bH[f][th] for f in range(NF) for th in range(2)])
        for j in range(11):
            sl = slots[j]
            wsv = WS[sl].rearrange("p (k c) -> p k c", k=8)
            for fl in range(2):
                f = 2 * j + fl
                for th in range(2):
                    ba, bb = nextAll(), nextAll()
                    xr = [bXT[4 * th + i] for i in range(4)]
                    for k in range(8):
                        mm(PS[ba][:], wsv[:, k, fl * 128:(fl + 1) * 128], XT[:, k, th * 512:(th + 1) * 512],
                           k == 0, k == 7, [bWS[sl]] + xr, [bPS[ba]], k == 7)
                    for k in range(8):
                        mm(PS[bb][:], wsv[:, k, 256 + fl * 128:256 + (fl + 1) * 128],
                           XT[:, k, th * 512:(th + 1) * 512],
                           k == 0, k == 7, [bWS[sl]] + xr, [bPS[bb]], k == 7)
                    ti = next_tmp()
                    act(TMPA[ti], PS[ba][:], AF.Silu, [bPS[ba]], [bTMPA[ti]])
                    tt("dve", H[:, f, th * 512:(th + 1) * 512], TMPA[ti], PS[bb][:], ALU.mult,
                       [bTMPA[ti], bPS[bb]], [bH[f][th]])
                    flush_pending()
                    if hook is not None:
                        kb.run_deferred(DEFER_RATE)
            if j + 3 < 11:
                load_w(j + 3)
            if hook is not None and j == HOOK_J:
                kb.run_deferred(100000)
                hook()
                load_wout()
                kb.handoff([bPRO], [bGB])
                load_gb(ln_idx)
        hs = [bH[f][th] for f in range(NF) for th in range(2)]
        for s in range(8):
            b0 = nextBpair()
            banks = [b0, b0 + 1]
            for hf in range(2):
                for k in range(NF):
                    mm(PS[banks[hf]][:], H[:, k, s * 128:(s + 1) * 128], WOUT[:, k, hf * 512:(hf + 1) * 512],
                       k == 0, k == NF - 1, [bH[k][s // 4], bWOUT[chunk_of(k)]], [bPS[banks[hf]]],
                       k == NF - 1)
            ln(s, PSALL[:, 512 * b0:512 * b0 + 1024], [bPS[b0], bPS[b0 + 1]], 2 * ALPHA, EPS4,
               TMPA[0].bitcast(BF16), bTMPA[0])
            if s > 1:
                transposes_tile(s - 2)
        transposes_tile(6)
        transposes_tile(7)
        return hs + bWOUT

    r1_users = []
    r2_users = [bPRO]

    def dump_and_finish(m):
        store_x(m)

    R1o, R2o = OFF["R1"], OFF["R2"]
    SU = view(R1o, F32, [8, 512])
    SU4 = view(R1o, F32, [32, 8, 16])
    EE_ = view(R1o, F32, [16, 2, 128])
    SSTK = view(R1o, BF16, [32, 128])
    MG = view(R1o, BF16, [8, 1024])
    VV = view(R1o + 16 * KBY, F32, [4, 8, 129])
    SS = view(R1o + 16 * KBY, F32, [16, 2, 129])
    MT = [view(R1o + 16 * KBY + 2 * KBY * i, F32, [512]) for i in range(6)]
    CBZ = view(R1o + 33 * KBY, BF16, [4, 1024])
    SCT8 = view(R1o + 41 * KBY, F32, [2, 16, 2, 8])
    UT = view(R2o, BF16, [32, 128])
    ST_ = view(R2o, BF16, [4, 1024])
    SBF = view(R2o + 8 * KBY, BF16, [16, 2, 128])
    ZZ = view(R2o, F32, [4, 1024])
    STM = view(R2o + 16 * KBY, BF16, [8, 512])
    WMO = view(R2o + 28 * KBY, BF16, [8, 1024])

    def mixer(m):
        mwi_v = mwi_d.rearrange("(kt p) c -> p kt c", p=128)
        glu_v = glu_d.rearrange("(kt p) c -> p kt c", p=128)
        cwo_v = cwo_d.rearrange("(kt p) c -> p kt c", p=128)
        mwo_v = mwo_d.rearrange("(kt p) c -> p kt c", p=128)
        load_gb(1)
        bSU, bE, bMG, bV, bS, bCBZ, bUT, bSBF, bZ, bSTM, bSTT, bWMO = [Buf() for _ in range(12)]
        bMT = [Buf() for _ in range(6)]
        r1_new = [bSU, bV, bCBZ]
        r2_new = [bZ, bSTM, bWMO]
        kb.handoff(r1_users, r1_new)
        kb.handoff(r2_users, r2_new)

        def load_slot(pairs_fn):
            sl = next_slot()
            kb.dma("pool", pairs_fn(sl), writes=[bWS[sl]])
            return sl

        def w8(sl):
            return WS[sl].rearrange("p (k c) -> p k c", k=8)

        sl_cc = [load_slot(lambda sl, q=q: [(w8(sl)[:, :, 0:256], mwi_v[:, :, 512 + 256 * q:768 + 256 * q]),
                                            (w8(sl)[:, :, 256:512], mwi_v[:, :, 1024 + 256 * q:1280 + 256 * q])])
                 for q in range(2)]
        sl_su = load_slot(lambda sl: [(w8(sl), mwi_v[:, :, 1536:2048])])
        kb.dma("pool", [(WMO, mwo_v)], writes=[bWMO])
        for ct in range(4):
            sl = sl_cc[ct // 2]
            ctl = ct % 2
            for th in range(2):
                ba, bb = nextA(), nextA()
                xr = [bXT[4 * th + i] for i in range(4)]
                for k in range(8):
                    mm(PS[ba][:], w8(sl)[:, k, ctl * 128:(ctl + 1) * 128], XT[:, k, th * 512:(th + 1) * 512],
                       k == 0, k == 7, [bWS[sl]] + xr, [bPS[ba]], k == 7)
                for k in range(8):
                    mm(PS[bb][:], w8(sl)[:, k, 256 + ctl * 128:256 + (ctl + 1) * 128],
                       XT[:, k, th * 512:(th + 1) * 512],
                       k == 0, k == 7, [bWS[sl]] + xr, [bPS[bb]], k == 7)
                ti = next_tmp()
                copy_op("act", TMPA[ti], PS[bb][:], [bPS[bb]], [bTMPA[ti]])
                tt("dve", VV[:, ct, 4 * th:4 * th + 4, 1:129], PS[ba][:].rearrange("p (a b) -> p a b", a=4),
                   TMPA[ti].rearrange("p (a b) -> p a b", a=4), ALU.mult,
                   [bPS[ba], bTMPA[ti]], [bV])
                flush_pending()
        sl_cb = load_slot(lambda sl: [(w8(sl), mwi_v[:, :, 0:512])])
        for s in range(8):
            b = nextB()
            for k in range(8):
                mm(PS[b][:], XT[:, k, s * 128:(s + 1) * 128], w8(sl_su)[:, k, :], k == 0, k == 7,
                   [bWS[sl_su], bXT[s]], [bPS[b]], k == 7)
            copy_op(ev_eng(), SU4[:, :, s, :], PS[b][:].rearrange("p (g j) -> p g j", g=32), [bPS[b]], [bSU])
        for ct in range(4):
            copy_op("dve", VV[:, ct, :, 0], VH[:, ct, :], [bVH], [bV], mark=False)
            zv = ZZ[:, ct, :].rearrange("p (a b) -> p a b", a=8)
            act(zv, VV[:, ct, :, 1:129], AF.Identity, [bV, bPRO], [bZ], bias=CW[:, ct, 3:4],
                scale=CW[:, ct, 2:3])
            w1, w0 = CW[:, ct, 1:2], CW[:, ct, 0:1]
            stt(zv[:, 1:8, :], VV[:, ct, 0:7, 1:129], w1, zv[:, 1:8, :], ALU.mult, ALU.add, [bV, bPRO], [bZ], False)
            stt(zv[:, 0, :], VV[:, ct, 7, 0:128], w1, zv[:, 0, :], ALU.mult, ALU.add, [bV, bPRO], [bZ], False)
            stt(zv[:, 2:8, :], VV[:, ct, 0:6, 1:129], w0, zv[:, 2:8, :], ALU.mult, ALU.add, [bV, bPRO], [bZ], False)
            stt(zv[:, 0:2, :], VV[:, ct, 6:8, 0:128], w0, zv[:, 0:2, :], ALU.mult, ALU.add, [bV, bPRO], [bZ], False)
            copy_op("dve", VH[:, ct, :], VV[:, ct, :, 128], [bV], [bVH], mark=True)
        for ct in range(4):
            for th in range(2):
                b = nextA()
                xr = [bXT[4 * th + i] for i in range(4)]
                for k in range(8):
                    mm(PS[b][:], w8(sl_cb)[:, k, ct * 128:(ct + 1) * 128], XT[:, k, th * 512:(th + 1) * 512],
                       k == 0, k == 7, [bWS[sl_cb]] + xr, [bPS[b]], k == 7)
                tt("dve", CBZ[:, ct, th * 512:(th + 1) * 512], PS[b][:], ZZ[:, ct, th * 512:(th + 1) * 512],
                   ALU.mult, [bPS[b], bZ], [bCBZ])
        slots_ft = {}

        def issue_bundle(ft):
            def bundle(sl, ft=ft):
                w = WS[sl]
                return [(w[:, 0:1024].rearrange("p (k c) -> p k c", k=8), mwi_v[:, :, 2048 + 128 * ft:2176 + 128 * ft]),
                        (w[:, 1024:2048].rearrange("p (k c) -> p k c", k=8), mwi_v[:, :, 3072 + 128 * ft:3200 + 128 * ft]),
                        (w[:, 2048:2560].rearrange("p (k c) -> p k c", k=4), glu_v[:, :, 128 * ft:128 * ft + 128]),
                        (w[:, 2560:3072].rearrange("p (k c) -> p k c", k=4), glu_v[:, :, 1024 + 128 * ft:1152 + 128 * ft]),
                        (w[:, 3072:3584].rearrange("p (k c) -> p k c", k=4), cwo_v[:, :, 128 * ft:128 * ft + 128])]
            slots_ft[ft] = load_slot(bundle)

        for ft in range(3):
            issue_bundle(ft)
        kb.handoff([bZ], [bUT, bSBF])
        for gb_ in range(8):
            b = nextB()
            for gl in range(4):
                g = 4 * gb_ + gl
                kb.op("pe", lambda e: e.transpose(PS[b][:, gl * 128:(gl + 1) * 128],
                                                  SU4[:, g, :, :].rearrange("p s j -> p (s j)"), IDF),
                      reads=[bSU, bMISC], writes=[bPS[b]], mark=(gl == 3))
            copy_op(ev_eng(), UT[:, 4 * gb_:4 * gb_ + 4, :].rearrange("p a b -> p (a b)"), PS[b][:],
                    [bPS[b]], [bUT])
        kb.handoff([bSU], [bE])
        kb.handoff([bV], [bS])
        for bk in range(8):
            b = nextA()
            for gpl in range(2):
                gp = 2 * bk + gpl
                for par in range(2):
                    g = 2 * gp + par
                    for ri in range(2):
                        c0 = gpl * 256 + ri * 128
                        mm(PS[b][par * 64:(par + 1) * 64, c0:c0 + 128], PT[:, g, ri, :], UT[:, g, :],
                           True, True, [bS5C, bUT], [bPS[b]], (gpl == 1 and par == 1 and ri == 1))
            copy_op("act", EE_[:, 2 * bk:2 * bk + 2, :, :].rearrange("p a b c -> p (a b c)"), PS[b][:],
                    [bPS[b]], [bE])
        def bv(ap4, off):
            return ap4[:, :, :, off:off + 121:8]

        SCT = view(R2o + 16 * KBY, F32, [2, 16, 2, 16])
        T1_, T2_ = SCT[:, 0, :, :, :], SCT[:, 1, :, :, :]
        CB = view(R2o + 8 * KBY, F32, [16, 2, 17])
        VA = view(R2o + 8 * KBY + 2304, F32, [16, 2, 16])
        VB = view(R2o + 8 * KBY + 2304 + 2048, F32, [16, 2, 16])
        A1b = A1.unsqueeze(3).broadcast_to([128, 16, 2, 16])
        A2b0 = A2[:, :, 0:1].broadcast_to([128, 16, 16])
        A2b1 = A2[:, :, 1:2].broadcast_to([128, 16, 16])
        WS_ = [bS, bSTM]
        copy_op("dve", SS[:, :, :, 0], SC, [bSC, bSM], [bS])
        copy_op("dve", bv(SS, 1), bv(EE_, 0), [bE], [bS])
        for k in range(1, 8):
            prev, cur, ek = bv(SS, k), bv(SS, k + 1), bv(EE_, k)
            tt("dve", T1_, A1b, prev, ALU.mult, [bE], WS_)
            tt("dve", T2_[:, :, 0, :], A2b0, prev[:, :, 1, :], ALU.mult, [], WS_)
            tt("dve", T2_[:, :, 1, :], A2b1, prev[:, :, 0, :], ALU.mult, [], WS_)
            tt("dve", T1_, T1_, T2_, ALU.add, [], WS_)
            tt("dve", cur, T1_, ek, ALU.add, [bE], WS_)
        WC_ = [bS, bSTM, bSBF]
        lend = SS[:, :, :, 8:129:8]
        copy_op("dve", VA, lend, [bS], WC_)
        copy_op("dve", CB[:, :, :, 0], SC, [bSC], WC_)
        t1s, t2s = T1_[:, :, :, 0], T2_[:, :, :, 0]
        tt("dve", t1s, KA1[0], SC, ALU.mult, [bSM, bSC], WC_)
        tt("dve", t2s[:, :, 0], KA2[0][:, :, 0], SC[:, :, 1], ALU.mult, [], WC_)
        tt("dve", t2s[:, :, 1], KA2[0][:, :, 1], SC[:, :, 0], ALU.mult, [], WC_)
        tt("dve", t1s, t1s, t2s, ALU.add, [], WC_)
        tt("dve", VA[:, :, :, 0], VA[:, :, :, 0], t1s, ALU.add, [], WC_)
        src, dst = VA, VB
        for r in range(4):
            d = 1 << r
            n = 16 - d
            if r == 3:
                dst = CB[:, :, :, 1:17]
            ka1 = KA1[r].unsqueeze(3).broadcast_to([128, 16, 2, n])
            ka20 = KA2[r][:, :, 0:1].broadcast_to([128, 16, n])
            ka21 = KA2[r][:, :, 1:2].broadcast_to([128, 16, n])
            tt("dve", T1_[:, :, :, 0:n], ka1, src[:, :, :, 0:n], ALU.mult, [bSM], WC_)
            tt("dve", T2_[:, :, 0, 0:n], ka20, src[:, :, 1, 0:n], ALU.mult, [], WC_)
            tt("dve", T2_[:, :, 1, 0:n], ka21, src[:, :, 0, 0:n], ALU.mult, [], WC_)
            tt("dve", T1_[:, :, :, 0:n], T1_[:, :, :, 0:n], T2_[:, :, :, 0:n], ALU.add, [], WC_)
            tt("dve", dst[:, :, :, d:16], src[:, :, :, d:16], T1_[:, :, :, 0:n], ALU.add, [], WC_)
            copy_op("dve", dst[:, :, :, 0:d], src[:, :, :, 0:d], [], WC_)
            src, dst = dst, (VA if dst is VB else VB)
        bFX = [Buf(), Buf()]
        bS2 = Buf()
        kb.handoff([bE], bFX)
        kb.handoff([bS], [bS2])
        for hh, en_ in ((0, "dve"), (1, "pool")):
            gsl = slice(8 * hh, 8 * hh + 8)
            TF1 = view(R1o + 8 * KBY * hh, F32, [8, 16, 8])
            TF2 = view(R1o + 8 * KBY * hh + 4 * KBY, F32, [8, 16, 8])
            pwr = PW16[:, 0, gsl, 0:8].unsqueeze(2).broadcast_to([128, 8, 16, 8])
            pwi = PW16[:, 1, gsl, 0:8].unsqueeze(2).broadcast_to([128, 8, 16, 8])
            cbr = CB[:, gsl, 0, 0:16].unsqueeze(3).broadcast_to([128, 8, 16, 8])
            cbi = CB[:, gsl, 1, 0:16].unsqueeze(3).broadcast_to([128, 8, 16, 8])
            sre = SS[:, gsl, 0, 1:129].rearrange("p g (b k) -> p g b k", b=16)
            sim = SS[:, gsl, 1, 1:129].rearrange("p g (b k) -> p g b k", b=16)
            bSh = bS if hh == 0 else bS2
            Rr, Ww = [bFX[hh], bSh, bSBF, bSM], [bFX[hh], bSh]
            tt(en_, TF1, pwr, cbr, ALU.mult, Rr, Ww)
            tt(en_, TF2, pwi, cbi, ALU.mult, Rr, Ww)
            tt(en_, TF1, TF1, TF2, ALU.subtract, Rr, Ww)
            tt(en_, sre, sre, TF1, ALU.add, Rr, Ww)
            tt(en_, TF1, pwr, cbi, ALU.mult, Rr, Ww)
            tt(en_, TF2, pwi, cbr, ALU.mult, Rr, Ww)
            tt(en_, TF1, TF1, TF2, ALU.add, Rr, Ww)
            tt(en_, sim, sim, TF1, ALU.add, Rr, Ww)
        copy_op("dve", SC, CB[:, :, :, 16], [bSBF], [bSC])
        copy_op("act", SBF, SS[:, :, :, 0:128], [bS, bS2], [bSBF])
        bSSTK = Buf()
        kb.handoff([bE] + bFX, [bSSTK])
        for par in range(2):
            for ri in range(2):
                kb.dma("sp", [(SSTK[ri * 64:(ri + 1) * 64, par:32:2, :], SBF[par * 64:(par + 1) * 64, :, ri, :])],
                       reads=[bSBF], writes=[bSSTK])
        for gb_ in range(8):
            b = nextB()
            for gl in range(4):
                g = 4 * gb_ + gl
                gp, par = g // 2, g % 2
                sl = slice(par * 64, (par + 1) * 64)
                o = PS[b][:, gl * 128:(gl + 1) * 128]
                mm(o, UT[:, g, :], TOEP[:, g, :], True, False, [bUT, bS5C], [bPS[b]], False)
                mm(o, SSTK[:, g, :], QT[:, g, :], False, True, [bSSTK, bS5C], [bPS[b]], gl == 3)
            act(STM[:, :, 64 * gb_:64 * gb_ + 64].rearrange("p t (g i) -> p g t i", g=4),
                PS[b][:].rearrange("p (g t i) -> p g t i", g=4, t=8), AF.Gelu_apprx_tanh,
                [bPS[b]], [bSTM])
        kb.handoff([bUT], [bSTT])
        for kt in range(4):
            b = nextA()
            psb = PS[b][:].bitcast(BF16)
            for t_ in range(8):
                kb.op("pe", lambda e: e.transpose(psb[:, t_ * 128:(t_ + 1) * 128],
                                                  STM[:, t_, kt * 128:(kt + 1) * 128], IDB),
                      reads=[bSTM, bMISC], writes=[bPS[b]], mark=(t_ == 7))
            copy_op(ev_eng(), ST_[:, kt, :], psb, [bPS[b]], [bSTT])
        kb.handoff([bSSTK], [bMG])
        kb.handoff([bS, bS2], bMT)
        for ft in range(8):
            sl = slots_ft[ft]
            w = WS[sl]
            wgc = w[:, 0:1024].rearrange("p (k c) -> p k c", k=8)
            wgs = w[:, 1024:2048].rearrange("p (k c) -> p k c", k=8)
            wga = w[:, 2048:2560].rearrange("p (k c) -> p k c", k=4)
            wgb = w[:, 2560:3072].rearrange("p (k c) -> p k c", k=4)
            wco = w[:, 3072:3584].rearrange("p (k c) -> p k c", k=4)
            for th in range(2):
                hs = slice(th * 512, (th + 1) * 512)
                xr = [bXT[4 * th + i] for i in range(4)]
                b1, b2, b3, b4, b5 = nextAll(), nextAll(), nextAll(), nextAll(), nextAll()
                for k in range(8):
                    mm(PS[b1][:], wgc[:, k, :], XT[:, k, hs], k == 0, k == 7, [bWS[sl]] + xr, [bPS[b1]], k == 7)
                for k in range(8):
                    mm(PS[b2][:], wgs[:, k, :], XT[:, k, hs], k == 0, k == 7, [bWS[sl]] + xr, [bPS[b2]], k == 7)
                for k in range(4):
                    mm(PS[b3][:], wgb[:, k, :], ST_[:, k, hs], k == 0, k == 3, [bWS[sl], bSTT], [bPS[b3]], k == 3)
                for k in range(4):
                    mm(PS[b4][:], wga[:, k, :], ST_[:, k, hs], k == 0, k == 3, [bWS[sl], bSTT], [bPS[b4]], k == 3)
                for k in range(4):
                    mm(PS[b5][:], wco[:, k, :], CBZ[:, k, hs], k == 0, k == 3, [bWS[sl], bCBZ], [bPS[b5]], k == 3)
                i0 = 3 * ((2 * ft + th) % 2)
                t1, t2, t3 = MT[i0], MT[i0 + 1], MT[i0 + 2]
                q1, q2, q3 = bMT[i0], bMT[i0 + 1], bMT[i0 + 2]
                act(t1, PS[b1][:], AF.Sigmoid, [bPS[b1]], [q1])
                act(t2, PS[b2][:], AF.Sigmoid, [bPS[b2]], [q2])
                act(t3, PS[b3][:], AF.Sigmoid, [bPS[b3]], [q3])
                tt("dve", t1, PS[b5][:], t1, ALU.mult, [bPS[b5]], [q1], mark=False)
                tt("dve", t3, PS[b4][:], t3, ALU.mult, [bPS[b4]], [q3], mark=False)
                tt("dve", t2, t3, t2, ALU.mult, [q3], [q2], mark=False)
                tt("dve", MG[:, ft, hs], t1, t2, ALU.add, [q1, q2], [bMG], mark=True)
            if ft + 3 < 8:
                issue_bundle(ft + 3)
        for s in range(8):
            b0 = nextBpair()
            banks = [b0, b0 + 1]
            for hf in range(2):
                for k in range(8):
                    mm(PS[banks[hf]][:], MG[:, k, s * 128:(s + 1) * 128], WMO[:, k, hf * 512:(hf + 1) * 512],
                       k == 0, k == 7, [bMG, bWMO], [bPS[banks[hf]]], k == 7)
            ln(s, PSALL[:, 512 * b0:512 * b0 + 1024], [bPS[b0], bPS[b0 + 1]], ALPHA, EPS1,
               TMPA[0].bitcast(BF16), bTMPA[0])
            if s > 2:
                transposes_tile(s - 3)
        transposes_tile(5)
        transposes_tile(6)
        transposes_tile(7)
        return ([bSU, bE, bSSTK, bMG, bV, bS, bS2, bCBZ] + bMT + bFX,
                [bUT, bSBF, bZ, bSTM, bSTT, bWMO])

    PSB = view(R1o, F32, [8, 256])
    PTT = view(R1o + 8 * KBY, BF16, [2, 1024])
    EEt = [view(R1o + 12 * KBY + 2 * KBY * i, F32, [512]) for i in range(4)]

    def ple(m):
        load_gb(3)
        bP, bPTT = Buf(), Buf()
        bEE = [Buf() for _ in range(4)]
        bJ = Buf()
        kb.handoff(r1_users, [bP, bPTT, bJ] + bEE)
        kb.dma("sp", [(PSB, p_v[m])], writes=[bP])
        pwg_v = pwg_d.rearrange("(kt p) c -> p kt c", p=128)
        pwi_v = pwi_d.rearrange("(kt p) c -> p kt c", p=128)
        sg = []
        for hf in range(2):
            sl = next_slot()
            kb.dma("pool", [(WS[sl].rearrange("p (k c) -> p k c", k=8), pwg_v[:, :, hf * 512:(hf + 1) * 512])],
                   writes=[bWS[sl]])
            sg.append(sl)
        sli = next_slot()
        wpi = WS[sli][:, 0:2048].rearrange("p (k c) -> p k c", k=2)
        kb.dma("pool", [(wpi, pwi_v)], writes=[bWS[sli]])
        for sp_ in range(4):
            b = nextA()
            for i in range(4):
                s, kt = 2 * sp_ + i // 2, i % 2
                kb.op("pe", lambda e: e.transpose(PS[b][:, i * 128:(i + 1) * 128],
                                                  PSB[:, s, kt * 128:(kt + 1) * 128], IDF),
                      reads=[bP, bMISC], writes=[bPS[b]], mark=(i == 3))
            copy_op(ev_eng(), PTT[:, :, 256 * sp_:256 * sp_ + 256].rearrange("p k (s c) -> p s k c", s=2),
                    PS[b][:].rearrange("p (s k c) -> p s k c", s=2, k=2), [bPS[b]], [bPTT])
        for s in range(8):
            b0 = nextBpair()
            for hf in range(2):
                be, bg = b0 + hf, nextA()
                w = WS[sg[hf]].rearrange("p (k c) -> p k c", k=8)
                for kt in range(2):
                    mm(PS[be][:], PTT[:, kt, s * 128:(s + 1) * 128], wpi[:, kt, hf * 512:(hf + 1) * 512],
                       kt == 0, kt == 1, [bPTT, bWS[sli]], [bPS[be]], kt == 1)
                for k in range(8):
                    mm(PS[bg][:], XT[:, k, s * 128:(s + 1) * 128], w[:, k, :], k == 0, k == 7,
                       [bXT[s], bWS[sg[hf]]], [bPS[bg]], k == 7)
                ti = next_tmp()
                act(TMPA[ti], PS[bg][:], AF.Sigmoid, [bPS[bg]], [bTMPA[ti]])
                tt("dve", PS[be][:], PS[be][:], TMPA[ti], ALU.mult, [bTMPA[ti]], [bPS[be]])
            ln(s, PSALL[:, 512 * b0:512 * b0 + 1024], [bPS[b0], bPS[b0 + 1]], ALPHA, EPS1,
               view(R1o + 20 * KBY, BF16, [1024]), bJ)
            kb.dma("sp", [(o_v[m, :, s, :], X[:, s, :])], reads=[bX[s]])
            if m == 0:
                kb.dma("sp", [(X[:, s, :], x_v[1, :, s, :])], writes=[bX[s]])
        return [bP, bPTT, bJ] + bEE

    load_x(0)
    kb.defer = True
    s5_prologue_math()
    kb.defer = False
    n_def = len(kb.deferred_q)
    HOOK_J = 7
    DEFER_RATE = (n_def + 4 * (HOOK_J + 1) - 1) // (4 * (HOOK_J + 1)) + 1
    if STAGE == -2:
        kb.run_deferred(100000)
    if STAGE == -2:
        s5_prologue_pe()
        S5ALL = view(OFF["S5C"], BF16, [12288])
        Xf = view(OFF["X"], F32, [8192])
        copy_op("act", Xf, S5ALL[:, 0:8192], [bS5C] + bX, bX)
        store_x(0)
        copy_op("act", Xf[:, 0:4096], S5ALL[:, 8192:12288], [bS5C] + bX, bX)
        copy_op("act", Xf[:, 4096:4096 + 64], view(OFF["MISC"], F32, [3840])[:, 0:64], [bSM] + bX, bX)
        store_x(1)
        kb.finish()
        es.close()
        return nc
    for m in range(2):
        if m > 0 and STAGE < 4:
            load_x(m)
        for s in range(8):
            transposes_tile(s)
        if STAGE <= 0:
            store_x(m)
            continue
        u = ffn(w1i_d, w1o_d, 0, hook=(s5_prologue_pe if (m == 0 and not os.environ.get('KDBG_NOHOOK')) else None))
        r1_users, r2_users = u[:2 * NF], u[2 * NF:]
        if STAGE == 1:
            flush_pending()
            store_x(m)
            continue
        a, b_ = mixer(m)
        r1_users, r2_users = a, b_
        if STAGE == 2:
            flush_pending()
            store_x(m)
            continue
        u = ffn(w2i_d, w2o_d, 2)
        r1_users, r2_users = u[:2 * NF], u[2 * NF:]
        if STAGE == 3:
            flush_pending()
            store_x(m)
            continue
        r1_users = ple(m)
    kb.finish()
    es.close()
    return nc


_NC_CACHE = {}


def kernel(**inputs):
    if "nc" not in _NC_CACHE:
        _NC_CACHE["nc"] = build_program()
    nc = _NC_CACHE["nc"]
    names = ["ffn1_w_in", "ffn1_w_out", "ffn2_w_in", "ffn2_w_out", "ln1_g", "ln1_b", "ln2_g", "ln2_b",
             "ln3_g", "ln3_b", "ln4_g", "ln4_b", "mix_w_in", "conv_w", "conv_b", "conv_w_out",
             "ssm_lam_re", "ssm_lam_im", "ssm_log_step", "ssm_b_re", "ssm_b_im", "ssm_c_re", "ssm_c_im",
             "ssm_d", "ssm_w_glu", "mix_w_out", "ple_w_in", "ple_w_gate"]
    shared = {}
    for n in names:
        a = np.ascontiguousarray(np.asarray(inputs[n], dtype=np.float32))
        if n.startswith("ln") or n in ("conv_b", "ssm_log_step"):
            shared[n] = a.reshape(1, -1)
        else:
            shared[n] = a[0]
    x = np.asarray(inputs["x"], dtype=np.float32)
    p = np.asarray(inputs["p"], dtype=np.float32)
    in_maps = []
    for c in range(NCORES):
        d = dict(shared)
        d["x"] = np.ascontiguousarray(x[c])
        d["p"] = np.ascontiguousarray(p[0, c])
        in_maps.append(d)
    res = run_bass_kernel_spmd(nc, in_maps, core_ids=list(range(NCORES)))
    return np.stack([np.asarray(r["out"], dtype=np.float32) for r in res.results], axis=0)
```

```python
import math
import os
from contextlib import ExitStack

import numpy as np
import concourse.bass as bass
import concourse.mybir as mybir
from concourse.bass_utils import run_bass_kernel_spmd

F32 = mybir.dt.float32
BF16 = mybir.dt.bfloat16
U8 = mybir.dt.uint8
AF = mybir.ActivationFunctionType
ALU = mybir.AluOpType

D = 1024
SEQ = 2048
DFF = 2816
NF = DFF // 128
PLE = 256
ALPHA = 2.0 ** 0.25
EPS = 1e-5
NCORES = int(os.environ.get('KDBG_CORES', '8'))
STAGE = 99


class Eng:
    def __init__(self, name, h, sem):
        self.name, self.h, self.sem = name, h, sem
        self.count = 0
        self.waited = {}


class DSem:
    def __init__(self, h):
        self.h = h
        self.count = 0


class Buf:
    __slots__ = ("w", "r", "ds", "name")

    def __init__(self, name=""):
        self.w = None
        self.r = []
        self.ds = None
        self.name = name


class KB:
    def __init__(self, nc, es):
        self.nc = nc
        self.es = es
        self.eng = {}
        for name, h in (("pe", nc.tensor), ("act", nc.scalar), ("dve", nc.vector),
                        ("pool", nc.gpsimd), ("sp", nc.sync)):
            sem = es.enter_context(nc.semaphore("s_" + name))
            self.eng[name] = Eng(name, h, sem)
        self.dsems = []
        self.nds = 0
        self.defer = False
        self.deferred_qs = {}
        self.chain = False

    def new_dsem(self):
        h = self.es.enter_context(self.nc.semaphore("d%d" % self.nds))
        self.nds += 1
        d = DSem(h)
        self.dsems.append(d)
        return d

    def _wait(self, E, tok):
        key, sem, val, owner = tok
        if owner == E.name and (owner == "pe" or self.chain):
            return
        if owner is not None:
            assert self.eng[owner].count >= val, "pending token"
        if E.waited.get(key, 0) >= val:
            return
        E.h.wait_ge(sem, val)
        E.waited[key] = val

    def _deps(self, E, reads, writes):
        for b in reads:
            if b.w is not None:
                self._wait(E, b.w)
        for b in writes:
            if b.w is not None:
                self._wait(E, b.w)
            for t in b.r:
                self._wait(E, t)

    def _commit(self, tok, reads, writes):
        for b in writes:
            b.w = tok
            b.r = []
        for b in reads:
            if b not in writes:
                b.r = [t for t in b.r if t[0] != tok[0]] + [tok]

    def run_deferred(self, n, label="main"):
        q = self.deferred_qs.setdefault(label, [])
        for _ in range(min(n, len(q))):
            a = q.pop(0)
            if a[0] == "__dma__":
                self.dma(*a[1:], _replay=True)
            else:
                self.op(*a, _replay=True)

    def op(self, en, fn, reads=(), writes=(), mark=True, _replay=False):
        if self.defer and not _replay:
            self.deferred_qs.setdefault(self.defer, []).append((en, fn, list(reads), list(writes), mark))
            return None
        E = self.eng[en]
        self._deps(E, reads, writes)
        ins = fn(E.h)
        if en != "pe":
            mark = True
        if mark:
            ins.then_inc(E.sem, 1)
            E.count += 1
            tok = ("e_" + en, E.sem, E.count, en)
        else:
            tok = ("e_" + en, E.sem, E.count + 1, en)
        self._commit(tok, reads, writes)
        return ins

    def dma(self, qn, pairs, reads=(), writes=(), ds=None, slow=False, eager=False, _replay=False):
        if self.defer and not eager and not _replay:
            self.deferred_qs.setdefault(self.defer, []).append(("__dma__", qn, list(pairs), list(reads), list(writes), ds, slow))
            return
        Q = self.eng[qn]
        self._deps(Q, reads, writes)
        if ds is None:
            tgt = writes[0] if writes else reads[0]
            if tgt.ds is None:
                tgt.ds = self.new_dsem()
            ds = tgt.ds
        for (o, i) in pairs:
            if slow:
                Q.h.dma_start(out=o, in_=i, allow_slow_non_contiguous=True).then_inc(ds.h, 16)
            else:
                Q.h.dma_start(out=o, in_=i).then_inc(ds.h, 16)
            ds.count += 16
        tok = ("d%d" % id(ds), ds.h, ds.count, None)
        self._commit(tok, reads, writes)

    def handoff(self, old, new):
        toks = {}
        for b in old:
            for t in ([b.w] if b.w is not None else []) + list(b.r):
                if t[0] not in toks or toks[t[0]][2] < t[2]:
                    toks[t[0]] = t
        for b in new:
            b.w = None
            b.r = list(toks.values())

    def finish(self):
        for E in self.eng.values():
            for O in self.eng.values():
                if O is not E and O.count > 0:
                    self._wait(E, ("e_" + O.name, O.sem, O.count, O.name))
        sp = self.eng["sp"]
        for d in self.dsems:
            if d.count > 0:
                sp.h.wait_ge(d.h, d.count)


def build_program():
    nc = bass.Bass("TRN2", target_bir_lowering=False)
    es = ExitStack()

    def din(name, shape):
        return nc.dram_tensor(name, list(shape), F32, kind="ExternalInput").ap()

    x_d = din("x", (SEQ, D))
    p_d = din("p", (SEQ, PLE))
    w1i_d = din("ffn1_w_in", (D, 2 * DFF))
    w1o_d = din("ffn1_w_out", (DFF, D))
    w2i_d = din("ffn2_w_in", (D, 2 * DFF))
    w2o_d = din("ffn2_w_out", (DFF, D))
    lng_d = [din("ln%d_g" % i, (1, D)) for i in range(1, 5)]
    lnb_d = [din("ln%d_b" % i, (1, D)) for i in range(1, 5)]
    mwi_d = din("mix_w_in", (D, 4096))
    cw_d = din("conv_w", (3, 512))
    cb_d = din("conv_b", (1, 512))
    cwo_d = din("conv_w_out", (512, D))
    lre_d = din("ssm_lam_re", (32, 64))
    lim_d = din("ssm_lam_im", (32, 64))
    lst_d = din("ssm_log_step", (1, 32))
    bre_d = din("ssm_b_re", (32, 64, 16))
    bim_d = din("ssm_b_im", (32, 64, 16))
    cre_d = din("ssm_c_re", (32, 16, 64))
    cim_d = din("ssm_c_im", (32, 16, 64))
    dsk_d = din("ssm_d", (32, 16))
    glu_d = din("ssm_w_glu", (512, 2 * D))
    mwo_d = din("mix_w_out", (D, D))
    pwi_d = din("ple_w_in", (PLE, D))
    pwg_d = din("ple_w_gate", (D, D))
    out_d = nc.dram_tensor("out", [SEQ, D], F32, kind="ExternalOutput").ap()

    KBY = 1024
    OFF = {}
    cur = 0

    def region(name, nbytes):
        nonlocal cur
        OFF[name] = cur
        cur += nbytes

    region("X", 32 * KBY)
    region("XT", 16 * KBY)
    region("GB", 8 * KBY)
    region("R1", 44 * KBY)
    region("R2", 44 * KBY)
    region("R3", 24 * KBY)
    region("S5C", 24 * KBY)
    MISC_BYTES = 16000
    region("MISC", MISC_BYTES)
    ARENA_BYTES = cur
    arena = es.enter_context(nc.sbuf_tensor("arena", [128, ARENA_BYTES], U8))

    def view(off, dtype, shape):
        n = 1
        for s in shape:
            n *= s
        nb = n * (4 if dtype == F32 else 2)
        a = arena[:, off:off + nb].bitcast(dtype)
        if len(shape) == 1:
            return a
        names = " ".join("d%d" % i for i in range(len(shape)))
        kw = {"d%d" % i: shape[i] for i in range(len(shape) - 1)}
        return a.rearrange("p (%s) -> p %s" % (names, names), **kw)

    X = view(OFF["X"], F32, [8, 1024])
    XT = view(OFF["XT"], BF16, [8, 1024])
    GB = view(OFF["GB"], F32, [2, 1024])
    H = view(OFF["R1"], BF16, [NF, 1024])
    WOUT = view(OFF["R2"], BF16, [NF, 1024])
    WS = [view(OFF["R3"] + 8 * KBY * i, BF16, [4096]) for i in range(3)]
    TOEP = view(OFF["S5C"], BF16, [32, 128])
    PT = view(OFF["S5C"] + 8 * KBY, BF16, [32, 2, 64])
    QT = view(OFF["S5C"] + 16 * KBY, BF16, [32, 128])
    mo = OFF["MISC"]

    def misc(dtype, shape):
        nonlocal mo
        n = 1
        for s in shape:
            n *= s
        v = view(mo, dtype, shape)
        mo += n * (4 if dtype == F32 else 2)
        mo = (mo + 63) // 64 * 64
        return v

    IDF = misc(F32, [128])
    IDB = misc(BF16, [128])
    MASK4 = misc(F32, [512])
    CW = misc(F32, [4, 4])
    VH = misc(F32, [4, 8])
    SC = misc(F32, [16, 2])
    A1 = misc(F32, [16, 2])
    A2 = misc(F32, [16, 2])
    DSK = misc(F32, [32])
    LNS0 = misc(F32, [4])
    LNS1 = misc(F32, [4])
    LNA0 = misc(F32, [4])
    LNA1 = misc(F32, [4])
    LNS = [LNS0, LNS1]
    LNA = [LNA0, LNA1]
    HALFPI = misc(F32, [1])
    SM = misc(F32, [22, 16])
    PW16 = misc(F32, [2, 16, 16])
    A1C = misc(F32, [16, 2])
    A2C = misc(F32, [16, 2])
    KA1 = [misc(F32, [16, 2]) for _ in range(4)]
    KA2 = [misc(F32, [16, 2]) for _ in range(4)]
    PWF = misc(F32, [2, 16, 8])
    PWR = misc(F32, [2, 16, 8])

    PSALL = es.enter_context(nc.psum_tensor("psall", [128, 4096], F32))
    PS = [PSALL[:, 512 * i:512 * (i + 1)] for i in range(8)]

    kb = KB(nc, es)
    TMPA = [misc(F32, [512]) for _ in range(2)]
    EPS4 = misc(F32, [1])
    EPS1 = misc(F32, [1])
    assert mo <= OFF["MISC"] + MISC_BYTES, (mo - OFF["MISC"])
    if os.environ.get("KDBG_PRINT"):
        print("MISC used", mo - OFF["MISC"], "of", MISC_BYTES)

    bX = [Buf() for _ in range(8)]
    bXT = [Buf() for _ in range(8)]
    bGB = Buf()
    bH = [[Buf() for _ in range(2)] for _ in range(NF)]
    WCH = [(0, 6), (6, 12), (12, 17), (17, 22)]
    bWOUT = [Buf() for _ in range(4)]
    bWS = [Buf() for _ in range(3)]
    bPS = [Buf() for _ in range(8)]
    bS5C = Buf()
    bPRO = Buf()
    bMISC = Buf()
    bTMPA = [Buf(), Buf()]
    bST = [Buf(), Buf()]
    bSTA = [Buf(), Buf()]
    bVH = Buf()
    bSC = Buf()

    def chunk_of(k):
        for i, (a, b) in enumerate(WCH):
            if a <= k < b:
                return i

    rr = {"A": 0, "B": 0, "slot": 0, "tmp": 0, "ev": 0}

    def nextA():
        rr["A"] = (rr["A"] + 1) % 4
        return rr["A"]

    def nextB():
        rr["B"] = (rr["B"] + 1) % 4
        return 4 + rr["B"]

    def nextAll():
        rr["all"] = (rr.get("all", 0) + 1) % 8
        return rr["all"]

    def nextBpair():
        rr["Bp"] = (rr.get("Bp", 0) + 1) % 2
        return 4 + 2 * rr["Bp"]

    def next_slot():
        rr["slot"] = (rr["slot"] + 1) % 3
        return rr["slot"]

    def next_tmp():
        rr["tmp"] = (rr["tmp"] + 1) % 2
        return rr["tmp"]

    def ev_eng():
        rr["ev"] = (rr["ev"] + 1) % 2
        return "act" if rr["ev"] else "dve"

    def mm(out, lhsT, rhs, start, stop, reads, writes, mark):
        kb.op("pe", lambda e: e.matmul(out, lhsT=lhsT, rhs=rhs, start=start, stop=stop),
              reads, writes, mark)

    def copy_op(en, out, in_, reads, writes, mark=True):
        if en == "act":
            kb.op("act", lambda e: e.activation(out=out, in_=in_, func=AF.Copy), reads, writes, mark)
        else:
            kb.op(en, lambda e: e.tensor_copy(out, in_), reads, writes, mark)

    def tt(en, out, a, b, op, reads, writes, mark=True):
        kb.op(en, lambda e: e.tensor_tensor(out, a, b, op), reads, writes, mark)

    def ts(en, out, a, s1, s2, op0, op1, reads, writes, mark=True):
        if s2 is None:
            kb.op(en, lambda e: e.tensor_scalar(out, a, s1, None, op0), reads, writes, mark)
        else:
            kb.op(en, lambda e: e.tensor_scalar(out, a, s1, s2, op0, op1), reads, writes, mark)

    def stt(out, in0, scalar, in1, op0, op1, reads, writes, mark=True):
        kb.op("dve", lambda e: e.scalar_tensor_tensor(out=out, in0=in0, scalar=scalar, in1=in1,
                                                      op0=op0, op1=op1), reads, writes, mark)

    def act(out, in_, func, reads, writes, bias=None, scale=None, mark=True):
        kw = {}
        if bias is not None:
            kw["bias"] = bias
        if scale is not None:
            kw["scale"] = scale
        kb.op("act", lambda e: e.activation(out=out, in_=in_, func=func, **kw), reads, writes, mark)

    kb.op("dve", lambda e: e.memset(IDF, 1.0), writes=[bMISC], mark=False)
    kb.op("dve", lambda e: e.memset(MASK4, 1.0), writes=[bMISC], mark=False)
    kb.op("dve", lambda e: e.memset(HALFPI, math.pi / 2), writes=[bMISC], mark=False)
    kb.op("dve", lambda e: e.memset(EPS4, 4 * EPS), writes=[bMISC], mark=False)
    kb.op("dve", lambda e: e.memset(EPS1, EPS), writes=[bMISC], mark=False)
    kb.op("dve", lambda e: e.memset(VH, 0.0), writes=[bVH], mark=False)
    kb.op("dve", lambda e: e.memset(SC, 0.0), writes=[bSC], mark=True)
    kb.op("pool", lambda e: e.affine_select(out=IDF, in_=IDF, pattern=[[-1, 128]],
                                            compare_op=ALU.is_equal, fill=0.0, base=0,
                                            channel_multiplier=1), writes=[bMISC])
    kb.op("pool", lambda e: e.affine_select(out=MASK4.rearrange("p (a b c) -> p a b c", a=4, b=8),
                                            in_=MASK4.rearrange("p (a b c) -> p a b c", a=4, b=8),
                                            pattern=[[0, 4], [16, 8], [0, 16]],
                                            compare_op=ALU.is_ge, fill=0.0, base=15,
                                            channel_multiplier=-1), writes=[bMISC])
    copy_op("dve", IDB, IDF, [bMISC], [bMISC])

    x_v = x_d.rearrange("(m c s) d -> m c s d", m=2, s=8)
    o_v = out_d.rearrange("(m c s) d -> m c s d", m=2, s=8)
    p_v = p_d.rearrange("(m c s) d -> m c s d", m=2, s=8)

    def load_x(m):
        for s in range(8):
            kb.dma("sp", [(X[:, s, :], x_v[m, :, s, :])], writes=[bX[s]])

    def store_x(m):
        for s in range(8):
            kb.dma("sp", [(o_v[m, :, s, :], X[:, s, :])], reads=[bX[s]])

    def transposes_tile(s):
        for kbk in range(2):
            b = nextA()
            for kk in range(4):
                k = 4 * kbk + kk
                kb.op("pe", lambda e: e.transpose(PS[b][:, kk * 128:(kk + 1) * 128],
                                                  X[:, s, k * 128:(k + 1) * 128], IDF),
                      reads=[bX[s], bMISC], writes=[bPS[b]], mark=(kk == 3))
            copy_op(ev_eng(), XT[:, 4 * kbk:4 * kbk + 4, s * 128:(s + 1) * 128],
                    PS[b][:].rearrange("p (a b) -> p a b", a=4), [bPS[b]], [bXT[s]])

    pending_tr = []

    def flush_pending():
        while pending_tr:
            transposes_tile(pending_tr.pop(0))

    def load_gb(i):
        kb.dma("sp", [(GB[:, 0, :], lng_d[i][0].partition_broadcast(128)),
                      (GB[:, 1, :], lnb_d[i][0].partition_broadcast(128))], writes=[bGB])

    def ln(s, src, src_bufs, scal, epsap, junk, junk_buf):
        sl = s % 2
        xs = X[:, s, :]
        kb.op("dve", lambda e: e.scalar_tensor_tensor(out=xs, in0=xs, scalar=scal, in1=src, op0=ALU.mult,
                                                      op1=ALU.add, accum_out=LNS[sl][:, 0:1]),
              reads=src_bufs, writes=[bX[s], bST[sl]])
        kb.op("act", lambda e: e.activation(out=junk, in_=xs, func=AF.Square, accum_out=LNA[sl][:, 0:1]),
              reads=[bX[s]], writes=[junk_buf, bSTA[sl]])
        ts("dve", LNS[sl][:, 1:2], LNS[sl][:, 0:1], -1.0 / 1024, None, ALU.mult, None, [bST[sl]], [bST[sl]])
        tt("dve", LNS[sl][:, 2:3], LNS[sl][:, 1:2], LNS[sl][:, 1:2], ALU.mult, [bST[sl]], [bST[sl]])
        stt(LNS[sl][:, 3:4], LNA[sl][:, 0:1], 1.0 / 1024, LNS[sl][:, 2:3], ALU.mult, ALU.subtract,
            [bST[sl], bSTA[sl]], [bST[sl]])
        act(LNA[sl][:, 1:2], LNS[sl][:, 3:4], AF.Ln, [bST[sl], bMISC], [bSTA[sl]], bias=epsap)
        act(LNA[sl][:, 2:3], LNA[sl][:, 1:2], AF.Exp, [bSTA[sl]], [bSTA[sl]], scale=-0.5)
        act(LNA[sl][:, 3:4], LNS[sl][:, 1:2], AF.Identity, [bST[sl], bSTA[sl]], [bSTA[sl]],
            scale=LNA[sl][:, 2:3])
        act(xs, xs, AF.Identity, [bSTA[sl]], [bX[s]], bias=LNA[sl][:, 3:4], scale=LNA[sl][:, 2:3])
        tt("pool", xs, xs, GB[:, 0, :], ALU.mult, [bGB], [bX[s]])
        tt("pool", xs, xs, GB[:, 1, :], ALU.add, [bGB], [bX[s]])

    bSM = Buf()
    GBs = view(OFF["GB"], F32, [6, 16, 16])
    PN = view(OFF["R2"], F32, [16, 2, 128])
    QN = view(OFF["R2"] + 16 * KBY, F32, [16, 2, 128])
    TA = view(OFF["R2"] + 32 * KBY, F32, [16, 128])
    TB = view(OFF["S5C"], F32, [16, 128])
    TMASK = view(OFF["R2"] + 40 * KBY, F32, [512])

    def T(i):
        return SM[:, i, :]

    def s5_prologue_math():
        e = "dve"
        (LR, LI, LS, DT, AA, TH, MAG, ZR, ZI, T1, T2, T3, T4,
         AR, DEN, WR, WI, NR, NI, IR, II) = [T(i) for i in range(21)]
        pairs = []
        for par in range(2):
            sl = slice(par * 64, (par + 1) * 64)
            pairs += [(LR[sl, :], lre_d[par:32:2, :].rearrange("g n -> n g")),
                      (LI[sl, :], lim_d[par:32:2, :].rearrange("g n -> n g")),
                      (LS[sl, :], lst_d[0, par:32:2].partition_broadcast(64))]
        kb.dma("sp", pairs, writes=[bSM], slow=True, eager=True)
        pairs = []
        for par in range(2):
            sl = slice(par * 64, (par + 1) * 64)
            pairs += [(GBs[sl, 0, :, :], bre_d[par:32:2].rearrange("g n j -> n g j")),
                      (GBs[sl, 1, :, :], bim_d[par:32:2].rearrange("g n j -> n g j"))]
            for gp in range(16):
                pairs += [(GBs[sl, 2, gp, :], cre_d[2 * gp + par].rearrange("i n -> n i")),
                          (GBs[sl, 3, gp, :], cim_d[2 * gp + par].rearrange("i n -> n i"))]
        for s in range(8):
            pairs.append((DSK[s * 16:(s + 1) * 16, :], dsk_d.rearrange("g j -> j g")))
        for k_ in range(3):
            pairs.append((CW[:, :, k_], cw_d[k_].rearrange("(ct p) -> p ct", p=128)))
        pairs.append((CW[:, :, 3], cb_d[0].rearrange("(ct p) -> p ct", p=128)))
        kb.dma("sp", pairs, writes=[bPRO], slow=True, eager=True)
        R, W = [bSM, bMISC], [bSM]
        act(DT, LS, AF.Exp, R, W)
        tt(e, AA, LR, DT, ALU.mult, R, W, mark=False)
        tt(e, TH, LI, DT, ALU.mult, R, W, mark=True)
        act(MAG, AA, AF.Exp, R, W, scale=1.0 / 32)
        act(ZI, TH, AF.Sin, R, W, scale=1.0 / 32)
        act(ZR, TH, AF.Sin, R, W, scale=1.0 / 32, bias=HALFPI)
        tt(e, ZR, ZR, MAG, ALU.mult, R, W, mark=False)
        tt(e, ZI, ZI, MAG, ALU.mult, R, W, mark=False)
        for _ in range(5):
            tt(e, T1, ZR, ZR, ALU.mult, R, W, mark=False)
            tt(e, T2, ZI, ZI, ALU.mult, R, W, mark=False)
            tt(e, T3, ZR, ZI, ALU.mult, R, W, mark=False)
            tt(e, ZR, T1, T2, ALU.subtract, R, W, mark=False)
            ts(e, ZI, T3, 2.0, None, ALU.mult, None, R, W, mark=False)
        kb.op(e, lambda en: en.memset(PWR[:, 0, :, 7], 1.0), reads=R, writes=W, mark=False)
        kb.op(e, lambda en: en.memset(PWR[:, 1, :, 7], 0.0), reads=R, writes=W, mark=False)
        copy_op(e, PWF[:, 0, :, 0], ZR, R, W, mark=False)
        copy_op(e, PWF[:, 1, :, 0], ZI, R, W, mark=False)
        for k in range(2, 9):
            pr, pi = PWF[:, 0, :, k - 2], PWF[:, 1, :, k - 2]
            nr, ni = PWF[:, 0, :, k - 1], PWF[:, 1, :, k - 1]
            tt(e, T1, pr, ZR, ALU.mult, R, W, mark=False)
            tt(e, T2, pi, ZI, ALU.mult, R, W, mark=False)
            tt(e, nr, T1, T2, ALU.subtract, R, W, mark=False)
            tt(e, T3, pr, ZI, ALU.mult, R, W, mark=False)
            tt(e, T4, pi, ZR, ALU.mult, R, W, mark=False)
            tt(e, ni, T3, T4, ALU.add, R, W, mark=False)
        for k in range(1, 8):
            copy_op(e, PWR[:, 0, :, 7 - k], PWF[:, 0, :, k - 1], R, W, mark=False)
            copy_op(e, PWR[:, 1, :, 7 - k], PWF[:, 1, :, k - 1], R, W, mark=False)
        ts(e, AR, ZR, -1.0, None, ALU.add, None, R, W, mark=False)
        tt(e, T1, LR, LR, ALU.mult, R, W, mark=False)
        tt(e, T2, LI, LI, ALU.mult, R, W, mark=False)
        tt(e, DEN, T1, T2, ALU.add, R, W, mark=False)
        kb.op(e, lambda en: en.reciprocal(DEN, DEN), reads=R, writes=W, mark=False)
        tt(e, T1, AR, LR, ALU.mult, R, W, mark=False)
        tt(e, T2, ZI, LI, ALU.mult, R, W, mark=False)
        tt(e, T1, T1, T2, ALU.add, R, W, mark=False)
        tt(e, WR, T1, DEN, ALU.mult, R, W, mark=False)
        tt(e, T1, ZI, LR, ALU.mult, R, W, mark=False)
        tt(e, T2, AR, LI, ALU.mult, R, W, mark=False)
        tt(e, T1, T1, T2, ALU.subtract, R, W, mark=False)
        tt(e, WI, T1, DEN, ALU.mult, R, W, mark=False)
        L8r, L8i = PWF[:, 0, :, 7], PWF[:, 1, :, 7]
        tt(e, T1, L8r, L8r, ALU.mult, R, W, mark=False)
        tt(e, T2, L8i, L8i, ALU.mult, R, W, mark=False)
        tt(e, T1, T1, T2, ALU.add, R, W, mark=False)
        kb.op(e, lambda en: en.reciprocal(T1, T1), reads=R, writes=W, mark=False)
        tt(e, IR, L8r, T1, ALU.mult, R, W, mark=False)
        stt(II, L8i, -1.0, T1, ALU.mult, ALU.mult, R, W, mark=False)
        copy_op(e, A1[:, :, 0], L8r, R, W, mark=False)
        copy_op(e, A1[:, :, 1], L8r, R, W, mark=False)
        ts(e, A2[:, :, 0], L8i, -1.0, None, ALU.mult, None, R, W, mark=False)
        copy_op(e, A2[:, :, 1], L8i, R, W, mark=True)
        prev_label = kb.defer
        if kb.defer:
            kb.defer = "late"
        copy_op(e, PW16[:, 0, :, 0], L8r, R, W)
        copy_op(e, PW16[:, 1, :, 0], L8i, R, W)
        for k in range(1, 16):
            pr, pi = PW16[:, 0, :, k - 1], PW16[:, 1, :, k - 1]
            nr, ni = PW16[:, 0, :, k], PW16[:, 1, :, k]
            tt(e, T1, pr, L8r, ALU.mult, R, W)
            tt(e, T2, pi, L8i, ALU.mult, R, W)
            tt(e, nr, T1, T2, ALU.subtract, R, W)
            tt(e, T3, pr, L8i, ALU.mult, R, W)
            tt(e, T4, pi, L8r, ALU.mult, R, W)
            tt(e, ni, T3, T4, ALU.add, R, W)
        copy_op(e, A1C[:, :, 0], PW16[:, 0, :, 15], R, W)
        copy_op(e, A1C[:, :, 1], PW16[:, 0, :, 15], R, W)
        ts(e, A2C[:, :, 0], PW16[:, 1, :, 15], -1.0, None, ALU.mult, None, R, W)
        copy_op(e, A2C[:, :, 1], PW16[:, 1, :, 15], R, W)
        KR, KI = T(17), T(18)
        for r in range(4):
            if r == 0:
                copy_op(e, KR, PW16[:, 0, :, 7], R, W)
                copy_op(e, KI, PW16[:, 1, :, 7], R, W)
            elif r == 1:
                copy_op(e, KR, PW16[:, 0, :, 15], R, W)
                copy_op(e, KI, PW16[:, 1, :, 15], R, W)
            else:
                tt(e, T1, KR, KR, ALU.mult, R, W)
                tt(e, T2, KI, KI, ALU.mult, R, W)
                tt(e, T3, KR, KI, ALU.mult, R, W)
                tt(e, KR, T1, T2, ALU.subtract, R, W)
                ts(e, KI, T3, 2.0, None, ALU.mult, None, R, W)
            copy_op(e, KA1[r][:, :, 0], KR, R, W)
            copy_op(e, KA1[r][:, :, 1], KR, R, W)
            ts(e, KA2[r][:, :, 0], KI, -1.0, None, ALU.mult, None, R, W)
            copy_op(e, KA2[r][:, :, 1], KI, R, W)
        kb.defer = prev_label
        R2_, W2_ = [bSM, bPRO], [bPRO]
        Bre, Bim, Cre, Cim, BBr, BBi = [GBs[:, i, :, :] for i in range(6)]
        wrb = WR.unsqueeze(2).broadcast_to([128, 16, 16])
        wib = WI.unsqueeze(2).broadcast_to([128, 16, 16])
        TAs = TA[:, :, 0:16]
        tt(e, BBr, Bre, wrb, ALU.mult, R2_, W2_, mark=False)
        tt(e, TAs, Bim, wib, ALU.mult, R2_, W2_, mark=False)
        tt(e, BBr, BBr, TAs, ALU.subtract, R2_, W2_, mark=False)
        tt(e, BBi, Bim, wrb, ALU.mult, R2_, W2_, mark=False)
        tt(e, TAs, Bre, wib, ALU.mult, R2_, W2_, mark=False)
        tt(e, BBi, BBi, TAs, ALU.add, R2_, W2_, mark=False)

        def v4(ap):
            return ap.rearrange("p g (s j) -> p g s j", s=8)

        def pw_b(pw, ri):
            return pw[:, ri, :, :].unsqueeze(3).broadcast_to([128, 16, 8, 16])

        def bc_b(x):
            return x.unsqueeze(2).broadcast_to([128, 16, 8, 16])

        PNr, PNi = v4(PN[:, :, 0, :]), v4(PN[:, :, 1, :])
        QNr, QNi = v4(QN[:, :, 0, :]), v4(QN[:, :, 1, :])
        TA4, TB4 = v4(TA), v4(TB)
        R3_, W3_ = [bSM, bPRO, bS5C], [bPRO, bS5C]
        tt(e, PNr, pw_b(PWR, 0), bc_b(BBr), ALU.mult, R3_, W3_, mark=False)
        tt(e, TA4, pw_b(PWR, 1), bc_b(BBi), ALU.mult, R3_, W3_, mark=False)
        tt(e, PNr, PNr, TA4, ALU.subtract, R3_, W3_, mark=False)
        tt(e, PNi, pw_b(PWR, 0), bc_b(BBi), ALU.mult, R3_, W3_, mark=False)
        tt(e, TA4, pw_b(PWR, 1), bc_b(BBr), ALU.mult, R3_, W3_, mark=False)
        tt(e, PNi, PNi, TA4, ALU.add, R3_, W3_, mark=False)
        tt(e, QNr, pw_b(PWF, 0), bc_b(Cre), ALU.mult, R3_, W3_, mark=False)
        tt(e, TA4, pw_b(PWF, 1), bc_b(Cim), ALU.mult, R3_, W3_, mark=False)
        tt(e, QNr, QNr, TA4, ALU.subtract, R3_, W3_, mark=False)
        tt(e, QNi, pw_b(PWF, 1), bc_b(Cre), ALU.mult, R3_, W3_, mark=False)
        tt(e, TA4, pw_b(PWF, 0), bc_b(Cim), ALU.mult, R3_, W3_, mark=False)
        stt(QNi, QNi, -1.0, TA4, ALU.mult, ALU.subtract, R3_, W3_, mark=True)
        QB = view(OFF["S5C"], BF16, [16, 2, 128])
        copy_op("act", QB, QN, R3_, W3_)
        for par in range(2):
            for ri in range(2):
                kb.dma("sp", [(QT[ri * 64:(ri + 1) * 64, par:32:2, :], QB[par * 64:(par + 1) * 64, :, ri, :])],
                       reads=[bPRO], writes=[bS5C])

    def s5_prologue_pe():
        e = "dve"
        R3_, W3_ = [bSM, bPRO, bS5C], [bPRO, bS5C]
        IR, II = T(19), T(20)
        parts = int(os.environ.get("KDBG_HOOKPARTS", "7"))
        for bk in range(8 if parts & 1 else 0):
            b = nextA()
            for gpl in range(2):
                gp = 2 * bk + gpl
                for ri in range(2):
                    c0 = (gpl * 2 + ri) * 128
                    kb.op("pe", lambda en: en.transpose(PS[b][:, c0:c0 + 128], PN[:, gp, ri, :], IDF),
                          reads=[bPRO, bMISC], writes=[bPS[b]], mark=(gpl == 1 and ri == 1))
            for gpl in range(2):
                gp = 2 * bk + gpl
                copy_op("act", PT[:, 2 * gp:2 * gp + 2, :, :],
                        PS[b][:, gpl * 256:(gpl + 1) * 256].rearrange("p (ri par n) -> p par ri n", ri=2, par=2),
                        [bPS[b]], [bS5C])
        if not parts & 2:
            return
        irb = IR.unsqueeze(2).broadcast_to([128, 16, 128])
        iib = II.unsqueeze(2).broadcast_to([128, 16, 128])
        Pr, Pi = PN[:, :, 0, :], PN[:, :, 1, :]
        tt(e, TA, Pr, irb, ALU.mult, R3_, W3_, mark=False)
        tt(e, TB, Pi, iib, ALU.mult, R3_, W3_, mark=False)
        tt(e, TA, TA, TB, ALU.subtract, R3_, W3_, mark=False)
        tt(e, TB, Pi, irb, ALU.mult, R3_, W3_, mark=False)
        tt(e, Pr, Pr, iib, ALU.mult, R3_, W3_, mark=False)
        tt(e, Pi, TB, Pr, ALU.add, R3_, W3_, mark=True)
        PPB = view(OFF["R2"] + 16 * KBY, BF16, [16, 2, 128])
        copy_op("act", PPB[:, :, 0, :], TA, R3_, W3_)
        copy_op("act", PPB[:, :, 1, :], Pi, R3_, W3_)
        PPS = view(OFF["R2"] + 32 * KBY, BF16, [32, 128])
        for par in range(2):
            for ri in range(2):
                kb.dma("sp", [(PPS[ri * 64:(ri + 1) * 64, par:32:2, :], PPB[par * 64:(par + 1) * 64, :, ri, :])],
                       reads=[bS5C], writes=[bPRO])
        if not parts & 4:
            return
        for bk in range(8):
            b = nextA()
            for gl in range(4):
                g = 4 * bk + gl
                gp, par = g // 2, g % 2
                sl = slice(par * 64, (par + 1) * 64)
                o = PS[b][:, gl * 128:(gl + 1) * 128]
                mm(o, PPS[:, g, :], QT[:, g, :], True, True, [bPRO, bS5C], [bPS[b]], gl == 3)
            tt(e, TMASK, PS[b][:], MASK4, ALU.mult, [bPS[b], bMISC], [bPRO], mark=False)
            for gl in range(4):
                g = 4 * bk + gl
                stt(TOEP[:, g, :], IDF, DSK[:, g:g + 1], TMASK[:, gl * 128:(gl + 1) * 128],
                    ALU.mult, ALU.add, [bPRO, bMISC], [bS5C], mark=(gl == 3))

    def ffn(wi_d, wo_d, ln_idx, hook=None):
        if hook is None:
            load_gb(ln_idx)
        wo_v = wo_d.rearrange("(kt p) n -> p kt n", p=128)
        wi_v = wi_d.rearrange("(kt p) c -> p kt c", p=128)
        slots = {}

        def load_w(j):
            sl = next_slot()
            slots[j] = sl
            wsv = WS[sl].rearrange("p (k c) -> p k c", k=8)
            kb.dma("pool", [(wsv[:, :, 0:256], wi_v[:, :, 256 * j:256 * j + 256]),
                            (wsv[:, :, 256:512], wi_v[:, :, DFF + 256 * j:DFF + 256 * j + 256])],
                   writes=[bWS[sl]])

        for j in range(3):
            load_w(j)
        def load_wout():
            kb.handoff(r2_users, bWOUT)
            for ci, (a, b_) in enumerate(WCH):
                kb.dma("pool", [(WOUT[:, a:b_, :], wo_v[:, a:b_, :])], writes=[bWOUT[ci]])

        if hook is None:
            load_wout()
        kb.handoff(r1_users, [bH[f][th] for f in range(NF) for th in range(2)])
        for j in range(11):
            sl = slots[j]
            wsv = WS[sl].rearrange("p (k c) -> p k c", k=8)
            for fl in range(2):
                f = 2 * j + fl
                for th in range(2):
                    ba, bb = nextA(), nextA()
                    xr = [bXT[4 * th + i] for i in range(4)]
                    for k in range(8):
                        mm(PS[ba][:], wsv[:, k, fl * 128:(fl + 1) * 128], XT[:, k, th * 512:(th + 1) * 512],
                           k == 0, k == 7, [bWS[sl]] + xr, [bPS[ba]], k == 7)
                    for k in range(8):
                        mm(PS[bb][:], wsv[:, k, 256 + fl * 128:256 + (fl + 1) * 128],
                           XT[:, k, th * 512:(th + 1) * 512],
                           k == 0, k == 7, [bWS[sl]] + xr, [bPS[bb]], k == 7)
                    ti = next_tmp()
                    act(TMPA[ti], PS[ba][:], AF.Silu, [bPS[ba]], [bTMPA[ti]])
                    tt("dve", H[:, f, th * 512:(th + 1) * 512], TMPA[ti], PS[bb][:], ALU.mult,
                       [bTMPA[ti], bPS[bb]], [bH[f][th]])
                    flush_pending()
                    if hook is not None:
                        if j <= HOOK_J:
                            kb.run_deferred(DEFER_RATE)
                        else:
                            kb.run_deferred(11, "late")
            if j + 3 < 11:
                load_w(j + 3)
            if hook is not None and j == HOOK_J:
                kb.run_deferred(100000)
                hook()
                load_wout()
                kb.handoff([bPRO], [bGB])
                load_gb(ln_idx)
        if hook is not None:
            kb.run_deferred(100000, "late")
        hs = [bH[f][th] for f in range(NF) for th in range(2)]
        for s in range(8):
            b0 = nextBpair()
            banks = [b0, b0 + 1]
            for hf in range(2):
                for k in range(NF):
                    mm(PS[banks[hf]][:], H[:, k, s * 128:(s + 1) * 128], WOUT[:, k, hf * 512:(hf + 1) * 512],
                       k == 0, k == NF - 1, [bH[k][s // 4], bWOUT[chunk_of(k)]], [bPS[banks[hf]]],
                       k == NF - 1)
            ln(s, PSALL[:, 512 * b0:512 * b0 + 1024], [bPS[b0], bPS[b0 + 1]], 2 * ALPHA, EPS4,
               TMPA[0].bitcast(BF16), bTMPA[0])
            if s > 1:
                transposes_tile(s - 2)
        transposes_tile(6)
        transposes_tile(7)
        return hs + bWOUT

    r1_users = []
    r2_users = [bPRO]

    def dump_and_finish(m):
        store_x(m)

    R1o, R2o = OFF["R1"], OFF["R2"]
    SU = view(R1o, F32, [8, 512])
    SU4 = view(R1o, F32, [32, 8, 16])
    EE_ = view(R1o, F32, [16, 2, 128])
    SSTK = view(R1o, BF16, [32, 128])
    MG = view(R1o, BF16, [8, 1024])
    VV = view(R1o + 16 * KBY, F32, [4, 8, 129])
    SS = view(R1o + 16 * KBY, F32, [16, 2, 129])
    MT = [view(R1o + 16 * KBY + 2 * KBY * i, F32, [512]) for i in range(6)]
    CBZ = view(R1o + 33 * KBY, BF16, [4, 1024])
    SCT8 = view(R1o + 41 * KBY, F32, [2, 16, 2, 8])
    UT = view(R2o, BF16, [32, 128])
    ST_ = view(R2o, BF16, [4, 1024])
    SBF = view(R2o + 8 * KBY, BF16, [16, 2, 128])
    ZZ = view(R2o, F32, [4, 1024])
    STM = view(R2o + 16 * KBY, BF16, [8, 512])
    WMO = view(R2o + 28 * KBY, BF16, [8, 1024])

    def mixer(m):
        mwi_v = mwi_d.rearrange("(kt p) c -> p kt c", p=128)
        glu_v = glu_d.rearrange("(kt p) c -> p kt c", p=128)
        cwo_v = cwo_d.rearrange("(kt p) c -> p kt c", p=128)
        mwo_v = mwo_d.rearrange("(kt p) c -> p kt c", p=128)
        load_gb(1)
        bSU, bE, bMG, bV, bS, bCBZ, bUT, bSBF, bZ, bSTM, bSTT, bWMO = [Buf() for _ in range(12)]
        bMT = [Buf() for _ in range(6)]
        r1_new = [bSU, bV, bCBZ]
        r2_new = [bZ, bSTM, bWMO]
        kb.handoff(r1_users, r1_new)
        kb.handoff(r2_users, r2_new)

        def load_slot(pairs_fn):
            sl = next_slot()
            kb.dma("pool", pairs_fn(sl), writes=[bWS[sl]])
            return sl

        def w8(sl):
            return WS[sl].rearrange("p (k c) -> p k c", k=8)

        sl_cc = [load_slot(lambda sl, q=q: [(w8(sl)[:, :, 0:256], mwi_v[:, :, 512 + 256 * q:768 + 256 * q]),
                                            (w8(sl)[:, :, 256:512], mwi_v[:, :, 1024 + 256 * q:1280 + 256 * q])])
                 for q in range(2)]
        sl_su = load_slot(lambda sl: [(w8(sl), mwi_v[:, :, 1536:2048])])
        kb.dma("pool", [(WMO, mwo_v)], writes=[bWMO])
        for ct in range(4):
            sl = sl_cc[ct // 2]
            ctl = ct % 2
            for th in range(2):
                ba, bb = nextA(), nextA()
                xr = [bXT[4 * th + i] for i in range(4)]
                for k in range(8):
                    mm(PS[ba][:], w8(sl)[:, k, ctl * 128:(ctl + 1) * 128], XT[:, k, th * 512:(th + 1) * 512],
                       k == 0, k == 7, [bWS[sl]] + xr, [bPS[ba]], k == 7)
                for k in range(8):
                    mm(PS[bb][:], w8(sl)[:, k, 256 + ctl * 128:256 + (ctl + 1) * 128],
                       XT[:, k, th * 512:(th + 1) * 512],
                       k == 0, k == 7, [bWS[sl]] + xr, [bPS[bb]], k == 7)
                ti = next_tmp()
                copy_op("act", TMPA[ti], PS[bb][:], [bPS[bb]], [bTMPA[ti]])
                tt("dve", VV[:, ct, 4 * th:4 * th + 4, 1:129], PS[ba][:].rearrange("p (a b) -> p a b", a=4),
                   TMPA[ti].rearrange("p (a b) -> p a b", a=4), ALU.mult,
                   [bPS[ba], bTMPA[ti]], [bV])
                flush_pending()
        sl_cb = load_slot(lambda sl: [(w8(sl), mwi_v[:, :, 0:512])])
        for s in range(8):
            b = nextB()
            for k in range(8):
                mm(PS[b][:], XT[:, k, s * 128:(s + 1) * 128], w8(sl_su)[:, k, :], k == 0, k == 7,
                   [bWS[sl_su], bXT[s]], [bPS[b]], k == 7)
            copy_op(ev_eng(), SU4[:, :, s, :], PS[b][:].rearrange("p (g j) -> p g j", g=32), [bPS[b]], [bSU])
        for ct in range(4):
            copy_op("dve", VV[:, ct, :, 0], VH[:, ct, :], [bVH], [bV], mark=False)
            zv = ZZ[:, ct, :].rearrange("p (a b) -> p a b", a=8)
            act(zv, VV[:, ct, :, 1:129], AF.Identity, [bV, bPRO], [bZ], bias=CW[:, ct, 3:4],
                scale=CW[:, ct, 2:3])
            w1, w0 = CW[:, ct, 1:2], CW[:, ct, 0:1]
            stt(zv[:, 1:8, :], VV[:, ct, 0:7, 1:129], w1, zv[:, 1:8, :], ALU.mult, ALU.add, [bV, bPRO], [bZ], False)
            stt(zv[:, 0, :], VV[:, ct, 7, 0:128], w1, zv[:, 0, :], ALU.mult, ALU.add, [bV, bPRO], [bZ], False)
            stt(zv[:, 2:8, :], VV[:, ct, 0:6, 1:129], w0, zv[:, 2:8, :], ALU.mult, ALU.add, [bV, bPRO], [bZ], False)
            stt(zv[:, 0:2, :], VV[:, ct, 6:8, 0:128], w0, zv[:, 0:2, :], ALU.mult, ALU.add, [bV, bPRO], [bZ], False)
            copy_op("dve", VH[:, ct, :], VV[:, ct, :, 128], [bV], [bVH], mark=True)
        for ct in range(4):
            for th in range(2):
                b = nextA()
                xr = [bXT[4 * th + i] for i in range(4)]
                for k in range(8):
                    mm(PS[b][:], w8(sl_cb)[:, k, ct * 128:(ct + 1) * 128], XT[:, k, th * 512:(th + 1) * 512],
                       k == 0, k == 7, [bWS[sl_cb]] + xr, [bPS[b]], k == 7)
                tt("dve", CBZ[:, ct, th * 512:(th + 1) * 512], PS[b][:], ZZ[:, ct, th * 512:(th + 1) * 512],
                   ALU.mult, [bPS[b], bZ], [bCBZ])
        slots_ft = {}

        def issue_bundle(ft):
            def bundle(sl, ft=ft):
                w = WS[sl]
                return [(w[:, 0:1024].rearrange("p (k c) -> p k c", k=8), mwi_v[:, :, 2048 + 128 * ft:2176 + 128 * ft]),
                        (w[:, 1024:2048].rearrange("p (k c) -> p k c", k=8), mwi_v[:, :, 3072 + 128 * ft:3200 + 128 * ft]),
                        (w[:, 2048:2560].rearrange("p (k c) -> p k c", k=4), glu_v[:, :, 128 * ft:128 * ft + 128]),
                        (w[:, 2560:3072].rearrange("p (k c) -> p k c", k=4), glu_v[:, :, 1024 + 128 * ft:1152 + 128 * ft]),
                        (w[:, 3072:3584].rearrange("p (k c) -> p k c", k=4), cwo_v[:, :, 128 * ft:128 * ft + 128])]
            slots_ft[ft] = load_slot(bundle)

        for ft in range(3):
            issue_bundle(ft)
        kb.handoff([bZ], [bUT, bSBF])
        for gb_ in range(8):
            b = nextB()
            for gl in range(4):
                g = 4 * gb_ + gl
                kb.op("pe", lambda e: e.transpose(PS[b][:, gl * 128:(gl + 1) * 128],
                                                  SU4[:, g, :, :].rearrange("p s j -> p (s j)"), IDF),
                      reads=[bSU, bMISC], writes=[bPS[b]], mark=(gl == 3))
            copy_op(ev_eng(), UT[:, 4 * gb_:4 * gb_ + 4, :].rearrange("p a b -> p (a b)"), PS[b][:],
                    [bPS[b]], [bUT])
        kb.handoff([bSU], [bE])
        kb.handoff([bV], [bS])
        for bk in range(8):
            b = nextA()
            for gpl in range(2):
                gp = 2 * bk + gpl
                for par in range(2):
                    g = 2 * gp + par
                    for ri in range(2):
                        c0 = gpl * 256 + ri * 128
                        mm(PS[b][par * 64:(par + 1) * 64, c0:c0 + 128], PT[:, g, ri, :], UT[:, g, :],
                           True, True, [bS5C, bUT], [bPS[b]], (gpl == 1 and par == 1 and ri == 1))
            copy_op("act", EE_[:, 2 * bk:2 * bk + 2, :, :].rearrange("p a b c -> p (a b c)"), PS[b][:],
                    [bPS[b]], [bE])
        def bv(ap4, off):
            return ap4[:, :, :, off:off + 121:8]

        SCT = view(R2o + 16 * KBY, F32, [2, 16, 2, 16])
        T1_, T2_ = SCT[:, 0, :, :, :], SCT[:, 1, :, :, :]
        CB = view(R2o + 8 * KBY, F32, [16, 2, 17])
        VA = view(R2o + 8 * KBY + 2304, F32, [16, 2, 16])
        VB = view(R2o + 8 * KBY + 2304 + 2048, F32, [16, 2, 16])
        A1b = A1.unsqueeze(3).broadcast_to([128, 16, 2, 16])
        A2b0 = A2[:, :, 0:1].broadcast_to([128, 16, 16])
        A2b1 = A2[:, :, 1:2].broadcast_to([128, 16, 16])
        WS_ = [bS, bSTM]
        copy_op("dve", SS[:, :, :, 0], SC, [bSC, bSM], [bS])
        copy_op("dve", bv(SS, 1), bv(EE_, 0), [bE], [bS])
        for k in range(1, 8):
            prev, cur, ek = bv(SS, k), bv(SS, k + 1), bv(EE_, k)
            tt("dve", T1_, A1b, prev, ALU.mult, [bE], WS_)
            tt("dve", T2_[:, :, 0, :], A2b0, prev[:, :, 1, :], ALU.mult, [], WS_)
            tt("dve", T2_[:, :, 1, :], A2b1, prev[:, :, 0, :], ALU.mult, [], WS_)
            tt("dve", T1_, T1_, T2_, ALU.add, [], WS_)
            tt("dve", cur, T1_, ek, ALU.add, [bE], WS_)
        WC_ = [bS, bSTM, bSBF]
        lend = SS[:, :, :, 8:129:8]
        copy_op("dve", VA, lend, [bS], WC_)
        copy_op("dve", CB[:, :, :, 0], SC, [bSC], WC_)
        t1s, t2s = T1_[:, :, :, 0], T2_[:, :, :, 0]
        tt("dve", t1s, KA1[0], SC, ALU.mult, [bSM, bSC], WC_)
        tt("dve", t2s[:, :, 0], KA2[0][:, :, 0], SC[:, :, 1], ALU.mult, [], WC_)
        tt("dve", t2s[:, :, 1], KA2[0][:, :, 1], SC[:, :, 0], ALU.mult, [], WC_)
        tt("dve", t1s, t1s, t2s, ALU.add, [], WC_)
        tt("dve", VA[:, :, :, 0], VA[:, :, :, 0], t1s, ALU.add, [], WC_)
        src, dst = VA, VB
        for r in range(4):
            d = 1 << r
            n = 16 - d
            if r == 3:
                dst = CB[:, :, :, 1:17]
            ka1 = KA1[r].unsqueeze(3).broadcast_to([128, 16, 2, n])
            ka20 = KA2[r][:, :, 0:1].broadcast_to([128, 16, n])
            ka21 = KA2[r][:, :, 1:2].broadcast_to([128, 16, n])
            tt("dve", T1_[:, :, :, 0:n], ka1, src[:, :, :, 0:n], ALU.mult, [bSM], WC_)
            tt("dve", T2_[:, :, 0, 0:n], ka20, src[:, :, 1, 0:n], ALU.mult, [], WC_)
            tt("dve", T2_[:, :, 1, 0:n], ka21, src[:, :, 0, 0:n], ALU.mult, [], WC_)
            tt("dve", T1_[:, :, :, 0:n], T1_[:, :, :, 0:n], T2_[:, :, :, 0:n], ALU.add, [], WC_)
            tt("dve", dst[:, :, :, d:16], src[:, :, :, d:16], T1_[:, :, :, 0:n], ALU.add, [], WC_)
            copy_op("dve", dst[:, :, :, 0:d], src[:, :, :, 0:d], [], WC_)
            src, dst = dst, (VA if dst is VB else VB)
        bFX = [Buf(), Buf()]
        bS2 = Buf()
        kb.handoff([bE], bFX)
        kb.handoff([bS], [bS2])
        for hh, en_ in ((0, "dve"), (1, "pool")):
            gsl = slice(8 * hh, 8 * hh + 8)
            TF1 = view(R1o + 8 * KBY * hh, F32, [8, 16, 8])
            TF2 = view(R1o + 8 * KBY * hh + 4 * KBY, F32, [8, 16, 8])
            pwr = PW16[:, 0, gsl, 0:8].unsqueeze(2).broadcast_to([128, 8, 16, 8])
            pwi = PW16[:, 1, gsl, 0:8].unsqueeze(2).broadcast_to([128, 8, 16, 8])
            cbr = CB[:, gsl, 0, 0:16].unsqueeze(3).broadcast_to([128, 8, 16, 8])
            cbi = CB[:, gsl, 1, 0:16].unsqueeze(3).broadcast_to([128, 8, 16, 8])
            sre = SS[:, gsl, 0, 1:129].rearrange("p g (b k) -> p g b k", b=16)
            sim = SS[:, gsl, 1, 1:129].rearrange("p g (b k) -> p g b k", b=16)
            bSh = bS if hh == 0 else bS2
            Rr, Ww = [bFX[hh], bSh, bSBF, bSM], [bFX[hh], bSh]
            tt(en_, TF1, pwr, cbr, ALU.mult, Rr, Ww)
            tt(en_, TF2, pwi, cbi, ALU.mult, Rr, Ww)
            tt(en_, TF1, TF1, TF2, ALU.subtract, Rr, Ww)
            tt(en_, sre, sre, TF1, ALU.add, Rr, Ww)
            tt(en_, TF1, pwr, cbi, ALU.mult, Rr, Ww)
            tt(en_, TF2, pwi, cbr, ALU.mult, Rr, Ww)
            tt(en_, TF1, TF1, TF2, ALU.add, Rr, Ww)
            tt(en_, sim, sim, TF1, ALU.add, Rr, Ww)
        copy_op("dve", SC, CB[:, :, :, 16], [bSBF], [bSC])
        copy_op("act", SBF, SS[:, :, :, 0:128], [bS, bS2], [bSBF])
        bSSTK = Buf()
        kb.handoff([bE] + bFX, [bSSTK])
        for par in range(2):
            for ri in range(2):
                kb.dma("sp", [(SSTK[ri * 64:(ri + 1) * 64, par:32:2, :], SBF[par * 64:(par + 1) * 64, :, ri, :])],
                       reads=[bSBF], writes=[bSSTK])
        for gb_ in range(8):
            b = nextB()
            for gl in range(4):
                g = 4 * gb_ + gl
                gp, par = g // 2, g % 2
                sl = slice(par * 64, (par + 1) * 64)
                o = PS[b][:, gl * 128:(gl + 1) * 128]
                mm(o, UT[:, g, :], TOEP[:, g, :], True, False, [bUT, bS5C], [bPS[b]], False)
                mm(o, SSTK[:, g, :], QT[:, g, :], False, True, [bSSTK, bS5C], [bPS[b]], gl == 3)
            act(STM[:, :, 64 * gb_:64 * gb_ + 64].rearrange("p t (g i) -> p g t i", g=4),
                PS[b][:].rearrange("p (g t i) -> p g t i", g=4, t=8), AF.Gelu_apprx_tanh,
                [bPS[b]], [bSTM])
        kb.handoff([bUT], [bSTT])
        for kt in range(4):
            b = nextA()
            psb = PS[b][:].bitcast(BF16)
            for t_ in range(8):
                kb.op("pe", lambda e: e.transpose(psb[:, t_ * 128:(t_ + 1) * 128],
                                                  STM[:, t_, kt * 128:(kt + 1) * 128], IDB),
                      reads=[bSTM, bMISC], writes=[bPS[b]], mark=(t_ == 7))
            copy_op(ev_eng(), ST_[:, kt, :], psb, [bPS[b]], [bSTT])
        kb.handoff([bSSTK], [bMG])
        kb.handoff([bS, bS2], bMT)
        for ft in range(8):
            sl = slots_ft[ft]
            w = WS[sl]
            wgc = w[:, 0:1024].rearrange("p (k c) -> p k c", k=8)
            wgs = w[:, 1024:2048].rearrange("p (k c) -> p k c", k=8)
            wga = w[:, 2048:2560].rearrange("p (k c) -> p k c", k=4)
            wgb = w[:, 2560:3072].rearrange("p (k c) -> p k c", k=4)
            wco = w[:, 3072:3584].rearrange("p (k c) -> p k c", k=4)
            for th in range(2):
                hs = slice(th * 512, (th + 1) * 512)
                xr = [bXT[4 * th + i] for i in range(4)]
                b1, b2, b3, b4, b5 = nextAll(), nextAll(), nextAll(), nextAll(), nextAll()
                for k in range(8):
                    mm(PS[b1][:], wgc[:, k, :], XT[:, k, hs], k == 0, k == 7, [bWS[sl]] + xr, [bPS[b1]], k == 7)
                for k in range(8):
                    mm(PS[b2][:], wgs[:, k, :], XT[:, k, hs], k == 0, k == 7, [bWS[sl]] + xr, [bPS[b2]], k == 7)
                for k in range(4):
                    mm(PS[b3][:], wgb[:, k, :], ST_[:, k, hs], k == 0, k == 3, [bWS[sl], bSTT], [bPS[b3]], k == 3)
                for k in range(4):
                    mm(PS[b4][:], wga[:, k, :], ST_[:, k, hs], k == 0, k == 3, [bWS[sl], bSTT], [bPS[b4]], k == 3)
                for k in range(4):
                    mm(PS[b5][:], wco[:, k, :], CBZ[:, k, hs], k == 0, k == 3, [bWS[sl], bCBZ], [bPS[b5]], k == 3)
                i0 = 3 * ((2 * ft + th) % 2)
                t1, t2, t3 = MT[i0], MT[i0 + 1], MT[i0 + 2]
                q1, q2, q3 = bMT[i0], bMT[i0 + 1], bMT[i0 + 2]
                act(t1, PS[b1][:], AF.Sigmoid, [bPS[b1]], [q1])
                act(t2, PS[b2][:], AF.Sigmoid, [bPS[b2]], [q2])
                act(t3, PS[b3][:], AF.Sigmoid, [bPS[b3]], [q3])
                tt("dve", t1, PS[b5][:], t1, ALU.mult, [bPS[b5]], [q1], mark=False)
                tt("dve", t3, PS[b4][:], t3, ALU.mult, [bPS[b4]], [q3], mark=False)
                tt("dve", t2, t3, t2, ALU.mult, [q3], [q2], mark=False)
                tt("dve", MG[:, ft, hs], t1, t2, ALU.add, [q1, q2], [bMG], mark=True)
            if ft + 3 < 8:
                issue_bundle(ft + 3)
        for s in range(8):
            b0 = nextBpair()
            banks = [b0, b0 + 1]
            for hf in range(2):
                for k in range(8):
                    mm(PS[banks[hf]][:], MG[:, k, s * 128:(s + 1) * 128], WMO[:, k, hf * 512:(hf + 1) * 512],
                       k == 0, k == 7, [bMG, bWMO], [bPS[banks[hf]]], k == 7)
            ln(s, PSALL[:, 512 * b0:512 * b0 + 1024], [bPS[b0], bPS[b0 + 1]], ALPHA, EPS1,
               TMPA[0].bitcast(BF16), bTMPA[0])
            if s > 2:
                transposes_tile(s - 3)
        transposes_tile(5)
        transposes_tile(6)
        transposes_tile(7)
        return ([bSU, bE, bSSTK, bMG, bV, bS, bS2, bCBZ] + bMT + bFX,
                [bUT, bSBF, bZ, bSTM, bSTT, bWMO])

    PSB = view(R1o, F32, [8, 256])
    PTT = view(R1o + 8 * KBY, BF16, [2, 1024])
    EEt = [view(R1o + 12 * KBY + 2 * KBY * i, F32, [512]) for i in range(4)]

    def ple(m):
        load_gb(3)
        bP, bPTT = Buf(), Buf()
        bEE = [Buf() for _ in range(4)]
        bJ = Buf()
        kb.handoff(r1_users, [bP, bPTT, bJ] + bEE)
        kb.dma("sp", [(PSB, p_v[m])], writes=[bP])
        pwg_v = pwg_d.rearrange("(kt p) c -> p kt c", p=128)
        pwi_v = pwi_d.rearrange("(kt p) c -> p kt c", p=128)
        sg = []
        for hf in range(2):
            sl = next_slot()
            kb.dma("pool", [(WS[sl].rearrange("p (k c) -> p k c", k=8), pwg_v[:, :, hf * 512:(hf + 1) * 512])],
                   writes=[bWS[sl]])
            sg.append(sl)
        sli = next_slot()
        wpi = WS[sli][:, 0:2048].rearrange("p (k c) -> p k c", k=2)
        kb.dma("pool", [(wpi, pwi_v)], writes=[bWS[sli]])
        for sp_ in range(4):
            b = nextA()
            for i in range(4):
                s, kt = 2 * sp_ + i // 2, i % 2
                kb.op("pe", lambda e: e.transpose(PS[b][:, i * 128:(i + 1) * 128],
                                                  PSB[:, s, kt * 128:(kt + 1) * 128], IDF),
                      reads=[bP, bMISC], writes=[bPS[b]], mark=(i == 3))
            copy_op(ev_eng(), PTT[:, :, 256 * sp_:256 * sp_ + 256].rearrange("p k (s c) -> p s k c", s=2),
                    PS[b][:].rearrange("p (s k c) -> p s k c", s=2, k=2), [bPS[b]], [bPTT])
        for s in range(8):
            b0 = nextBpair()
            for hf in range(2):
                be, bg = b0 + hf, nextA()
                w = WS[sg[hf]].rearrange("p (k c) -> p k c", k=8)
                for kt in range(2):
                    mm(PS[be][:], PTT[:, kt, s * 128:(s + 1) * 128], wpi[:, kt, hf * 512:(hf + 1) * 512],
                       kt == 0, kt == 1, [bPTT, bWS[sli]], [bPS[be]], kt == 1)
                for k in range(8):
                    mm(PS[bg][:], XT[:, k, s * 128:(s + 1) * 128], w[:, k, :], k == 0, k == 7,
                       [bXT[s], bWS[sg[hf]]], [bPS[bg]], k == 7)
                ti = next_tmp()
                act(TMPA[ti], PS[bg][:], AF.Sigmoid, [bPS[bg]], [bTMPA[ti]])
                tt("dve", PS[be][:], PS[be][:], TMPA[ti], ALU.mult, [bTMPA[ti]], [bPS[be]])
            ln(s, PSALL[:, 512 * b0:512 * b0 + 1024], [bPS[b0], bPS[b0 + 1]], ALPHA, EPS1,
               view(R1o + 20 * KBY, BF16, [1024]), bJ)
            kb.dma("sp", [(o_v[m, :, s, :], X[:, s, :])], reads=[bX[s]])
            if m == 0:
                kb.dma("sp", [(X[:, s, :], x_v[1, :, s, :])], writes=[bX[s]])
        return [bP, bPTT, bJ] + bEE

    load_x(0)
    kb.defer = "main"
    s5_prologue_math()
    kb.defer = False
    n_def = len(kb.deferred_qs["main"])
    HOOK_J = 7
    DEFER_RATE = (n_def + 4 * (HOOK_J + 1) - 1) // (4 * (HOOK_J + 1)) + 1
    if STAGE == -2:
        kb.run_deferred(100000)
        kb.run_deferred(100000, "late")
    if STAGE == -2:
        s5_prologue_pe()
        S5ALL = view(OFF["S5C"], BF16, [12288])
        Xf = view(OFF["X"], F32, [8192])
        copy_op("act", Xf, S5ALL[:, 0:8192], [bS5C] + bX, bX)
        store_x(0)
        copy_op("act", Xf[:, 0:4096], S5ALL[:, 8192:12288], [bS5C] + bX, bX)
        copy_op("act", Xf[:, 4096:4096 + 64], view(OFF["MISC"], F32, [3840])[:, 0:64], [bSM] + bX, bX)
        store_x(1)
        kb.finish()
        es.close()
        return nc
    for m in range(2):
        if m > 0 and STAGE < 4:
            load_x(m)
        for s in range(8):
            transposes_tile(s)
        if STAGE <= 0:
            store_x(m)
            continue
        u = ffn(w1i_d, w1o_d, 0, hook=(s5_prologue_pe if (m == 0 and not os.environ.get('KDBG_NOHOOK')) else None))
        r1_users, r2_users = u[:2 * NF], u[2 * NF:]
        if STAGE == 1:
            flush_pending()
            store_x(m)
            continue
        a, b_ = mixer(m)
        r1_users, r2_users = a, b_
        if STAGE == 2:
            flush_pending()
            store_x(m)
            continue
        u = ffn(w2i_d, w2o_d, 2)
        r1_users, r2_users = u[:2 * NF], u[2 * NF:]
        if STAGE == 3:
            flush_pending()
            store_x(m)
            continue
        r1_users = ple(m)
    kb.finish()
    es.close()
    return nc


_NC_CACHE = {}


def kernel(**inputs):
    if "nc" not in _NC_CACHE:
        _NC_CACHE["nc"] = build_program()
    nc = _NC_CACHE["nc"]
    names = ["ffn1_w_in", "ffn1_w_out", "ffn2_w_in", "ffn2_w_out", "ln1_g", "ln1_b", "ln2_g", "ln2_b",
             "ln3_g", "ln3_b", "ln4_g", "ln4_b", "mix_w_in", "conv_w", "conv_b", "conv_w_out",
             "ssm_lam_re", "ssm_lam_im", "ssm_log_step", "ssm_b_re", "ssm_b_im", "ssm_c_re", "ssm_c_im",
             "ssm_d", "ssm_w_glu", "mix_w_out", "ple_w_in", "ple_w_gate"]
    shared = {}
    for n in names:
        a = np.ascontiguousarray(np.asarray(inputs[n], dtype=np.float32))
        if n.startswith("ln") or n in ("conv_b", "ssm_log_step"):
            shared[n] = a.reshape(1, -1)
        else:
            shared[n] = a[0]
    x = np.asarray(inputs["x"], dtype=np.float32)
    p = np.asarray(inputs["p"], dtype=np.float32)
    in_maps = []
    for c in range(NCORES):
        d = dict(shared)
        d["x"] = np.ascontiguousarray(x[c])
        d["p"] = np.ascontiguousarray(p[0, c])
        in_maps.append(d)
    res = run_bass_kernel_spmd(nc, in_maps, core_ids=list(range(NCORES)))
    return np.stack([np.asarray(r["out"], dtype=np.float32) for r in res.results], axis=0)
```

```python
import math
import os
from contextlib import ExitStack

import numpy as np
import concourse.bass as bass
import concourse.mybir as mybir
from concourse.bass_utils import run_bass_kernel_spmd

F32 = mybir.dt.float32
BF16 = mybir.dt.bfloat16
U8 = mybir.dt.uint8
AF = mybir.ActivationFunctionType
ALU = mybir.AluOpType

D = 1024
SEQ = 2048
DFF = 2816
NF = DFF // 128
PLE = 256
ALPHA = 2.0 ** 0.25
EPS = 1e-5
NCORES = int(os.environ.get('KDBG_CORES', '8'))
STAGE = 99


class Eng:
    def __init__(self, name, h, sem):
        self.name, self.h, self.sem = name, h, sem
        self.count = 0
        self.waited = {}


class DSem:
    def __init__(self, h):
        self.h = h
        self.count = 0


class Buf:
    __slots__ = ("w", "r", "ds", "name")

    def __init__(self, name=""):
        self.w = None
        self.r = []
        self.ds = None
        self.name = name


class KB:
    def __init__(self, nc, es):
        self.nc = nc
        self.es = es
        self.eng = {}
        for name, h in (("pe", nc.tensor), ("act", nc.scalar), ("dve", nc.vector),
                        ("pool", nc.gpsimd), ("sp", nc.sync)):
            sem = es.enter_context(nc.semaphore("s_" + name))
            self.eng[name] = Eng(name, h, sem)
        self.dsems = []
        self.nds = 0
        self.defer = False
        self.deferred_q = []
        self.chain = False

    def new_dsem(self):
        h = self.es.enter_context(self.nc.semaphore("d%d" % self.nds))
        self.nds += 1
        d = DSem(h)
        self.dsems.append(d)
        return d

    def _wait(self, E, tok):
        key, sem, val, owner = tok
        if owner == E.name and (owner == "pe" or self.chain):
            return
        if owner is not None:
            assert self.eng[owner].count >= val, "pending token"
        if E.waited.get(key, 0) >= val:
            return
        E.h.wait_ge(sem, val)
        E.waited[key] = val

    def _deps(self, E, reads, writes):
        for b in reads:
            if b.w is not None:
                self._wait(E, b.w)
        for b in writes:
            if b.w is not None:
                self._wait(E, b.w)
            for t in b.r:
                self._wait(E, t)

    def _commit(self, tok, reads, writes):
        for b in writes:
            b.w = tok
            b.r = []
        for b in reads:
            if b not in writes:
                b.r = [t for t in b.r if t[0] != tok[0]] + [tok]

    def run_deferred(self, n):
        q = self.deferred_q
        for _ in range(min(n, len(q))):
            a = q.pop(0)
            if a[0] == "__dma__":
                self.dma(*a[1:], _replay=True)
            else:
                self.op(*a, _replay=True)

    def op(self, en, fn, reads=(), writes=(), mark=True, _replay=False):
        if self.defer and not _replay:
            self.deferred_q.append((en, fn, list(reads), list(writes), mark))
            return None
        E = self.eng[en]
        self._deps(E, reads, writes)
        ins = fn(E.h)
        if en != "pe":
            mark = True
        if mark:
            ins.then_inc(E.sem, 1)
            E.count += 1
            tok = ("e_" + en, E.sem, E.count, en)
        else:
            tok = ("e_" + en, E.sem, E.count + 1, en)
        self._commit(tok, reads, writes)
        return ins

    def dma(self, qn, pairs, reads=(), writes=(), ds=None, slow=False, eager=False, _replay=False):
        if self.defer and not eager and not _replay:
            self.deferred_q.append(("__dma__", qn, list(pairs), list(reads), list(writes), ds, slow))
            return
        Q = self.eng[qn]
        self._deps(Q, reads, writes)
        if ds is None:
            tgt = writes[0] if writes else reads[0]
            if tgt.ds is None:
                tgt.ds = self.new_dsem()
            ds = tgt.ds
        for (o, i) in pairs:
            if slow:
                Q.h.dma_start(out=o, in_=i, allow_slow_non_contiguous=True).then_inc(ds.h, 16)
            else:
                Q.h.dma_start(out=o, in_=i).then_inc(ds.h, 16)
            ds.count += 16
        tok = ("d%d" % id(ds), ds.h, ds.count, None)
        self._commit(tok, reads, writes)

    def handoff(self, old, new):
        toks = {}
        for b in old:
            for t in ([b.w] if b.w is not None else []) + list(b.r):
                if t[0] not in toks or toks[t[0]][2] < t[2]:
                    toks[t[0]] = t
        for b in new:
            b.w = None
            b.r = list(toks.values())

    def finish(self):
        for E in self.eng.values():
            for O in self.eng.values():
                if O is not E and O.count > 0:
                    self._wait(E, ("e_" + O.name, O.sem, O.count, O.name))
        sp = self.eng["sp"]
        for d in self.dsems:
            if d.count > 0:
                sp.h.wait_ge(d.h, d.count)


def build_program():
    nc = bass.Bass("TRN2", target_bir_lowering=False)
    es = ExitStack()

    def din(name, shape):
        return nc.dram_tensor(name, list(shape), F32, kind="ExternalInput").ap()

    x_d = din("x", (SEQ, D))
    p_d = din("p", (SEQ, PLE))
    w1i_d = din("ffn1_w_in", (D, 2 * DFF))
    w1o_d = din("ffn1_w_out", (DFF, D))
    w2i_d = din("ffn2_w_in", (D, 2 * DFF))
    w2o_d = din("ffn2_w_out", (DFF, D))
    lng_d = [din("ln%d_g" % i, (1, D)) for i in range(1, 5)]
    lnb_d = [din("ln%d_b" % i, (1, D)) for i in range(1, 5)]
    mwi_d = din("mix_w_in", (D, 4096))
    cw_d = din("conv_w", (3, 512))
    cb_d = din("conv_b", (1, 512))
    cwo_d = din("conv_w_out", (512, D))
    lre_d = din("ssm_lam_re", (32, 64))
    lim_d = din("ssm_lam_im", (32, 64))
    lst_d = din("ssm_log_step", (1, 32))
    bre_d = din("ssm_b_re", (32, 64, 16))
    bim_d = din("ssm_b_im", (32, 64, 16))
    cre_d = din("ssm_c_re", (32, 16, 64))
    cim_d = din("ssm_c_im", (32, 16, 64))
    dsk_d = din("ssm_d", (32, 16))
    glu_d = din("ssm_w_glu", (512, 2 * D))
    mwo_d = din("mix_w_out", (D, D))
    pwi_d = din("ple_w_in", (PLE, D))
    pwg_d = din("ple_w_gate", (D, D))
    out_d = nc.dram_tensor("out", [SEQ, D], F32, kind="ExternalOutput").ap()

    KBY = 1024
    OFF = {}
    cur = 0

    def region(name, nbytes):
        nonlocal cur
        OFF[name] = cur
        cur += nbytes

    region("X", 32 * KBY)
    region("XT", 16 * KBY)
    region("GB", 8 * KBY)
    region("R1", 44 * KBY)
    region("R2", 44 * KBY)
    region("R3", 24 * KBY)
    region("S5C", 24 * KBY)
    MISC_BYTES = 16000
    region("MISC", MISC_BYTES)
    ARENA_BYTES = cur
    arena = es.enter_context(nc.sbuf_tensor("arena", [128, ARENA_BYTES], U8))

    def view(off, dtype, shape):
        n = 1
        for s in shape:
            n *= s
        nb = n * (4 if dtype == F32 else 2)
        a = arena[:, off:off + nb].bitcast(dtype)
        if len(shape) == 1:
            return a
        names = " ".join("d%d" % i for i in range(len(shape)))
        kw = {"d%d" % i: shape[i] for i in range(len(shape) - 1)}
        return a.rearrange("p (%s) -> p %s" % (names, names), **kw)

    X = view(OFF["X"], F32, [8, 1024])
    XT = view(OFF["XT"], BF16, [8, 1024])
    GB = view(OFF["GB"], F32, [2, 1024])
    H = view(OFF["R1"], BF16, [NF, 1024])
    WOUT = view(OFF["R2"], BF16, [NF, 1024])
    WS = [view(OFF["R3"] + 8 * KBY * i, BF16, [4096]) for i in range(3)]
    TOEP = view(OFF["S5C"], BF16, [32, 128])
    PT = view(OFF["S5C"] + 8 * KBY, BF16, [32, 2, 64])
    QT = view(OFF["S5C"] + 16 * KBY, BF16, [32, 128])
    mo = OFF["MISC"]

    def misc(dtype, shape):
        nonlocal mo
        n = 1
        for s in shape:
            n *= s
        v = view(mo, dtype, shape)
        mo += n * (4 if dtype == F32 else 2)
        mo = (mo + 63) // 64 * 64
        return v

    IDF = misc(F32, [128])
    IDB = misc(BF16, [128])
    MASK4 = misc(F32, [512])
    CW = misc(F32, [4, 4])
    VH = misc(F32, [4, 8])
    SC = misc(F32, [16, 2])
    A1 = misc(F32, [16, 2])
    A2 = misc(F32, [16, 2])
    DSK = misc(F32, [32])
    LNS0 = misc(F32, [4])
    LNS1 = misc(F32, [4])
    LNA0 = misc(F32, [4])
    LNA1 = misc(F32, [4])
    LNS = [LNS0, LNS1]
    LNA = [LNA0, LNA1]
    HALFPI = misc(F32, [1])
    SM = misc(F32, [22, 16])
    PW16 = misc(F32, [2, 16, 16])
    A1C = misc(F32, [16, 2])
    A2C = misc(F32, [16, 2])
    KA1 = [misc(F32, [16, 2]) for _ in range(4)]
    KA2 = [misc(F32, [16, 2]) for _ in range(4)]
    PWF = misc(F32, [2, 16, 8])
    PWR = misc(F32, [2, 16, 8])

    PSALL = es.enter_context(nc.psum_tensor("psall", [128, 4096], F32))
    PS = [PSALL[:, 512 * i:512 * (i + 1)] for i in range(8)]

    kb = KB(nc, es)
    TMPA = [misc(F32, [512]) for _ in range(2)]
    EPS4 = misc(F32, [1])
    EPS1 = misc(F32, [1])
    assert mo <= OFF["MISC"] + MISC_BYTES, (mo - OFF["MISC"])
    if os.environ.get("KDBG_PRINT"):
        print("MISC used", mo - OFF["MISC"], "of", MISC_BYTES)

    bX = [Buf() for _ in range(8)]
    bXT = [Buf() for _ in range(8)]
    bGB = Buf()
    bH = [[Buf() for _ in range(2)] for _ in range(NF)]
    WCH = [(0, 6), (6, 12), (12, 17), (17, 22)]
    bWOUT = [Buf() for _ in range(4)]
    bWS = [Buf() for _ in range(3)]
    bPS = [Buf() for _ in range(8)]
    bS5C = Buf()
    bPRO = Buf()
    bMISC = Buf()
    bTMPA = [Buf(), Buf()]
    bST = [Buf(), Buf()]
    bSTA = [Buf(), Buf()]
    bVH = Buf()
    bSC = Buf()

    def chunk_of(k):
        for i, (a, b) in enumerate(WCH):
            if a <= k < b:
                return i

    rr = {"A": 0, "B": 0, "slot": 0, "tmp": 0, "ev": 0}

    def nextA():
        rr["A"] = (rr["A"] + 1) % 4
        return rr["A"]

    def nextB():
        rr["B"] = (rr["B"] + 1) % 4
        return 4 + rr["B"]

    def nextAll():
        rr["all"] = (rr.get("all", 0) + 1) % 8
        return rr["all"]

    def nextBpair():
        rr["Bp"] = (rr.get("Bp", 0) + 1) % 2
        return 4 + 2 * rr["Bp"]

    def next_slot():
        rr["slot"] = (rr["slot"] + 1) % 3
        return rr["slot"]

    def next_tmp():
        rr["tmp"] = (rr["tmp"] + 1) % 2
        return rr["tmp"]

    def ev_eng():
        rr["ev"] = (rr["ev"] + 1) % 2
        return "act" if rr["ev"] else "dve"

    def mm(out, lhsT, rhs, start, stop, reads, writes, mark):
        kb.op("pe", lambda e: e.matmul(out, lhsT=lhsT, rhs=rhs, start=start, stop=stop),
              reads, writes, mark)

    def copy_op(en, out, in_, reads, writes, mark=True):
        if en == "act":
            kb.op("act", lambda e: e.activation(out=out, in_=in_, func=AF.Copy), reads, writes, mark)
        else:
            kb.op(en, lambda e: e.tensor_copy(out, in_), reads, writes, mark)

    def tt(en, out, a, b, op, reads, writes, mark=True):
        kb.op(en, lambda e: e.tensor_tensor(out, a, b, op), reads, writes, mark)

    def ts(en, out, a, s1, s2, op0, op1, reads, writes, mark=True):
        if s2 is None:
            kb.op(en, lambda e: e.tensor_scalar(out, a, s1, None, op0), reads, writes, mark)
        else:
            kb.op(en, lambda e: e.tensor_scalar(out, a, s1, s2, op0, op1), reads, writes, mark)

    def stt(out, in0, scalar, in1, op0, op1, reads, writes, mark=True):
        kb.op("dve", lambda e: e.scalar_tensor_tensor(out=out, in0=in0, scalar=scalar, in1=in1,
                                                      op0=op0, op1=op1), reads, writes, mark)

    def act(out, in_, func, reads, writes, bias=None, scale=None, mark=True):
        kw = {}
        if bias is not None:
            kw["bias"] = bias
        if scale is not None:
            kw["scale"] = scale
        kb.op("act", lambda e: e.activation(out=out, in_=in_, func=func, **kw), reads, writes, mark)

    kb.op("dve", lambda e: e.memset(IDF, 1.0), writes=[bMISC], mark=False)
    kb.op("dve", lambda e: e.memset(MASK4, 1.0), writes=[bMISC], mark=False)
    kb.op("dve", lambda e: e.memset(HALFPI, math.pi / 2), writes=[bMISC], mark=False)
    kb.op("dve", lambda e: e.memset(EPS4, 4 * EPS), writes=[bMISC], mark=False)
    kb.op("dve", lambda e: e.memset(EPS1, EPS), writes=[bMISC], mark=False)
    kb.op("dve", lambda e: e.memset(VH, 0.0), writes=[bVH], mark=False)
    kb.op("dve", lambda e: e.memset(SC, 0.0), writes=[bSC], mark=True)
    kb.op("pool", lambda e: e.affine_select(out=IDF, in_=IDF, pattern=[[-1, 128]],
                                            compare_op=ALU.is_equal, fill=0.0, base=0,
                                            channel_multiplier=1), writes=[bMISC])
    kb.op("pool", lambda e: e.affine_select(out=MASK4.rearrange("p (a b c) -> p a b c", a=4, b=8),
                                            in_=MASK4.rearrange("p (a b c) -> p a b c", a=4, b=8),
                                            pattern=[[0, 4], [16, 8], [0, 16]],
                                            compare_op=ALU.is_ge, fill=0.0, base=15,
                                            channel_multiplier=-1), writes=[bMISC])
    copy_op("dve", IDB, IDF, [bMISC], [bMISC])

    x_v = x_d.rearrange("(m c s) d -> m c s d", m=2, s=8)
    o_v = out_d.rearrange("(m c s) d -> m c s d", m=2, s=8)
    p_v = p_d.rearrange("(m c s) d -> m c s d", m=2, s=8)

    def load_x(m):
        for s in range(8):
            kb.dma("sp", [(X[:, s, :], x_v[m, :, s, :])], writes=[bX[s]])

    def store_x(m):
        for s in range(8):
            kb.dma("sp", [(o_v[m, :, s, :], X[:, s, :])], reads=[bX[s]])

    def transposes_tile(s):
        for kbk in range(2):
            b = nextA()
            for kk in range(4):
                k = 4 * kbk + kk
                kb.op("pe", lambda e: e.transpose(PS[b][:, kk * 128:(kk + 1) * 128],
                                                  X[:, s, k * 128:(k + 1) * 128], IDF),
                      reads=[bX[s], bMISC], writes=[bPS[b]], mark=(kk == 3))
            copy_op(ev_eng(), XT[:, 4 * kbk:4 * kbk + 4, s * 128:(s + 1) * 128],
                    PS[b][:].rearrange("p (a b) -> p a b", a=4), [bPS[b]], [bXT[s]])

    pending_tr = []

    def flush_pending():
        while pending_tr:
            transposes_tile(pending_tr.pop(0))

    def load_gb(i):
        kb.dma("sp", [(GB[:, 0, :], lng_d[i][0].partition_broadcast(128)),
                      (GB[:, 1, :], lnb_d[i][0].partition_broadcast(128))], writes=[bGB])

    def ln(s, src, src_bufs, scal, epsap, junk, junk_buf):
        sl = s % 2
        xs = X[:, s, :]
        kb.op("dve", lambda e: e.scalar_tensor_tensor(out=xs, in0=xs, scalar=scal, in1=src, op0=ALU.mult,
                                                      op1=ALU.add, accum_out=LNS[sl][:, 0:1]),
              reads=src_bufs, writes=[bX[s], bST[sl]])
        kb.op("act", lambda e: e.activation(out=junk, in_=xs, func=AF.Square, accum_out=LNA[sl][:, 0:1]),
              reads=[bX[s]], writes=[junk_buf, bSTA[sl]])
        ts("dve", LNS[sl][:, 1:2], LNS[sl][:, 0:1], -1.0 / 1024, None, ALU.mult, None, [bST[sl]], [bST[sl]])
        tt("dve", LNS[sl][:, 2:3], LNS[sl][:, 1:2], LNS[sl][:, 1:2], ALU.mult, [bST[sl]], [bST[sl]])
        stt(LNS[sl][:, 3:4], LNA[sl][:, 0:1], 1.0 / 1024, LNS[sl][:, 2:3], ALU.mult, ALU.subtract,
            [bST[sl], bSTA[sl]], [bST[sl]])
        act(LNA[sl][:, 1:2], LNS[sl][:, 3:4], AF.Ln, [bST[sl], bMISC], [bSTA[sl]], bias=epsap)
        act(LNA[sl][:, 2:3], LNA[sl][:, 1:2], AF.Exp, [bSTA[sl]], [bSTA[sl]], scale=-0.5)
        act(LNA[sl][:, 3:4], LNS[sl][:, 1:2], AF.Identity, [bST[sl], bSTA[sl]], [bSTA[sl]],
            scale=LNA[sl][:, 2:3])
        act(xs, xs, AF.Identity, [bSTA[sl]], [bX[s]], bias=LNA[sl][:, 3:4], scale=LNA[sl][:, 2:3])
        tt("pool", xs, xs, GB[:, 0, :], ALU.mult, [bGB], [bX[s]])
        tt("pool", xs, xs, GB[:, 1, :], ALU.add, [bGB], [bX[s]])

    bSM = Buf()
    GBs = view(OFF["GB"], F32, [6, 16, 16])
    PN = view(OFF["R2"], F32, [16, 2, 128])
    QN = view(OFF["R2"] + 16 * KBY, F32, [16, 2, 128])
    TA = view(OFF["R2"] + 32 * KBY, F32, [16, 128])
    TB = view(OFF["S5C"], F32, [16, 128])
    TMASK = view(OFF["R2"] + 40 * KBY, F32, [512])

    def T(i):
        return SM[:, i, :]

    def s5_prologue_math():
        e = "dve"
        (LR, LI, LS, DT, AA, TH, MAG, ZR, ZI, T1, T2, T3, T4,
         AR, DEN, WR, WI, NR, NI, IR, II) = [T(i) for i in range(21)]
        pairs = []
        for par in range(2):
            sl = slice(par * 64, (par + 1) * 64)
            pairs += [(LR[sl, :], lre_d[par:32:2, :].rearrange("g n -> n g")),
                      (LI[sl, :], lim_d[par:32:2, :].rearrange("g n -> n g")),
                      (LS[sl, :], lst_d[0, par:32:2].partition_broadcast(64))]
        kb.dma("sp", pairs, writes=[bSM], slow=True, eager=True)
        pairs = []
        for par in range(2):
            sl = slice(par * 64, (par + 1) * 64)
            pairs += [(GBs[sl, 0, :, :], bre_d[par:32:2].rearrange("g n j -> n g j")),
                      (GBs[sl, 1, :, :], bim_d[par:32:2].rearrange("g n j -> n g j"))]
            for gp in range(16):
                pairs += [(GBs[sl, 2, gp, :], cre_d[2 * gp + par].rearrange("i n -> n i")),
                          (GBs[sl, 3, gp, :], cim_d[2 * gp + par].rearrange("i n -> n i"))]
        for s in range(8):
            pairs.append((DSK[s * 16:(s + 1) * 16, :], dsk_d.rearrange("g j -> j g")))
        for k_ in range(3):
            pairs.append((CW[:, :, k_], cw_d[k_].rearrange("(ct p) -> p ct", p=128)))
        pairs.append((CW[:, :, 3], cb_d[0].rearrange("(ct p) -> p ct", p=128)))
        kb.dma("sp", pairs, writes=[bPRO], slow=True, eager=True)
        R, W = [bSM, bMISC], [bSM]
        act(DT, LS, AF.Exp, R, W)
        tt(e, AA, LR, DT, ALU.mult, R, W, mark=False)
        tt(e, TH, LI, DT, ALU.mult, R, W, mark=True)
        act(MAG, AA, AF.Exp, R, W, scale=1.0 / 32)
        act(ZI, TH, AF.Sin, R, W, scale=1.0 / 32)
        act(ZR, TH, AF.Sin, R, W, scale=1.0 / 32, bias=HALFPI)
        tt(e, ZR, ZR, MAG, ALU.mult, R, W, mark=False)
        tt(e, ZI, ZI, MAG, ALU.mult, R, W, mark=False)
        for _ in range(5):
            tt(e, T1, ZR, ZR, ALU.mult, R, W, mark=False)
            tt(e, T2, ZI, ZI, ALU.mult, R, W, mark=False)
            tt(e, T3, ZR, ZI, ALU.mult, R, W, mark=False)
            tt(e, ZR, T1, T2, ALU.subtract, R, W, mark=False)
            ts(e, ZI, T3, 2.0, None, ALU.mult, None, R, W, mark=False)
        kb.op(e, lambda en: en.memset(PWR[:, 0, :, 7], 1.0), reads=R, writes=W, mark=False)
        kb.op(e, lambda en: en.memset(PWR[:, 1, :, 7], 0.0), reads=R, writes=W, mark=False)
        copy_op(e, PWF[:, 0, :, 0], ZR, R, W, mark=False)
        copy_op(e, PWF[:, 1, :, 0], ZI, R, W, mark=False)
        for k in range(2, 9):
            pr, pi = PWF[:, 0, :, k - 2], PWF[:, 1, :, k - 2]
            nr, ni = PWF[:, 0, :, k - 1], PWF[:, 1, :, k - 1]
            tt(e, T1, pr, ZR, ALU.mult, R, W, mark=False)
            tt(e, T2, pi, ZI, ALU.mult, R, W, mark=False)
            tt(e, nr, T1, T2, ALU.subtract, R, W, mark=False)
            tt(e, T3, pr, ZI, ALU.mult, R, W, mark=False)
            tt(e, T4, pi, ZR, ALU.mult, R, W, mark=False)
            tt(e, ni, T3, T4, ALU.add, R, W, mark=False)
        for k in range(1, 8):
            copy_op(e, PWR[:, 0, :, 7 - k], PWF[:, 0, :, k - 1], R, W, mark=False)
            copy_op(e, PWR[:, 1, :, 7 - k], PWF[:, 1, :, k - 1], R, W, mark=False)
        ts(e, AR, ZR, -1.0, None, ALU.add, None, R, W, mark=False)
        tt(e, T1, LR, LR, ALU.mult, R, W, mark=False)
        tt(e, T2, LI, LI, ALU.mult, R, W, mark=False)
        tt(e, DEN, T1, T2, ALU.add, R, W, mark=False)
        kb.op(e, lambda en: en.reciprocal(DEN, DEN), reads=R, writes=W, mark=False)
        tt(e, T1, AR, LR, ALU.mult, R, W, mark=False)
        tt(e, T2, ZI, LI, ALU.mult, R, W, mark=False)
        tt(e, T1, T1, T2, ALU.add, R, W, mark=False)
        tt(e, WR, T1, DEN, ALU.mult, R, W, mark=False)
        tt(e, T1, ZI, LR, ALU.mult, R, W, mark=False)
        tt(e, T2, AR, LI, ALU.mult, R, W, mark=False)
        tt(e, T1, T1, T2, ALU.subtract, R, W, mark=False)
        tt(e, WI, T1, DEN, ALU.mult, R, W, mark=False)
        L8r, L8i = PWF[:, 0, :, 7], PWF[:, 1, :, 7]
        tt(e, T1, L8r, L8r, ALU.mult, R, W, mark=False)
        tt(e, T2, L8i, L8i, ALU.mult, R, W, mark=False)
        tt(e, T1, T1, T2, ALU.add, R, W, mark=False)
        kb.op(e, lambda en: en.reciprocal(T1, T1), reads=R, writes=W, mark=False)
        tt(e, IR, L8r, T1, ALU.mult, R, W, mark=False)
        stt(II, L8i, -1.0, T1, ALU.mult, ALU.mult, R, W, mark=False)
        copy_op(e, A1[:, :, 0], L8r, R, W, mark=False)
        copy_op(e, A1[:, :, 1], L8r, R, W, mark=False)
        ts(e, A2[:, :, 0], L8i, -1.0, None, ALU.mult, None, R, W, mark=False)
        copy_op(e, A2[:, :, 1], L8i, R, W, mark=True)
        copy_op(e, PW16[:, 0, :, 0], L8r, R, W)
        copy_op(e, PW16[:, 1, :, 0], L8i, R, W)
        for k in range(1, 16):
            pr, pi = PW16[:, 0, :, k - 1], PW16[:, 1, :, k - 1]
            nr, ni = PW16[:, 0, :, k], PW16[:, 1, :, k]
            tt(e, T1, pr, L8r, ALU.mult, R, W)
            tt(e, T2, pi, L8i, ALU.mult, R, W)
            tt(e, nr, T1, T2, ALU.subtract, R, W)
            tt(e, T3, pr, L8i, ALU.mult, R, W)
            tt(e, T4, pi, L8r, ALU.mult, R, W)
            tt(e, ni, T3, T4, ALU.add, R, W)
        copy_op(e, A1C[:, :, 0], PW16[:, 0, :, 15], R, W)
        copy_op(e, A1C[:, :, 1], PW16[:, 0, :, 15], R, W)
        ts(e, A2C[:, :, 0], PW16[:, 1, :, 15], -1.0, None, ALU.mult, None, R, W)
        copy_op(e, A2C[:, :, 1], PW16[:, 1, :, 15], R, W)
        KR, KI = T(17), T(18)
        for r in range(4):
            if r == 0:
                copy_op(e, KR, PW16[:, 0, :, 7], R, W)
                copy_op(e, KI, PW16[:, 1, :, 7], R, W)
            elif r == 1:
                copy_op(e, KR, PW16[:, 0, :, 15], R, W)
                copy_op(e, KI, PW16[:, 1, :, 15], R, W)
            else:
                tt(e, T1, KR, KR, ALU.mult, R, W)
                tt(e, T2, KI, KI, ALU.mult, R, W)
                tt(e, T3, KR, KI, ALU.mult, R, W)
                tt(e, KR, T1, T2, ALU.subtract, R, W)
                ts(e, KI, T3, 2.0, None, ALU.mult, None, R, W)
            copy_op(e, KA1[r][:, :, 0], KR, R, W)
            copy_op(e, KA1[r][:, :, 1], KR, R, W)
            ts(e, KA2[r][:, :, 0], KI, -1.0, None, ALU.mult, None, R, W)
            copy_op(e, KA2[r][:, :, 1], KI, R, W)
        R2_, W2_ = [bSM, bPRO], [bPRO]
        Bre, Bim, Cre, Cim, BBr, BBi = [GBs[:, i, :, :] for i in range(6)]
        wrb = WR.unsqueeze(2).broadcast_to([128, 16, 16])
        wib = WI.unsqueeze(2).broadcast_to([128, 16, 16])
        TAs = TA[:, :, 0:16]
        tt(e, BBr, Bre, wrb, ALU.mult, R2_, W2_, mark=False)
        tt(e, TAs, Bim, wib, ALU.mult, R2_, W2_, mark=False)
        tt(e, BBr, BBr, TAs, ALU.subtract, R2_, W2_, mark=False)
        tt(e, BBi, Bim, wrb, ALU.mult, R2_, W2_, mark=False)
        tt(e, TAs, Bre, wib, ALU.mult, R2_, W2_, mark=False)
        tt(e, BBi, BBi, TAs, ALU.add, R2_, W2_, mark=False)

        def v4(ap):
            return ap.rearrange("p g (s j) -> p g s j", s=8)

        def pw_b(pw, ri):
            return pw[:, ri, :, :].unsqueeze(3).broadcast_to([128, 16, 8, 16])

        def bc_b(x):
            return x.unsqueeze(2).broadcast_to([128, 16, 8, 16])

        PNr, PNi = v4(PN[:, :, 0, :]), v4(PN[:, :, 1, :])
        QNr, QNi = v4(QN[:, :, 0, :]), v4(QN[:, :, 1, :])
        TA4, TB4 = v4(TA), v4(TB)
        R3_, W3_ = [bSM, bPRO, bS5C], [bPRO, bS5C]
        tt(e, PNr, pw_b(PWR, 0), bc_b(BBr), ALU.mult, R3_, W3_, mark=False)
        tt(e, TA4, pw_b(PWR, 1), bc_b(BBi), ALU.mult, R3_, W3_, mark=False)
        tt(e, PNr, PNr, TA4, ALU.subtract, R3_, W3_, mark=False)
        tt(e, PNi, pw_b(PWR, 0), bc_b(BBi), ALU.mult, R3_, W3_, mark=False)
        tt(e, TA4, pw_b(PWR, 1), bc_b(BBr), ALU.mult, R3_, W3_, mark=False)
        tt(e, PNi, PNi, TA4, ALU.add, R3_, W3_, mark=False)
        tt(e, QNr, pw_b(PWF, 0), bc_b(Cre), ALU.mult, R3_, W3_, mark=False)
        tt(e, TA4, pw_b(PWF, 1), bc_b(Cim), ALU.mult, R3_, W3_, mark=False)
        tt(e, QNr, QNr, TA4, ALU.subtract, R3_, W3_, mark=False)
        tt(e, QNi, pw_b(PWF, 1), bc_b(Cre), ALU.mult, R3_, W3_, mark=False)
        tt(e, TA4, pw_b(PWF, 0), bc_b(Cim), ALU.mult, R3_, W3_, mark=False)
        stt(QNi, QNi, -1.0, TA4, ALU.mult, ALU.subtract, R3_, W3_, mark=True)
        QB = view(OFF["S5C"], BF16, [16, 2, 128])
        copy_op("act", QB, QN, R3_, W3_)
        for par in range(2):
            for ri in range(2):
                kb.dma("sp", [(QT[ri * 64:(ri + 1) * 64, par:32:2, :], QB[par * 64:(par + 1) * 64, :, ri, :])],
                       reads=[bPRO], writes=[bS5C])

    def s5_prologue_pe():
        e = "dve"
        R3_, W3_ = [bSM, bPRO, bS5C], [bPRO, bS5C]
        IR, II = T(19), T(20)
        parts = int(os.environ.get("KDBG_HOOKPARTS", "7"))
        for bk in range(8 if parts & 1 else 0):
            b = nextA()
            for gpl in range(2):
                gp = 2 * bk + gpl
                for ri in range(2):
                    c0 = (gpl * 2 + ri) * 128
                    kb.op("pe", lambda en: en.transpose(PS[b][:, c0:c0 + 128], PN[:, gp, ri, :], IDF),
                          reads=[bPRO, bMISC], writes=[bPS[b]], mark=(gpl == 1 and ri == 1))
            for gpl in range(2):
                gp = 2 * bk + gpl
                copy_op("act", PT[:, 2 * gp:2 * gp + 2, :, :],
                        PS[b][:, gpl * 256:(gpl + 1) * 256].rearrange("p (ri par n) -> p par ri n", ri=2, par=2),
                        [bPS[b]], [bS5C])
        if not parts & 2:
            return
        irb = IR.unsqueeze(2).broadcast_to([128, 16, 128])
        iib = II.unsqueeze(2).broadcast_to([128, 16, 128])
        Pr, Pi = PN[:, :, 0, :], PN[:, :, 1, :]
        tt(e, TA, Pr, irb, ALU.mult, R3_, W3_, mark=False)
        tt(e, TB, Pi, iib, ALU.mult, R3_, W3_, mark=False)
        tt(e, TA, TA, TB, ALU.subtract, R3_, W3_, mark=False)
        tt(e, TB, Pi, irb, ALU.mult, R3_, W3_, mark=False)
        tt(e, Pr, Pr, iib, ALU.mult, R3_, W3_, mark=False)
        tt(e, Pi, TB, Pr, ALU.add, R3_, W3_, mark=True)
        PPB = view(OFF["R2"] + 16 * KBY, BF16, [16, 2, 128])
        copy_op("act", PPB[:, :, 0, :], TA, R3_, W3_)
        copy_op("act", PPB[:, :, 1, :], Pi, R3_, W3_)
        PPS = view(OFF["R2"] + 32 * KBY, BF16, [32, 128])
        for par in range(2):
            for ri in range(2):
                kb.dma("sp", [(PPS[ri * 64:(ri + 1) * 64, par:32:2, :], PPB[par * 64:(par + 1) * 64, :, ri, :])],
                       reads=[bS5C], writes=[bPRO])
        if not parts & 4:
            return
        for bk in range(8):
            b = nextA()
            for gl in range(4):
                g = 4 * bk + gl
                gp, par = g // 2, g % 2
                sl = slice(par * 64, (par + 1) * 64)
                o = PS[b][:, gl * 128:(gl + 1) * 128]
                mm(o, PPS[:, g, :], QT[:, g, :], True, True, [bPRO, bS5C], [bPS[b]], gl == 3)
            tt(e, TMASK, PS[b][:], MASK4, ALU.mult, [bPS[b], bMISC], [bPRO], mark=False)
            for gl in range(4):
                g = 4 * bk + gl
                stt(TOEP[:, g, :], IDF, DSK[:, g:g + 1], TMASK[:, gl * 128:(gl + 1) * 128],
                    ALU.mult, ALU.add, [bPRO, bMISC], [bS5C], mark=(gl == 3))

    def ffn(wi_d, wo_d, ln_idx, hook=None):
        if hook is None:
            load_gb(ln_idx)
        wo_v = wo_d.rearrange("(kt p) n -> p kt n", p=128)
        wi_v = wi_d.rearrange("(kt p) c -> p kt c", p=128)
        slots = {}

        def load_w(j):
            sl = next_slot()
            slots[j] = sl
            wsv = WS[sl].rearrange("p (k c) -> p k c", k=8)
            kb.dma("pool", [(wsv[:, :, 0:256], wi_v[:, :, 256 * j:256 * j + 256]),
                            (wsv[:, :, 256:512], wi_v[:, :, DFF + 256 * j:DFF + 256 * j + 256])],
                   writes=[bWS[sl]])

        for j in range(3):
            load_w(j)
        def load_wout():
            kb.handoff(r2_users, bWOUT)
            for ci, (a, b_) in enumerate(WCH):
                kb.dma("pool", [(WOUT[:, a:b_, :], wo_v[:, a:b_, :])], writes=[bWOUT[ci]])

        if hook is None:
            load_wout()
        kb.handoff(r1_users, [bH[f][th] for f in range(NF) for th in range(2)])
        for j in range(11):
            sl = slots[j]
            wsv = WS[sl].rearrange("p (k c) -> p k c", k=8)
            for fl in range(2):
                f = 2 * j + fl
                for th in range(2):
                    ba, bb = nextA(), nextA()
                    xr = [bXT[4 * th + i] for i in range(4)]
                    for k in range(8):
                        mm(PS[ba][:], wsv[:, k, fl * 128:(fl + 1) * 128], XT[:, k, th * 512:(th + 1) * 512],
                           k == 0, k == 7, [bWS[sl]] + xr, [bPS[ba]], k == 7)
                    for k in range(8):
                        mm(PS[bb][:], wsv[:, k, 256 + fl * 128:256 + (fl + 1) * 128],
                           XT[:, k, th * 512:(th + 1) * 512],
                           k == 0, k == 7, [bWS[sl]] + xr, [bPS[bb]], k == 7)
                    ti = next_tmp()
                    act(TMPA[ti], PS[ba][:], AF.Silu, [bPS[ba]], [bTMPA[ti]])
                    tt("dve", H[:, f, th * 512:(th + 1) * 512], TMPA[ti], PS[bb][:], ALU.mult,
                       [bTMPA[ti], bPS[bb]], [bH[f][th]])
                    flush_pending()
                    if hook is not None:
                        kb.run_deferred(DEFER_RATE)
            if j + 3 < 11:
                load_w(j + 3)
            if hook is not None and j == HOOK_J:
                kb.run_deferred(100000)
                hook()
                load_wout()
                kb.handoff([bPRO], [bGB])
                load_gb(ln_idx)
        hs = [bH[f][th] for f in range(NF) for th in range(2)]
        for s in range(8):
            b0 = nextBpair()
            banks = [b0, b0 + 1]
            for hf in range(2):
                for k in range(NF):
                    mm(PS[banks[hf]][:], H[:, k, s * 128:(s + 1) * 128], WOUT[:, k, hf * 512:(hf + 1) * 512],
                       k == 0, k == NF - 1, [bH[k][s // 4], bWOUT[chunk_of(k)]], [bPS[banks[hf]]],
                       k == NF - 1)
            ln(s, PSALL[:, 512 * b0:512 * b0 + 1024], [bPS[b0], bPS[b0 + 1]], 2 * ALPHA, EPS4,
               TMPA[0].bitcast(BF16), bTMPA[0])
            if s > 1:
                transposes_tile(s - 2)
        transposes_tile(6)
        transposes_tile(7)
        return hs + bWOUT

    r1_users = []
    r2_users = [bPRO]

    def dump_and_finish(m):
        store_x(m)

    R1o, R2o = OFF["R1"], OFF["R2"]
    SU = view(R1o, F32, [8, 512])
    SU4 = view(R1o, F32, [32, 8, 16])
    EE_ = view(R1o, F32, [16, 2, 128])
    SSTK = view(R1o, BF16, [32, 128])
    MG = view(R1o, BF16, [8, 1024])
    VV = view(R1o + 16 * KBY, F32, [4, 8, 129])
    SS = view(R1o + 16 * KBY, F32, [16, 2, 129])
    MT = [view(R1o + 16 * KBY + 2 * KBY * i, F32, [512]) for i in range(6)]
    CBZ = view(R1o + 33 * KBY, BF16, [4, 1024])
    SCT8 = view(R1o + 41 * KBY, F32, [2, 16, 2, 8])
    UT = view(R2o, BF16, [32, 128])
    ST_ = view(R2o, BF16, [4, 1024])
    SBF = view(R2o + 8 * KBY, BF16, [16, 2, 128])
    ZZ = view(R2o, F32, [4, 1024])
    STM = view(R2o + 16 * KBY, BF16, [8, 512])
    WMO = view(R2o + 28 * KBY, BF16, [8, 1024])

    def mixer(m):
        mwi_v = mwi_d.rearrange("(kt p) c -> p kt c", p=128)
        glu_v = glu_d.rearrange("(kt p) c -> p kt c", p=128)
        cwo_v = cwo_d.rearrange("(kt p) c -> p kt c", p=128)
        mwo_v = mwo_d.rearrange("(kt p) c -> p kt c", p=128)
        load_gb(1)
        bSU, bE, bMG, bV, bS, bCBZ, bUT, bSBF, bZ, bSTM, bSTT, bWMO = [Buf() for _ in range(12)]
        bMT = [Buf() for _ in range(6)]
        r1_new = [bSU, bV, bCBZ]
        r2_new = [bZ, bSTM, bWMO]
        kb.handoff(r1_users, r1_new)
        kb.handoff(r2_users, r2_new)

        def load_slot(pairs_fn):
            sl = next_slot()
            kb.dma("pool", pairs_fn(sl), writes=[bWS[sl]])
            return sl

        def w8(sl):
            return WS[sl].rearrange("p (k c) -> p k c", k=8)

        sl_cc = [load_slot(lambda sl, q=q: [(w8(sl)[:, :, 0:256], mwi_v[:, :, 512 + 256 * q:768 + 256 * q]),
                                            (w8(sl)[:, :, 256:512], mwi_v[:, :, 1024 + 256 * q:1280 + 256 * q])])
                 for q in range(2)]
        sl_su = load_slot(lambda sl: [(w8(sl), mwi_v[:, :, 1536:2048])])
        kb.dma("pool", [(WMO, mwo_v)], writes=[bWMO])
        for ct in range(4):
            sl = sl_cc[ct // 2]
            ctl = ct % 2
            for th in range(2):
                ba, bb = nextA(), nextA()
                xr = [bXT[4 * th + i] for i in range(4)]
                for k in range(8):
                    mm(PS[ba][:], w8(sl)[:, k, ctl * 128:(ctl + 1) * 128], XT[:, k, th * 512:(th + 1) * 512],
                       k == 0, k == 7, [bWS[sl]] + xr, [bPS[ba]], k == 7)
                for k in range(8):
                    mm(PS[bb][:], w8(sl)[:, k, 256 + ctl * 128:256 + (ctl + 1) * 128],
                       XT[:, k, th * 512:(th + 1) * 512],
                       k == 0, k == 7, [bWS[sl]] + xr, [bPS[bb]], k == 7)
                ti = next_tmp()
                copy_op("act", TMPA[ti], PS[bb][:], [bPS[bb]], [bTMPA[ti]])
                tt("dve", VV[:, ct, 4 * th:4 * th + 4, 1:129], PS[ba][:].rearrange("p (a b) -> p a b", a=4),
                   TMPA[ti].rearrange("p (a b) -> p a b", a=4), ALU.mult,
                   [bPS[ba], bTMPA[ti]], [bV])
                flush_pending()
        sl_cb = load_slot(lambda sl: [(w8(sl), mwi_v[:, :, 0:512])])
        for s in range(8):
            b = nextB()
            for k in range(8):
                mm(PS[b][:], XT[:, k, s * 128:(s + 1) * 128], w8(sl_su)[:, k, :], k == 0, k == 7,
                   [bWS[sl_su], bXT[s]], [bPS[b]], k == 7)
            copy_op(ev_eng(), SU4[:, :, s, :], PS[b][:].rearrange("p (g j) -> p g j", g=32), [bPS[b]], [bSU])
        for ct in range(4):
            copy_op("dve", VV[:, ct, :, 0], VH[:, ct, :], [bVH], [bV], mark=False)
            zv = ZZ[:, ct, :].rearrange("p (a b) -> p a b", a=8)
            act(zv, VV[:, ct, :, 1:129], AF.Identity, [bV, bPRO], [bZ], bias=CW[:, ct, 3:4],
                scale=CW[:, ct, 2:3])
            w1, w0 = CW[:, ct, 1:2], CW[:, ct, 0:1]
            stt(zv[:, 1:8, :], VV[:, ct, 0:7, 1:129], w1, zv[:, 1:8, :], ALU.mult, ALU.add, [bV, bPRO], [bZ], False)
            stt(zv[:, 0, :], VV[:, ct, 7, 0:128], w1, zv[:, 0, :], ALU.mult, ALU.add, [bV, bPRO], [bZ], False)
            stt(zv[:, 2:8, :], VV[:, ct, 0:6, 1:129], w0, zv[:, 2:8, :], ALU.mult, ALU.add, [bV, bPRO], [bZ], False)
            stt(zv[:, 0:2, :], VV[:, ct, 6:8, 0:128], w0, zv[:, 0:2, :], ALU.mult, ALU.add, [bV, bPRO], [bZ], False)
            copy_op("dve", VH[:, ct, :], VV[:, ct, :, 128], [bV], [bVH], mark=True)
        for ct in range(4):
            for th in range(2):
                b = nextA()
                xr = [bXT[4 * th + i] for i in range(4)]
                for k in range(8):
                    mm(PS[b][:], w8(sl_cb)[:, k, ct * 128:(ct + 1) * 128], XT[:, k, th * 512:(th + 1) * 512],
                       k == 0, k == 7, [bWS[sl_cb]] + xr, [bPS[b]], k == 7)
                tt("dve", CBZ[:, ct, th * 512:(th + 1) * 512], PS[b][:], ZZ[:, ct, th * 512:(th + 1) * 512],
                   ALU.mult, [bPS[b], bZ], [bCBZ])
        slots_ft = {}

        def issue_bundle(ft):
            def bundle(sl, ft=ft):
                w = WS[sl]
                return [(w[:, 0:1024].rearrange("p (k c) -> p k c", k=8), mwi_v[:, :, 2048 + 128 * ft:2176 + 128 * ft]),
                        (w[:, 1024:2048].rearrange("p (k c) -> p k c", k=8), mwi_v[:, :, 3072 + 128 * ft:3200 + 128 * ft]),
                        (w[:, 2048:2560].rearrange("p (k c) -> p k c", k=4), glu_v[:, :, 128 * ft:128 * ft + 128]),
                        (w[:, 2560:3072].rearrange("p (k c) -> p k c", k=4), glu_v[:, :, 1024 + 128 * ft:1152 + 128 * ft]),
                        (w[:, 3072:3584].rearrange("p (k c) -> p k c", k=4), cwo_v[:, :, 128 * ft:128 * ft + 128])]
            slots_ft[ft] = load_slot(bundle)

        for ft in range(3):
            issue_bundle(ft)
        kb.handoff([bZ], [bUT, bSBF])
        for gb_ in range(8):
            b = nextB()
            for gl in range(4):
                g = 4 * gb_ + gl
                kb.op("pe", lambda e: e.transpose(PS[b][:, gl * 128:(gl + 1) * 128],
                                                  SU4[:, g, :, :].rearrange("p s j -> p (s j)"), IDF),
                      reads=[bSU, bMISC], writes=[bPS[b]], mark=(gl == 3))
            copy_op(ev_eng(), UT[:, 4 * gb_:4 * gb_ + 4, :].rearrange("p a b -> p (a b)"), PS[b][:],
                    [bPS[b]], [bUT])
        kb.handoff([bSU], [bE])
        kb.handoff([bV], [bS])
        for bk in range(8):
            b = nextA()
            for gpl in range(2):
                gp = 2 * bk + gpl
                for par in range(2):
                    g = 2 * gp + par
                    for ri in range(2):
                        c0 = gpl * 256 + ri * 128
                        mm(PS[b][par * 64:(par + 1) * 64, c0:c0 + 128], PT[:, g, ri, :], UT[:, g, :],
                           True, True, [bS5C, bUT], [bPS[b]], (gpl == 1 and par == 1 and ri == 1))
            copy_op("act", EE_[:, 2 * bk:2 * bk + 2, :, :].rearrange("p a b c -> p (a b c)"), PS[b][:],
                    [bPS[b]], [bE])
        def bv(ap4, off):
            return ap4[:, :, :, off:off + 121:8]

        SCT = view(R2o + 16 * KBY, F32, [2, 16, 2, 16])
        T1_, T2_ = SCT[:, 0, :, :, :], SCT[:, 1, :, :, :]
        CB = view(R2o + 8 * KBY, F32, [16, 2, 17])
        VA = view(R2o + 8 * KBY + 2304, F32, [16, 2, 16])
        VB = view(R2o + 8 * KBY + 2304 + 2048, F32, [16, 2, 16])
        A1b = A1.unsqueeze(3).broadcast_to([128, 16, 2, 16])
        A2b0 = A2[:, :, 0:1].broadcast_to([128, 16, 16])
        A2b1 = A2[:, :, 1:2].broadcast_to([128, 16, 16])
        WS_ = [bS, bSTM]
        copy_op("dve", SS[:, :, :, 0], SC, [bSC, bSM], [bS])
        copy_op("dve", bv(SS, 1), bv(EE_, 0), [bE], [bS])
        for k in range(1, 8):
            prev, cur, ek = bv(SS, k), bv(SS, k + 1), bv(EE_, k)
            tt("dve", T1_, A1b, prev, ALU.mult, [bE], WS_)
            tt("dve", T2_[:, :, 0, :], A2b0, prev[:, :, 1, :], ALU.mult, [], WS_)
            tt("dve", T2_[:, :, 1, :], A2b1, prev[:, :, 0, :], ALU.mult, [], WS_)
            tt("dve", T1_, T1_, T2_, ALU.add, [], WS_)
            tt("dve", cur, T1_, ek, ALU.add, [bE], WS_)
        WC_ = [bS, bSTM, bSBF]
        lend = SS[:, :, :, 8:129:8]
        copy_op("dve", VA, lend, [bS], WC_)
        copy_op("dve", CB[:, :, :, 0], SC, [bSC], WC_)
        t1s, t2s = T1_[:, :, :, 0], T2_[:, :, :, 0]
        tt("dve", t1s, KA1[0], SC, ALU.mult, [bSM, bSC], WC_)
        tt("dve", t2s[:, :, 0], KA2[0][:, :, 0], SC[:, :, 1], ALU.mult, [], WC_)
        tt("dve", t2s[:, :, 1], KA2[0][:, :, 1], SC[:, :, 0], ALU.mult, [], WC_)
        tt("dve", t1s, t1s, t2s, ALU.add, [], WC_)
        tt("dve", VA[:, :, :, 0], VA[:, :, :, 0], t1s, ALU.add, [], WC_)
        src, dst = VA, VB
        for r in range(4):
            d = 1 << r
            n = 16 - d
            if r == 3:
                dst = CB[:, :, :, 1:17]
            ka1 = KA1[r].unsqueeze(3).broadcast_to([128, 16, 2, n])
            ka20 = KA2[r][:, :, 0:1].broadcast_to([128, 16, n])
            ka21 = KA2[r][:, :, 1:2].broadcast_to([128, 16, n])
            tt("dve", T1_[:, :, :, 0:n], ka1, src[:, :, :, 0:n], ALU.mult, [bSM], WC_)
            tt("dve", T2_[:, :, 0, 0:n], ka20, src[:, :, 1, 0:n], ALU.mult, [], WC_)
            tt("dve", T2_[:, :, 1, 0:n], ka21, src[:, :, 0, 0:n], ALU.mult, [], WC_)
            tt("dve", T1_[:, :, :, 0:n], T1_[:, :, :, 0:n], T2_[:, :, :, 0:n], ALU.add, [], WC_)
            tt("dve", dst[:, :, :, d:16], src[:, :, :, d:16], T1_[:, :, :, 0:n], ALU.add, [], WC_)
            copy_op("dve", dst[:, :, :, 0:d], src[:, :, :, 0:d], [], WC_)
            src, dst = dst, (VA if dst is VB else VB)
        bFX = [Buf(), Buf()]
        bS2 = Buf()
        kb.handoff([bE], bFX)
        kb.handoff([bS], [bS2])
        for hh, en_ in ((0, "dve"), (1, "pool")):
            gsl = slice(8 * hh, 8 * hh + 8)
            TF1 = view(R1o + 8 * KBY * hh, F32, [8, 16, 8])
            TF2 = view(R1o + 8 * KBY * hh + 4 * KBY, F32, [8, 16, 8])
            pwr = PW16[:, 0, gsl, 0:8].unsqueeze(2).broadcast_to([128, 8, 16, 8])
            pwi = PW16[:, 1, gsl, 0:8].unsqueeze(2).broadcast_to([128, 8, 16, 8])
            cbr = CB[:, gsl, 0, 0:16].unsqueeze(3).broadcast_to([128, 8, 16, 8])
            cbi = CB[:, gsl, 1, 0:16].unsqueeze(3).broadcast_to([128, 8, 16, 8])
            sre = SS[:, gsl, 0, 1:129].rearrange("p g (b k) -> p g b k", b=16)
            sim = SS[:, gsl, 1, 1:129].rearrange("p g (b k) -> p g b k", b=16)
            bSh = bS if hh == 0 else bS2
            Rr, Ww = [bFX[hh], bSh, bSBF, bSM], [bFX[hh], bSh]
            tt(en_, TF1, pwr, cbr, ALU.mult, Rr, Ww)
            tt(en_, TF2, pwi, cbi, ALU.mult, Rr, Ww)
            tt(en_, TF1, TF1, TF2, ALU.subtract, Rr, Ww)
            tt(en_, sre, sre, TF1, ALU.add, Rr, Ww)
            tt(en_, TF1, pwr, cbi, ALU.mult, Rr, Ww)
            tt(en_, TF2, pwi, cbr, ALU.mult, Rr, Ww)
            tt(en_, TF1, TF1, TF2, ALU.add, Rr, Ww)
            tt(en_, sim, sim, TF1, ALU.add, Rr, Ww)
        copy_op("dve", SC, CB[:, :, :, 16], [bSBF], [bSC])
        copy_op("act", SBF, SS[:, :, :, 0:128], [bS, bS2], [bSBF])
        bSSTK = Buf()
        kb.handoff([bE] + bFX, [bSSTK])
        for par in range(2):
            for ri in range(2):
                kb.dma("sp", [(SSTK[ri * 64:(ri + 1) * 64, par:32:2, :], SBF[par * 64:(par + 1) * 64, :, ri, :])],
                       reads=[bSBF], writes=[bSSTK])
        for gb_ in range(8):
            b = nextB()
            for gl in range(4):
                g = 4 * gb_ + gl
                gp, par = g // 2, g % 2
                sl = slice(par * 64, (par + 1) * 64)
                o = PS[b][:, gl * 128:(gl + 1) * 128]
                mm(o, UT[:, g, :], TOEP[:, g, :], True, False, [bUT, bS5C], [bPS[b]], False)
                mm(o, SSTK[:, g, :], QT[:, g, :], False, True, [bSSTK, bS5C], [bPS[b]], gl == 3)
            act(STM[:, :, 64 * gb_:64 * gb_ + 64].rearrange("p t (g i) -> p g t i", g=4),
                PS[b][:].rearrange("p (g t i) -> p g t i", g=4, t=8), AF.Gelu_apprx_tanh,
                [bPS[b]], [bSTM])
        kb.handoff([bUT], [bSTT])
        for kt in range(4):
            b = nextA()
            psb = PS[b][:].bitcast(BF16)
            for t_ in range(8):
                kb.op("pe", lambda e: e.transpose(psb[:, t_ * 128:(t_ + 1) * 128],
                                                  STM[:, t_, kt * 128:(kt + 1) * 128], IDB),
                      reads=[bSTM, bMISC], writes=[bPS[b]], mark=(t_ == 7))
            copy_op(ev_eng(), ST_[:, kt, :], psb, [bPS[b]], [bSTT])
        kb.handoff([bSSTK], [bMG])
        kb.handoff([bS, bS2], bMT)
        for ft in range(8):
            sl = slots_ft[ft]
            w = WS[sl]
            wgc = w[:, 0:1024].rearrange("p (k c) -> p k c", k=8)
            wgs = w[:, 1024:2048].rearrange("p (k c) -> p k c", k=8)
            wga = w[:, 2048:2560].rearrange("p (k c) -> p k c", k=4)
            wgb = w[:, 2560:3072].rearrange("p (k c) -> p k c", k=4)
            wco = w[:, 3072:3584].rearrange("p (k c) -> p k c", k=4)
            for th in range(2):
                hs = slice(th * 512, (th + 1) * 512)
                xr = [bXT[4 * th + i] for i in range(4)]
                b1, b2, b3, b4, b5 = nextAll(), nextAll(), nextAll(), nextAll(), nextAll()
                for k in range(8):
                    mm(PS[b1][:], wgc[:, k, :], XT[:, k, hs], k == 0, k == 7, [bWS[sl]] + xr, [bPS[b1]], k == 7)
                for k in range(8):
                    mm(PS[b2][:], wgs[:, k, :], XT[:, k, hs], k == 0, k == 7, [bWS[sl]] + xr, [bPS[b2]], k == 7)
                for k in range(4):
                    mm(PS[b3][:], wgb[:, k, :], ST_[:, k, hs], k == 0, k == 3, [bWS[sl], bSTT], [bPS[b3]], k == 3)
                for k in range(4):
                    mm(PS[b4][:], wga[:, k, :], ST_[:, k, hs], k == 0, k == 3, [bWS[sl], bSTT], [bPS[b4]], k == 3)
                for k in range(4):
                    mm(PS[b5][:], wco[:, k, :], CBZ[:, k, hs], k == 0, k == 3, [bWS[sl], bCBZ], [bPS[b5]], k == 3)
                i0 = 3 * ((2 * ft + th) % 2)
                t1, t2, t3 = MT[i0], MT[i0 + 1], MT[i0 + 2]
                q1, q2, q3 = bMT[i0], bMT[i0 + 1], bMT[i0 + 2]
                act(t1, PS[b1][:], AF.Sigmoid, [bPS[b1]], [q1])
                act(t2, PS[b2][:], AF.Sigmoid, [bPS[b2]], [q2])
                act(t3, PS[b3][:], AF.Sigmoid, [bPS[b3]], [q3])
                tt("dve", t1, PS[b5][:], t1, ALU.mult, [bPS[b5]], [q1], mark=False)
                tt("dve", t3, PS[b4][:], t3, ALU.mult, [bPS[b4]], [q3], mark=False)
                tt("dve", t2, t3, t2, ALU.mult, [q3], [q2], mark=False)
                tt("dve", MG[:, ft, hs], t1, t2, ALU.add, [q1, q2], [bMG], mark=True)
            if ft + 3 < 8:
                issue_bundle(ft + 3)
        for s in range(8):
            b0 = nextBpair()
            banks = [b0, b0 + 1]
            for hf in range(2):
                for k in range(8):
                    mm(PS[banks[hf]][:], MG[:, k, s * 128:(s + 1) * 128], WMO[:, k, hf * 512:(hf + 1) * 512],
                       k == 0, k == 7, [bMG, bWMO], [bPS[banks[hf]]], k == 7)
            ln(s, PSALL[:, 512 * b0:512 * b0 + 1024], [bPS[b0], bPS[b0 + 1]], ALPHA, EPS1,
               TMPA[0].bitcast(BF16), bTMPA[0])
            if s > 2:
                transposes_tile(s - 3)
        transposes_tile(5)
        transposes_tile(6)
        transposes_tile(7)
        return ([bSU, bE, bSSTK, bMG, bV, bS, bS2, bCBZ] + bMT + bFX,
                [bUT, bSBF, bZ, bSTM, bSTT, bWMO])

    PSB = view(R1o, F32, [8, 256])
    PTT = view(R1o + 8 * KBY, BF16, [2, 1024])
    EEt = [view(R1o + 12 * KBY + 2 * KBY * i, F32, [512]) for i in range(4)]

    def ple(m):
        load_gb(3)
        bP, bPTT = Buf(), Buf()
        bEE = [Buf() for _ in range(4)]
        bJ = Buf()
        kb.handoff(r1_users, [bP, bPTT, bJ] + bEE)
        kb.dma("sp", [(PSB, p_v[m])], writes=[bP])
        pwg_v = pwg_d.rearrange("(kt p) c -> p kt c", p=128)
        pwi_v = pwi_d.rearrange("(kt p) c -> p kt c", p=128)
        sg = []
        for hf in range(2):
            sl = next_slot()
            kb.dma("pool", [(WS[sl].rearrange("p (k c) -> p k c", k=8), pwg_v[:, :, hf * 512:(hf + 1) * 512])],
                   writes=[bWS[sl]])
            sg.append(sl)
        sli = next_slot()
        wpi = WS[sli][:, 0:2048].rearrange("p (k c) -> p k c", k=2)
        kb.dma("pool", [(wpi, pwi_v)], writes=[bWS[sli]])
        for sp_ in range(4):
            b = nextA()
            for i in range(4):
                s, kt = 2 * sp_ + i // 2, i % 2
                kb.op("pe", lambda e: e.transpose(PS[b][:, i * 128:(i + 1) * 128],
                                                  PSB[:, s, kt * 128:(kt + 1) * 128], IDF),
                      reads=[bP, bMISC], writes=[bPS[b]], mark=(i == 3))
            copy_op(ev_eng(), PTT[:, :, 256 * sp_:256 * sp_ + 256].rearrange("p k (s c) -> p s k c", s=2),
                    PS[b][:].rearrange("p (s k c) -> p s k c", s=2, k=2), [bPS[b]], [bPTT])
        for sp2 in range(4):
            pair = []
            for s in (2 * sp2, 2 * sp2 + 1):
                b0 = nextBpair()
                pair.append((s, b0))
                for hf in range(2):
                    be, bg = b0 + hf, nextA()
                    w = WS[sg[hf]].rearrange("p (k c) -> p k c", k=8)
                    for kt in range(2):
                        mm(PS[be][:], PTT[:, kt, s * 128:(s + 1) * 128], wpi[:, kt, hf * 512:(hf + 1) * 512],
                           kt == 0, kt == 1, [bPTT, bWS[sli]], [bPS[be]], kt == 1)
                    for k in range(8):
                        mm(PS[bg][:], XT[:, k, s * 128:(s + 1) * 128], w[:, k, :], k == 0, k == 7,
                           [bXT[s], bWS[sg[hf]]], [bPS[bg]], k == 7)
                    ti = next_tmp()
                    act(TMPA[ti], PS[bg][:], AF.Sigmoid, [bPS[bg]], [bTMPA[ti]])
                    tt("dve", PS[be][:], PS[be][:], TMPA[ti], ALU.mult, [bTMPA[ti]], [bPS[be]])
            for (s, b0) in pair:
                ln(s, PSALL[:, 512 * b0:512 * b0 + 1024], [bPS[b0], bPS[b0 + 1]], ALPHA, EPS1,
                   view(R1o + 20 * KBY, BF16, [1024]), bJ)
                kb.dma("sp", [(o_v[m, :, s, :], X[:, s, :])], reads=[bX[s]])
                if m == 0:
                    kb.dma("sp", [(X[:, s, :], x_v[1, :, s, :])], writes=[bX[s]])
        return [bP, bPTT, bJ] + bEE

    load_x(0)
    kb.defer = True
    s5_prologue_math()
    kb.defer = False
    n_def = len(kb.deferred_q)
    HOOK_J = 7
    DEFER_RATE = (n_def + 4 * (HOOK_J + 1) - 1) // (4 * (HOOK_J + 1)) + 1
    if STAGE == -2:
        kb.run_deferred(100000)
    if STAGE == -2:
        s5_prologue_pe()
        S5ALL = view(OFF["S5C"], BF16, [12288])
        Xf = view(OFF["X"], F32, [8192])
        copy_op("act", Xf, S5ALL[:, 0:8192], [bS5C] + bX, bX)
        store_x(0)
        copy_op("act", Xf[:, 0:4096], S5ALL[:, 8192:12288], [bS5C] + bX, bX)
        copy_op("act", Xf[:, 4096:4096 + 64], view(OFF["MISC"], F32, [3840])[:, 0:64], [bSM] + bX, bX)
        store_x(1)
        kb.finish()
        es.close()
        return nc
    for m in range(2):
        if m > 0 and STAGE < 4:
            load_x(m)
        for s in range(8):
            transposes_tile(s)
        if STAGE <= 0:
            store_x(m)
            continue
        u = ffn(w1i_d, w1o_d, 0, hook=(s5_prologue_pe if (m == 0 and not os.environ.get('KDBG_NOHOOK')) else None))
        r1_users, r2_users = u[:2 * NF], u[2 * NF:]
        if STAGE == 1:
            flush_pending()
            store_x(m)
            continue
        a, b_ = mixer(m)
        r1_users, r2_users = a, b_
        if STAGE == 2:
            flush_pending()
            store_x(m)
            continue
        u = ffn(w2i_d, w2o_d, 2)
        r1_users, r2_users = u[:2 * NF], u[2 * NF:]
        if STAGE == 3:
            flush_pending()
            store_x(m)
            continue
        r1_users = ple(m)
    kb.finish()
    es.close()
    return nc


_NC_CACHE = {}


def kernel(**inputs):
    if "nc" not in _NC_CACHE:
        _NC_CACHE["nc"] = build_program()
    nc = _NC_CACHE["nc"]
    names = ["ffn1_w_in", "ffn1_w_out", "ffn2_w_in", "ffn2_w_out", "ln1_g", "ln1_b", "ln2_g", "ln2_b",
             "ln3_g", "ln3_b", "ln4_g", "ln4_b", "mix_w_in", "conv_w", "conv_b", "conv_w_out",
             "ssm_lam_re", "ssm_lam_im", "ssm_log_step", "ssm_b_re", "ssm_b_im", "ssm_c_re", "ssm_c_im",
             "ssm_d", "ssm_w_glu", "mix_w_out", "ple_w_in", "ple_w_gate"]
    shared = {}
    for n in names:
        a = np.ascontiguousarray(np.asarray(inputs[n], dtype=np.float32))
        if n.startswith("ln") or n in ("conv_b", "ssm_log_step"):
            shared[n] = a.reshape(1, -1)
        else:
            shared[n] = a[0]
    x = np.asarray(inputs["x"], dtype=np.float32)
    p = np.asarray(inputs["p"], dtype=np.float32)
    in_maps = []
    for c in range(NCORES):
        d = dict(shared)
        d["x"] = np.ascontiguousarray(x[c])
        d["p"] = np.ascontiguousarray(p[0, c])
        in_maps.append(d)
    res = run_bass_kernel_spmd(nc, in_maps, core_ids=list(range(NCORES)))
    return np.stack([np.asarray(r["out"], dtype=np.float32) for r in res.results], axis=0)
```
